# Optimizing a Trainium2 kernel written in Bass

```python
import jax
import jax.numpy as jnp
from jax import lax
import numpy as np


D_MODEL = 1024
BATCH = 8
SEQ = 4096
DEPTH = 4

N_MIXERS = 4
N_A = (DEPTH + 3) // 4
N_B = (DEPTH + 2) // 4
N_C = (DEPTH + 1) // 4
N_D = DEPTH // 4
HEAD_DIM = 64
NORM_EPS = 1e-5
FFN_HIDDEN = ((8 * D_MODEL + 3 * 256 - 1) // (3 * 256)) * 256

RW_HEADS = D_MODEL // HEAD_DIM
RW_N = HEAD_DIM
RW_DECAY_LORA = 64
RW_AAA_LORA = 64
RW_GATE_LORA = 128
RW_GN_EPS = 64e-5

SW_Q_HEADS = 16
SW_KV_HEADS = 2
SW_GROUP = SW_Q_HEADS // SW_KV_HEADS
SW_WINDOW = 128
SW_BLOCK = 128
ROPE_THETA = 10000.0

SG_CHUNK = 128
SG_WIDTH = 2 * D_MODEL
SG_GROUPS = 16
SG_GROUP_DIM = SG_WIDTH // SG_GROUPS

GLA_HEADS = 4
GLA_DK = D_MODEL // 2
GLA_DV = D_MODEL
GLA_HK = GLA_DK // GLA_HEADS
GLA_HV = GLA_DV // GLA_HEADS
GLA_GATE_LORA = 16
GLA_TAU = 16.0
GLA_CHUNK = 64

kernel_name = 'hybrid_interleaved_rwkv7_swa_sgu_gla'


def rmsnorm(x, g):
    xf = x.astype(jnp.float32)
    y = xf * lax.rsqrt(jnp.mean(jnp.square(xf), -1, keepdims=True) + NORM_EPS)
    return (y * g).astype(x.dtype)


def layernorm(x, g, b):
    xf = x.astype(jnp.float32)
    mu = jnp.mean(xf, -1, keepdims=True)
    var = jnp.mean(jnp.square(xf - mu), -1, keepdims=True)
    return ((xf - mu) * lax.rsqrt(var + NORM_EPS) * g + b).astype(x.dtype)


def token_shift(x):
    return jnp.pad(x[:, :-1], ((0, 0), (1, 0), (0, 0)))


def rope_tables(positions):
    f32 = jnp.float32
    inv_freq = ROPE_THETA ** (-jnp.arange(0, HEAD_DIM, 2, dtype=f32) / HEAD_DIM)
    ang = positions.astype(f32)[..., None] * inv_freq
    return jnp.cos(ang)[:, :, None, :], jnp.sin(ang)[:, :, None, :]


def apply_rope(t, cos, sin):
    half = t.shape[-1] // 2
    t1, t2 = t[..., :half], t[..., half:]
    return jnp.concatenate([t1 * cos - t2 * sin, t2 * cos + t1 * sin], -1).astype(t.dtype)


def swiglu(x, w_in, w_out):
    gate, up = jnp.split(x @ w_in, 2, axis=-1)
    return (jax.nn.silu(gate) * up) @ w_out


def rwkv7_mix(x, mu, w_rkv, w0, w1, w2, a0, a1, a2, g1, g2, k_k, k_a, r_k, gn_g, gn_b, w_o):
    B, S, D = x.shape
    H, N = RW_HEADS, RW_N
    f32 = jnp.float32
    xx = token_shift(x) - x
    xr, xw, xk, xv, xa, xg = (x + xx * mu[c] for c in range(6))
    r, k, v = jnp.einsum('cbsd,cde->cbse', jnp.stack([xr, xk, xv]), w_rkv).astype(f32)
    w = -jax.nn.softplus(-(w0 + jnp.tanh(xw @ w1) @ w2).astype(f32)) - 0.5
    a = jax.nn.sigmoid((a0 + (xa @ a1) @ a2).astype(f32))
    g = jax.nn.sigmoid(xg @ g1) @ g2
    hs = lambda t: t.reshape(B, S, H, N)
    kk = hs(k * k_k)
    kk = kk * lax.rsqrt(jnp.maximum(jnp.sum(kk * kk, -1, keepdims=True), 1e-24))
    k = k * (1.0 + (a - 1.0) * k_a)
    decay = jnp.exp(-jnp.exp(w))

    def step(state, inp):
        r_t, d_t, k_t, v_t, kk_t, a_t = inp
        sa = jnp.einsum('bhvk,bhk->bhv', state, kk_t)
        state = (state * d_t[:, :, None, :]
                 - sa[..., None] * (kk_t * a_t)[:, :, None, :]
                 + v_t[..., None] * k_t[:, :, None, :])
        return state, jnp.einsum('bhvk,bhk->bhv', state, r_t)

    seq_first = lambda t: jnp.moveaxis(t, 1, 0)
    xs = tuple(seq_first(t) for t in (hs(r), hs(decay), hs(k), hs(v), kk, hs(a)))
    _, y = lax.scan(step, jnp.zeros((B, H, N, N), f32), xs)
    y = jnp.moveaxis(y, 0, 1)
    mean = jnp.mean(y, -1, keepdims=True)
    var = jnp.mean(jnp.square(y - mean), -1, keepdims=True)
    y = ((y - mean) * lax.rsqrt(var + RW_GN_EPS)).reshape(B, S, D) * gn_g + gn_b
    bonus = (jnp.sum(hs(r) * hs(k) * r_k, -1, keepdims=True) * hs(v)).reshape(B, S, D)
    return ((y + bonus).astype(x.dtype) * g) @ w_o


def swa_sink_mix(x, cos, sin, w_qkv, b_qkv, sinks, w_o, b_o):
    B, S, _ = x.shape
    f32 = jnp.float32
    qd = SW_Q_HEADS * HEAD_DIM
    kd = SW_KV_HEADS * HEAD_DIM
    qkv = x @ w_qkv + b_qkv
    q = apply_rope(qkv[..., :qd].reshape(B, S, SW_Q_HEADS, HEAD_DIM), cos, sin)
    k = apply_rope(qkv[..., qd:qd + kd].reshape(B, S, SW_KV_HEADS, HEAD_DIM), cos, sin)
    v = qkv[..., qd + kd:].reshape(B, S, SW_KV_HEADS, HEAD_DIM)
    nb = S // SW_BLOCK
    qb = q.reshape(B, nb, SW_BLOCK, SW_KV_HEADS, SW_GROUP, HEAD_DIM)

    def band(t):
        tb = t.reshape(B, nb, SW_BLOCK, SW_KV_HEADS, HEAD_DIM)
        prev = jnp.pad(tb[:, :-1], ((0, 0), (1, 0), (0, 0), (0, 0), (0, 0)))
        return jnp.concatenate([prev, tb], axis=2)

    kw, vw = band(k), band(v)
    s = jnp.einsum('bnqhgd,bnkhd->bnhgqk', qb, kw).astype(f32) * (HEAD_DIM ** -0.5)
    qi = jnp.arange(SW_BLOCK)[:, None]
    kj = jnp.arange(2 * SW_BLOCK)[None, :]
    rel = qi + SW_BLOCK - kj
    blk = jnp.arange(nb)[:, None, None]
    valid = (rel >= 0) & (rel < SW_WINDOW) & (blk * SW_BLOCK + kj - SW_BLOCK >= 0)
    s = jnp.where(valid[None, :, None, None], s, -jnp.inf)
    sink = sinks.astype(f32).reshape(SW_KV_HEADS, SW_GROUP)[None, None, :, :, None, None]
    m = jnp.maximum(jnp.max(s, -1, keepdims=True), sink)
    p = jnp.exp(s - m)
    p = p / (jnp.sum(p, -1, keepdims=True) + jnp.exp(sink - m))
    o = jnp.einsum('bnhgqk,bnkhd->bnqhgd', p.astype(x.dtype), vw).reshape(B, S, qd)
    return o @ w_o + b_o


def sgu_chunk_mix(x, w_in, b_in, ln_g, ln_b, w_s, b_s, w_o, b_o):
    B, S, _ = x.shape
    h = jax.nn.gelu(x @ w_in + b_in, approximate=False)
    u, v = h[..., :SG_WIDTH], h[..., SG_WIDTH:]
    v = layernorm(v, ln_g, ln_b)
    nc = S // SG_CHUNK
    vb = v.reshape(B, nc, SG_CHUNK, SG_GROUPS, SG_GROUP_DIM)
    causal = jnp.tril(jnp.ones((SG_CHUNK, SG_CHUNK), dtype=bool))
    ws = jnp.where(causal[None], w_s, 0.0).astype(v.dtype)
    sv = jnp.einsum('gts,bnsgc->bntgc', ws, vb) + b_s.T[None, None, :, :, None]
    return (u * sv.reshape(B, S, SG_WIDTH)) @ w_o + b_o


def gla_mix(x, w_in, w_a2, b_a, gn_g, w_o):
    B, S, _ = x.shape
    f32 = jnp.float32
    H = GLA_HEADS
    proj = x @ w_in
    q, k, v, gate, a_low = jnp.split(
        proj, [GLA_DK, 2 * GLA_DK, 2 * GLA_DK + GLA_DV, 2 * GLA_DK + 2 * GLA_DV], axis=-1)
    log_a = jax.nn.log_sigmoid((a_low @ w_a2 + b_a).astype(f32)) / GLA_TAU
    nc = S // GLA_CHUNK
    shp_k = (B, nc, GLA_CHUNK, H, GLA_HK)
    shp_v = (B, nc, GLA_CHUNK, H, GLA_HV)
    q = q.astype(f32).reshape(shp_k) * (GLA_HK ** -0.5)
    k = k.astype(f32).reshape(shp_k)
    v = v.astype(f32).reshape(shp_v)
    bcum = jnp.cumsum(log_a.reshape(shp_k), axis=2)
    b_last = bcum[:, :, -1:]
    q_g = q * jnp.exp(bcum)
    k_g = k * jnp.exp(-bcum)
    k_s = k * jnp.exp(b_last - bcum)
    causal = jnp.tril(jnp.ones((GLA_CHUNK, GLA_CHUNK), dtype=bool))
    att = jnp.where(causal, jnp.einsum('bnihk,bnjhk->bnhij', q_g, k_g), 0.0)
    o_intra = jnp.einsum('bnhij,bnjhv->bnihv', att, v)

    def step(state, inp):
        qc, kc, vc, dc = inp
        o = jnp.einsum('bihk,bhkv->bihv', qc, state)
        state = state * dc[..., None] + jnp.einsum('bjhk,bjhv->bhkv', kc, vc)
        return state, o

    seq_first = lambda t: jnp.moveaxis(t, 1, 0)
    xs = (seq_first(q_g), seq_first(k_s), seq_first(v), seq_first(jnp.exp(b_last[:, :, 0])))
    _, o_inter = lax.scan(step, jnp.zeros((B, H, GLA_HK, GLA_HV), f32), xs)
    o = (o_intra + jnp.moveaxis(o_inter, 0, 1)).reshape(B, S, H, GLA_HV)
    o = o * lax.rsqrt(jnp.mean(jnp.square(o), -1, keepdims=True) + NORM_EPS)
    o = (o.reshape(B, S, GLA_DV) * gn_g).astype(x.dtype)
    return (o * jax.nn.silu(gate)) @ w_o


def setup_inputs(seed: int = 0) -> dict:
    key = jax.random.key(seed)
    ks = iter(jax.random.split(key, 64))
    f32 = jnp.float32
    D = D_MODEL

    def nrm(shape, scale):
        return jax.random.normal(next(ks), shape, f32) * scale

    def uni(shape, lo, hi):
        return jax.random.uniform(next(ks), shape, f32, lo, hi)

    sw_qkv_dim = (SW_Q_HEADS + 2 * SW_KV_HEADS) * HEAD_DIM
    gla_in_dim = 2 * GLA_DK + 2 * GLA_DV + GLA_GATE_LORA
    return {
        'x': nrm((BATCH, SEQ, D), 1.0),
        'positions': jnp.broadcast_to(jnp.arange(SEQ, dtype=jnp.int32), (BATCH, SEQ)),
        'norm_mix': 1.0 + nrm((DEPTH, D), 0.02),
        'norm_ffn': 1.0 + nrm((DEPTH, D), 0.02),
        'ffn_w_in': nrm((DEPTH, D, 2 * FFN_HIDDEN), D ** -0.5),
        'ffn_w_out': nrm((DEPTH, FFN_HIDDEN, D), FFN_HIDDEN ** -0.5),
        'norm_final': 1.0 + nrm((D,), 0.02),
        'rw_mu': uni((N_A, 6, D), 0.0, 1.0),
        'rw_w_rkv': nrm((N_A, 3, D, D), D ** -0.5),
        'rw_w0': nrm((N_A, D), 1.0) - 1.0,
        'rw_w1': nrm((N_A, D, RW_DECAY_LORA), D ** -0.5),
        'rw_w2': nrm((N_A, RW_DECAY_LORA, D), 0.5 * RW_DECAY_LORA ** -0.5),
        'rw_a0': nrm((N_A, D), 0.1),
        'rw_a1': nrm((N_A, D, RW_AAA_LORA), D ** -0.5),
        'rw_a2': nrm((N_A, RW_AAA_LORA, D), 0.5 * RW_AAA_LORA ** -0.5),
        'rw_g1': nrm((N_A, D, RW_GATE_LORA), D ** -0.5),
        'rw_g2': nrm((N_A, RW_GATE_LORA, D), RW_GATE_LORA ** -0.5),
        'rw_k_k': 0.85 + nrm((N_A, D), 0.02),
        'rw_k_a': 1.0 + nrm((N_A, D), 0.02),
        'rw_r_k': nrm((N_A, RW_HEADS, RW_N), 0.1),
        'rw_gn_g': 1.0 + nrm((N_A, D), 0.02),
        'rw_gn_b': nrm((N_A, D), 0.02),
        'rw_w_o': nrm((N_A, D, D), D ** -0.5),
        'sw_w_qkv': nrm((N_B, D, sw_qkv_dim), D ** -0.5),
        'sw_b_qkv': nrm((N_B, sw_qkv_dim), 0.02),
        'sw_sinks': nrm((N_B, SW_Q_HEADS), 1.0),
        'sw_w_o': nrm((N_B, SW_Q_HEADS * HEAD_DIM, D), (SW_Q_HEADS * HEAD_DIM) ** -0.5),
        'sw_b_o': nrm((N_B, D), 0.02),
        'sg_w_in': nrm((N_C, D, 2 * SG_WIDTH), D ** -0.5),
        'sg_b_in': nrm((N_C, 2 * SG_WIDTH), 0.02),
        'sg_ln_g': 1.0 + nrm((N_C, SG_WIDTH), 0.02),
        'sg_ln_b': nrm((N_C, SG_WIDTH), 0.02),
        'sg_w_s': nrm((N_C, SG_GROUPS, SG_CHUNK, SG_CHUNK), SG_CHUNK ** -0.5),
        'sg_b_s': 1.0 + nrm((N_C, SG_GROUPS, SG_CHUNK), 0.02),
        'sg_w_o': nrm((N_C, SG_WIDTH, D), SG_WIDTH ** -0.5),
        'sg_b_o': nrm((N_C, D), 0.02),
        'gla_w_in': nrm((N_D, D, gla_in_dim), D ** -0.5),
        'gla_w_a2': nrm((N_D, GLA_GATE_LORA, GLA_DK), GLA_GATE_LORA ** -0.5),
        'gla_b_a': nrm((N_D, GLA_DK), 0.1),
        'gla_gn_g': 1.0 + nrm((N_D, GLA_DV), 0.02),
        'gla_w_o': nrm((N_D, GLA_DV, D), GLA_DV ** -0.5),
    }


def reference(x, positions, norm_mix, norm_ffn, ffn_w_in, ffn_w_out, norm_final,
              rw_mu, rw_w_rkv, rw_w0, rw_w1, rw_w2, rw_a0, rw_a1, rw_a2, rw_g1, rw_g2,
              rw_k_k, rw_k_a, rw_r_k, rw_gn_g, rw_gn_b, rw_w_o,
              sw_w_qkv, sw_b_qkv, sw_sinks, sw_w_o, sw_b_o,
              sg_w_in, sg_b_in, sg_ln_g, sg_ln_b, sg_w_s, sg_b_s, sg_w_o, sg_b_o,
              gla_w_in, gla_w_a2, gla_b_a, gla_gn_g, gla_w_o):
    cos, sin = rope_tables(positions)
    h = x
    for i in range(DEPTH):
        t, j = i % N_MIXERS, i // N_MIXERS
        hn = rmsnorm(h, norm_mix[i])
        if t == 0:
            m = rwkv7_mix(hn, rw_mu[j], rw_w_rkv[j], rw_w0[j], rw_w1[j], rw_w2[j],
                          rw_a0[j], rw_a1[j], rw_a2[j], rw_g1[j], rw_g2[j],
                          rw_k_k[j], rw_k_a[j], rw_r_k[j], rw_gn_g[j], rw_gn_b[j], rw_w_o[j])
        elif t == 1:
            m = swa_sink_mix(hn, cos, sin, sw_w_qkv[j], sw_b_qkv[j], sw_sinks[j],
                             sw_w_o[j], sw_b_o[j])
        elif t == 2:
            m = sgu_chunk_mix(hn, sg_w_in[j], sg_b_in[j], sg_ln_g[j], sg_ln_b[j],
                              sg_w_s[j], sg_b_s[j], sg_w_o[j], sg_b_o[j])
        else:
            m = gla_mix(hn, gla_w_in[j], gla_w_a2[j], gla_b_a[j], gla_gn_g[j], gla_w_o[j])
        h = h + m
        h = h + swiglu(rmsnorm(h, norm_ffn[i]), ffn_w_in[i], ffn_w_out[i])
    return rmsnorm(h, norm_final)
```

```python
import numpy as np
from collections import defaultdict
from contextlib import ExitStack
import concourse.bass as bass
import concourse.mybir as mybir
from concourse.bass_utils import run_bass_kernel_spmd

F32 = mybir.dt.float32
BF16 = mybir.dt.bfloat16
I32 = mybir.dt.int32
AF = mybir.ActivationFunctionType
ALU = mybir.AluOpType

D = 1024
KC = 8
T = 512
TR = 256
FH = 2816
NJ = 22
SLOT = 4096
NSLOT = 4
EPS = 1e-5
ARENA = 29 * 1024
ENG = ('pe', 'act', 'dve', 'pool', 'sp')


class Buf:
    __slots__ = ('name', 'w', 'r', 'excl')

    def __init__(self, name):
        self.name = name
        self.w = None
        self.r = {}
        self.excl = False


class Tile:
    def __init__(self, t, name):
        self.t = t
        self.b = Buf(name)


class Packer:
    def __init__(self):
        self.pieces = {}
        self.chunks = []
        self.off = 0
        self.vec_cols = {}
        self.vecs = []
        self.nv = 0

    def piece(self, name, arr):
        arr = np.ascontiguousarray(arr, dtype=np.float32).reshape(arr.shape[0], -1)
        p, n = arr.shape
        assert p <= 128 and n <= SLOT, (name, arr.shape)
        if p < 128:
            arr = np.concatenate([arr, np.zeros((128 - p, n), np.float32)], 0)
        self.pieces[name] = (self.off, n)
        self.chunks.append(arr.reshape(-1))
        self.off += 128 * n

    def vec(self, name, v):
        v = np.asarray(v, np.float32).reshape(-1, 128).T
        self.vec_cols[name] = self.nv
        self.vecs.append(v)
        self.nv += v.shape[1]

    def vec_raw(self, name, m):
        m = np.asarray(m, np.float32)
        assert m.shape[0] == 128
        self.vec_cols[name] = self.nv
        self.vecs.append(m)
        self.nv += m.shape[1]

    def finish(self):
        wpk = np.concatenate(self.chunks) if self.chunks else np.zeros((128,), np.float32)
        vec = np.concatenate(self.vecs, 1)
        return wpk, np.ascontiguousarray(vec)


def lhsT_blocks(W):
    K, N = W.shape
    return W.reshape(K // 128, 128, N // 128, 128).transpose(2, 1, 0, 3)


def pack_layer(pk, inp, li, mt, j):
    pre = f"L{li}_"
    pk.vec(pre + "nmix", inp['norm_mix'][li])
    pk.vec(pre + "nffn", inp['norm_ffn'][li])
    return pre


def pack_ffn(pk, inp, li, order):
    pre = f"L{li}_"
    w_in = inp['ffn_w_in'][li]
    w_out = inp['ffn_w_out'][li]
    A = w_in.reshape(8, 128, 2, NJ, 128).transpose(3, 1, 2, 0, 4)
    for j in range(NJ):
        pk.piece(pre + f"fin{j}", A[j].reshape(128, -1))
        order.append(pre + f"fin{j}")
    Bm = w_out.reshape(NJ, 128, 8, 128).transpose(2, 1, 0, 3)
    for fc in range(8):
        pk.piece(pre + f"fout{fc}", Bm[fc].reshape(128, -1))
        order.append(pre + f"fout{fc}")


def pack_sgu(pk, inp, li, order):
    pre = f"L{li}_"
    blk = lhsT_blocks(inp['sg_w_in'][0])
    for g in range(8):
        pk.piece(pre + f"sgin{g}", blk[4 * g:4 * g + 4].transpose(1, 0, 2, 3).reshape(128, -1))
    ws = inp['sg_w_s'][0]
    pk.piece(pre + "sgws", ws.transpose(2, 0, 1).reshape(128, -1))
    blk = lhsT_blocks(inp['sg_w_o'][0])
    for g in range(4):
        pk.piece(pre + f"sgwo{g}", blk[2 * g:2 * g + 2].transpose(1, 0, 2, 3).reshape(128, -1))
    pk.vec(pre + "sg_b_in", inp['sg_b_in'][0])
    pk.vec(pre + "sg_ln_g", inp['sg_ln_g'][0])
    pk.vec(pre + "sg_ln_b", inp['sg_ln_b'][0])
    pk.vec(pre + "sg_b_o", inp['sg_b_o'][0])


def pack_rwkv(pk, inp, li, order):
    pre = f"L{li}_"
    kc = lambda W: W.reshape(8, 128, -1).transpose(1, 0, 2).reshape(128, -1)
    pk.piece(pre + "rwl1", np.concatenate([kc(inp['rw_w1'][0]), kc(inp['rw_a1'][0]), kc(inp['rw_g1'][0])], 1))
    W = inp['rw_w_rkv'][0]
    blk = np.stack([lhsT_blocks(W[j]) for j in range(3)], 0)
    for hp in range(8):
        l2 = np.zeros((128, 384), np.float32)
        l2[0:64, 0:128] = inp['rw_w2'][0][:, hp * 128:(hp + 1) * 128]
        l2[0:64, 128:256] = inp['rw_a2'][0][:, hp * 128:(hp + 1) * 128]
        l2[:, 256:384] = inp['rw_g2'][0][:, hp * 128:(hp + 1) * 128]
        pk.piece(pre + f"rwrkv{hp}", np.concatenate([blk[:, hp].transpose(1, 0, 2, 3).reshape(128, -1), l2], 1))
    blk = lhsT_blocks(inp['rw_w_o'][0])
    for g in range(2):
        pk.piece(pre + f"rwwo{g}", blk[4 * g:4 * g + 4].transpose(1, 0, 2, 3).reshape(128, -1))
    pk.vec(pre + "rw_mu", inp['rw_mu'][0].reshape(-1))
    for nm in ("w0", "a0", "k_k", "k_a", "gn_g", "gn_b"):
        pk.vec(pre + "rw_" + nm, inp['rw_' + nm][0])
    pk.vec(pre + "rw_r_k", inp['rw_r_k'][0].reshape(-1))


def sw_perm():
    idx = []
    for c in range(8):
        idx += list(range(c * 64, c * 64 + 64)) + list(range((c + 8) * 64, (c + 8) * 64 + 64))
    return np.array(idx)


def pack_swa(pk, inp, li, order):
    pre = f"L{li}_"
    perm = sw_perm()
    w = inp['sw_w_qkv'][0]
    b = inp['sw_b_qkv'][0]
    cols = np.concatenate([perm, np.arange(1024, 1280)])
    wp = w[:, cols]
    blk = lhsT_blocks(wp)
    pk.piece(pre + "swin0", blk[0:4].transpose(1, 0, 2, 3).reshape(128, -1))
    pk.piece(pre + "swin1", blk[4:8].transpose(1, 0, 2, 3).reshape(128, -1))
    pk.piece(pre + "swin2", blk[8:10].transpose(1, 0, 2, 3).reshape(128, -1))
    wo = inp['sw_w_o'][0][perm, :]
    blk = lhsT_blocks(wo)
    for g in range(2):
        pk.piece(pre + f"swwo{g}", blk[4 * g:4 * g + 4].transpose(1, 0, 2, 3).reshape(128, -1))
    pk.vec(pre + "sw_b_qkv", b[cols])
    pk.vec(pre + "sw_b_o", inp['sw_b_o'][0])
    sk = inp['sw_sinks'][0]
    m = np.zeros((128, 8), np.float32)
    for c in range(8):
        m[0:64, c] = sk[c]
        m[64:128, c] = sk[c + 8]
    pk.vec_raw(pre + "sw_sink", m)


def pack_gla(pk, inp, li, order):
    pre = f"L{li}_"
    w_in = inp['gla_w_in'][0]
    blk = lhsT_blocks(w_in[:, :3072])
    for g in range(6):
        pk.piece(pre + f"glin{g}", blk[4 * g:4 * g + 4].transpose(1, 0, 2, 3).reshape(128, -1))
    pk.piece(pre + "glal", w_in[:, 3072:3088].reshape(8, 128, 16).transpose(1, 0, 2).reshape(128, -1))
    pk.piece(pre + "glwa2", inp['gla_w_a2'][0])
    blk = lhsT_blocks(inp['gla_w_o'][0])
    for g in range(2):
        pk.piece(pre + f"glwo{g}", blk[4 * g:4 * g + 4].transpose(1, 0, 2, 3).reshape(128, -1))
    pk.vec(pre + "gl_b_a", inp['gla_b_a'][0])
    pk.vec(pre + "gl_gn_g", inp['gla_gn_g'][0])


class KB:
    NDS = 8

    def __init__(self, nc, es):
        self.nc = nc
        self.es = es
        self.prog = {e: [] for e in ENG}
        self.cnt = {e: 0 for e in ENG}
        self.waited = {e: {} for e in ENG}
        self.sem = {e: es.enter_context(nc.semaphore(f"s_{e}")) for e in ENG}
        for q in ('sp', 'pool'):
            for i in range(self.NDS):
                self.sem[f"d{q}{i}"] = es.enter_context(nc.semaphore(f"d{q}{i}"))
        self.dma_i = {'sp': 0, 'pool': 0}
        self.dma_tot = defaultdict(int)
        self.same_sync = {'pe': False, 'act': True, 'dve': True, 'pool': True, 'sp': False}
        self.ntile = 0

    def sb(self, name, shape, dt):
        t = self.es.enter_context(self.nc.sbuf_tensor(name, list(shape), dt))
        return Tile(t, name)

    def handoff(self, old, new):
        toks = {}
        for b in old:
            if b.w:
                toks[b.w[0]] = max(toks.get(b.w[0], 0), b.w[1])
            for s_, v in b.r.items():
                toks[s_] = max(toks.get(s_, 0), v)
            b.w = None
            b.r = {}
        for b in new:
            for s_, v in toks.items():
                b.r[s_] = max(b.r.get(s_, 0), v)

    def _deps(self, reads, writes):
        deps = []
        for b in reads:
            if b.w:
                deps.append((b.w[0], b.w[1], True))
            if b.excl:
                deps.extend((s_, v_, False) for s_, v_ in b.r.items())
        for b in writes:
            if b.w:
                deps.append((b.w[0], b.w[1], False))
            deps.extend((s_, v_, False) for s_, v_ in b.r.items())
        return deps

    def _waits(self, eng, deps):
        out = {}
        for s, v, raw in deps:
            if s == eng and (not self.same_sync[eng] or not raw):
                continue
            if self.waited[eng].get(s, 0) >= v:
                continue
            out[s] = max(out.get(s, 0), v)
        for s, v in out.items():
            self.waited[eng][s] = v
        return list(out.items())

    def _mark(self, tok, reads, writes):
        for b in reads:
            b.r[tok[0]] = max(b.r.get(tok[0], 0), tok[1])
        for b in writes:
            b.w = tok
            b.r = {}

    def op(self, eng, fn, reads=(), writes=()):
        waits = self._waits(eng, self._deps(reads, writes))
        self.cnt[eng] += 1
        tok = (eng, self.cnt[eng])
        sem = self.sem

        def run(e):
            for s, v in waits:
                e.wait_ge(sem[s], v)
            fn(e).then_inc(sem[eng], 1)
        self.prog[eng].append(run)
        self._mark(tok, reads, writes)

    def dma(self, q, out, in_, reads=(), writes=()):
        deps = self._deps(reads, writes)
        i = self.dma_i[q]
        self.dma_i[q] += 1
        sname = f"d{q}{i % self.NDS}"
        prev = self.dma_tot[sname]
        if prev:
            deps.append((sname, prev, True))
        waits = self._waits(q, deps)
        self.dma_tot[sname] = prev + 16
        tok = (sname, prev + 16)
        sem = self.sem

        def run(e):
            for s, v in waits:
                e.wait_ge(sem[s], v)
            e.dma_start(out=out, in_=in_).then_inc(sem[sname], 16)
        self.prog[q].append(run)
        self._mark(tok, reads, writes)

    def finish(self):
        sem = self.sem
        finals = [(s, v) for s, v in self.dma_tot.items() if v]

        def run(e):
            for s, v in finals:
                e.wait_ge(sem[s], v)
        self.prog['sp'].append(run)
        prog = self.prog
        with self.nc.Block() as block:
            @block.sync
            def _(e):
                for f in prog['sp']:
                    f(e)

            @block.tensor
            def _(e):
                for f in prog['pe']:
                    f(e)

            @block.scalar
            def _(e):
                for f in prog['act']:
                    f(e)

            @block.vector
            def _(e):
                for f in prog['dve']:
                    f(e)

            @block.gpsimd
            def _(e):
                for f in prog['pool']:
                    f(e)

    def mm(self, out, lhsT, rhs, start, stop, reads, writes, skip=False):
        if skip:
            self.op('pe', lambda e: e.matmul(out, lhsT=lhsT, rhs=rhs, start=start, stop=stop, skip_group_check=True), reads, writes)
        else:
            self.op('pe', lambda e: e.matmul(out, lhsT=lhsT, rhs=rhs, start=start, stop=stop), reads, writes)

    def tr(self, out, in_, ident, reads, writes):
        self.op('pe', lambda e: e.transpose(out, in_, ident), reads, writes)

    def act(self, out, in_, func, reads, writes, bias=None, scale=None):
        kw = {}
        if bias is not None:
            kw['bias'] = bias
        if scale is not None:
            kw['scale'] = scale
        self.op('act', lambda e: e.activation(out=out, in_=in_, func=func, **kw), reads, writes)

    def tt(self, out, in0, in1, op, reads, writes, eng='dve'):
        self.op(eng, lambda e: e.tensor_tensor(out=out, in0=in0, in1=in1, op=op), reads, writes)

    def ts(self, out, in0, s1, op0, reads, writes, s2=None, op1=None, eng='dve'):
        if op1 is None:
            self.op(eng, lambda e: e.tensor_scalar(out=out, in0=in0, scalar1=s1, scalar2=None, op0=op0), reads, writes)
        else:
            self.op(eng, lambda e: e.tensor_scalar(out=out, in0=in0, scalar1=s1, scalar2=s2, op0=op0, op1=op1), reads, writes)

    def stt(self, out, in0, scalar, in1, op0, op1, reads, writes):
        self.op('dve', lambda e: e.scalar_tensor_tensor(out=out, in0=in0, scalar=scalar, in1=in1, op0=op0, op1=op1), reads, writes)

    def cp(self, out, in_, reads, writes, eng='dve'):
        if eng == 'act':
            self.op(eng, lambda e: e.activation(out=out, in_=in_, func=AF.Copy), reads, writes)
        else:
            self.op(eng, lambda e: e.tensor_copy(out=out, in_=in_), reads, writes)


class Prog:
    def __init__(self, nc, es, S, layers, pk, do_ffn=True):
        self.nc = nc
        self.S = S
        self.NT = S // T
        self.layers = layers
        self.pk = pk
        self.do_ffn = do_ffn
        k = self.k = KB(nc, es)
        nv = pk.nv
        self.xT = nc.dram_tensor("xT", [D, S], F32, kind="ExternalInput").ap()
        self.pos = nc.dram_tensor("pos", [1, S], I32, kind="ExternalInput").ap()
        self.wpk = nc.dram_tensor("wpk", [max(pk.off, 128)], F32, kind="ExternalInput").ap()
        self.vecd = nc.dram_tensor("vec", [128, nv], F32, kind="ExternalInput").ap()
        self.cstd = nc.dram_tensor("cst", [128, CST_N], F32, kind="ExternalInput").ap()
        self.outT = nc.dram_tensor("outT", [D, S], F32, kind="ExternalOutput").ap()
        self.vec = k.sb("vec_sb", [128, nv], F32)
        self.cf = k.sb("cst_f", [128, CST_N], F32)
        self.cb = k.sb("cst_b", [128, CSTB_N], BF16)
        self.hT = k.sb("hT", [128, KC * T], F32)
        self.xn = k.sb("xn", [128, KC * T], BF16)
        self.sq = k.sb("sq", [128, KC * T], BF16)
        self.nrm1 = k.sb("nrm1", [128, T], F32)
        self.nrm2 = k.sb("nrm2", [128, T], F32)
        self.ring = [k.sb(f"ring{i}", [128, SLOT], BF16) for i in range(NSLOT)]
        self.ring_i = 0
        self.arena = es.enter_context(nc.sbuf_tensor("arena", [128, ARENA], F32))
        self.owners = defaultdict(list)
        self.cur_owner = None
        self.aoff = {}
        self.hid = self.carve('ffn', "hid", NJ * T, BF16)
        self.sg = [self.carve('ffn', f"sg{i}", T, F32) for i in range(2)]
        self.fin = self.carve('ffn', "fin", KC * T, F32)
        self.ps = []
        for i in range(8):
            t = es.enter_context(nc.psum_tensor(f"ps{i}", [128, 512], F32))
            self.ps.append(Tile(t, f"ps{i}"))
            self.ps[-1].b.excl = True
        self.hTb = [Buf(f"hTb{c}") for c in range(KC)]
        self.xnb = [Buf(f"xnb{c}") for c in range(KC)]
        self.sqb8 = [Buf(f"sqb{c}") for c in range(KC)]
        self.hTv = self.hT.t[:].rearrange("p (c t) -> p c t", c=KC)
        self.xnv = self.xn.t[:].rearrange("p (c t) -> p c t", c=KC)
        self.sqv = self.sq.t[:].rearrange("p (c t) -> p c t", c=KC)
        self.hidv = self.hid.t[:].rearrange("p (c t) -> p c t", c=NJ)
        self.alloc_mixers()

    def carve(self, owner, name, n, dt, parent=None):
        words = n if dt != BF16 else (n + 1) // 2
        if parent is not None and owner not in self.aoff:
            self.aoff[owner] = self.aoff[parent + "_end"]
        off = self.aoff.get(owner, 0)
        assert off + words <= ARENA, (owner, name, off, words)
        self.aoff[owner] = off + words
        v = self.arena[:, off:off + words]
        if dt != F32:
            v = v.bitcast(dt)
        tl = Tile(v, name)
        self.owners[owner].append(tl.b)
        if parent is not None:
            self.owners[parent].append(tl.b)
        return tl

    def switch(self, owner):
        if self.cur_owner is not None and self.cur_owner != owner:
            self.k.handoff(self.owners[self.cur_owner], self.owners[owner])
        self.cur_owner = owner

    def vcol(self, name, c=0, p0=0, p1=128):
        i = self.pk.vec_cols[name] + c
        return self.vec.t[p0:p1, i:i + 1]

    def cF(self, name, p0=0, p1=128):
        o, n = CST[name]
        return self.cf.t[p0:p1, o:o + n]

    def cB(self, name, p0=0, p1=128):
        o, n = CST[name]
        return self.cb.t[p0:p1, o:o + n]

    def load_piece(self, name):
        off, n = self.pk.pieces[name]
        slot = self.ring[self.ring_i % NSLOT]
        self.ring_i += 1
        src = self.wpk[off:off + 128 * n].rearrange("(p n) -> p n", p=128)
        self.k.dma('pool', slot.t[:, 0:n], src, reads=[], writes=[slot.b])
        return slot

    def alloc_mixers(self):
        k = self.k
        types = set(mt for mt, _ in self.layers)
        if 2 in types:
            self.sg_u = self.carve('m2', "sg_u", 16 * T, BF16)
            self.sg_v = self.carve('m2', "sg_v", 16 * T, F32)
            self.sg_vn = [self.carve('m2', f"sg_vn{i}", T, BF16) for i in range(2)]
            self.sg_vt = [self.carve('m2', f"sg_vt{i}", T, BF16) for i in range(2)]
            self.sg_sqa = [self.carve('m2', f"sg_sqa{i}", T, BF16) for i in range(2)]
            self.sg_z = self.carve('m2', "sg_z", 16 * T, BF16)
            self.sg_ws = self.carve('m2', "sg_ws", 16 * 128, BF16)
            self.sg_st = [self.carve('m2', f"sg_st{i}", T, F32) for i in range(3)]
            self.sg_t1 = [self.carve('m2', f"sg_t1{i}", T, F32) for i in range(2)]
            self.sg_bs = self.carve('m2', "sg_bs", 16 * 128, BF16)
            self.sg_zb = [Buf(f"sg_zb{i}") for i in range(16)]
            self.sg_vb = [Buf(f"sg_vb{i}") for i in range(16)]
            self.owners['m2'] += self.sg_zb + self.sg_vb
        if 0 in types:
            self.rw_Sf = [k.sb(f"rw_Sf{h}", [64, 128], F32) for h in range(8)]
            self.rw_Sb = [k.sb(f"rw_Sb{h}", [64, 128], BF16) for h in range(8)]
            self.rw_prev = k.sb("rw_prev", [128, 8], F32)
            self.rw_Sfb = [[Buf(f"rw_Sfb{h}_{e}") for e in range(2)] for h in range(8)]
            self.rw_nb = k.sb("rw_nb", [128, 16], F32)
            c_ = lambda n, sz, dt: self.carve('m0', n, sz, dt)
            R = self.R = {}
            R["xnF"] = c_("rw_xnF", KC * TR, F32)
            R["xx"] = c_("rw_xx", KC * TR, F32)
            for nm in ("xr", "xk", "xv", "zT"):
                R[nm] = c_("rw_" + nm, KC * TR, BF16)
            for nm in ("w1T", "a1T", "g1T", "BT", "KT", "BH", "KH", "VB", "BTh", "KTh"):
                R[nm] = c_("rw_" + nm, TR, BF16)
            R["bg"] = [{"bonus": c_(f"rw_bonus{i}", TR, F32), "gf": c_(f"rw_gf{i}", TR, F32)} for i in range(4)]
            R["sets"] = []
            for s_ in range(3):
                st = {}
                st["AR"] = c_(f"rw_AR{s_}", 2 * TR, BF16)
                st["ARh"] = c_(f"rw_ARh{s_}", 2 * TR, BF16)
                st["tok"] = c_(f"rw_tok{s_}", 4 * 512, BF16)
                st["AT"] = c_(f"rw_AT{s_}", 8 * 64, BF16)
                st["Uv"] = c_(f"rw_Uv{s_}", 8 * 64, BF16)
                st["SC"] = c_(f"rw_SC{s_}", 8 * 320, BF16)
                st["Tall"] = c_(f"rw_Tall{s_}", 8 * 64, BF16)
                st["WL"] = c_(f"rw_WL{s_}", 4, F32)
                st["WLh"] = c_(f"rw_WLh{s_}", 4, F32)
                R["sets"].append(st)
            R["PP"] = [c_(f"rw_PP{i}", 8 * 128, BF16) for i in range(2)]
            R["TT"] = [c_(f"rw_TT{i}", 8 * 64, BF16) for i in range(2)]
            R["Ub"] = c_("rw_Ub", 128, BF16)
            R["Ub2"] = [c_(f"rw_Ub2{i}", TR, BF16) for i in range(2)]
            for nm in ("rf", "kf", "vf", "sgw", "cs", "a", "kk", "t1", "t2", "W", "Winv", "Wex", "WLr", "ka", "kp"):
                R[nm] = c_("rw_" + nm, TR, F32)
            R["y"] = [c_(f"rw_y{i}", TR, F32) for i in range(2)]
            for nm in ("yc", "ysq"):
                R[nm] = c_("rw_" + nm, TR, F32)
        if 1 in types:
            self.sw_kT = k.sb("sw_kT", [128, 640], BF16)
            self.sw_vt = k.sb("sw_vt", [128, 640], BF16)
            self.sw_es = k.sb("sw_es", [128, 8], F32)
            c_ = lambda n, sz, dt: self.carve('m1', n, sz, dt)
            self.sw_pi = c_("sw_pi", T, I32)
            self.sw_y = c_("sw_y", T, F32)
            self.sw_f = c_("sw_f", T, F32)
            self.sw_g = c_("sw_g", T, F32)
            self.sw_ni = c_("sw_ni", T, I32)
            self.sw_cos = c_("sw_cos", T, F32)
            self.sw_sin = c_("sw_sin", T, F32)
            self.sw_qf = [c_(f"sw_qf{i}", T, F32) for i in range(2)]
            self.sw_t1 = [c_(f"sw_t1{i}", T, F32) for i in range(2)]
            self.sw_t2 = [c_(f"sw_t2{i}", T, F32) for i in range(2)]
            self.sw_qb = [c_(f"sw_qb{i}", T, BF16) for i in range(8)]
            self.sw_vb = c_("sw_vb", T, BF16)
            self.sw_E = [[c_(f"sw_E{i}_{kb}", 512, BF16) for kb in range(5)] for i in range(2)]
            self.sw_ds = [c_(f"sw_ds{i}", T, F32) for i in range(2)]
            self.sw_o = c_("sw_o", 8 * T, BF16)
        if 3 in types:
            self.gl_Sf = [k.sb(f"gl_Sf{h}", [128, 256], F32) for h in range(4)]
            self.gl_Sb = [k.sb(f"gl_Sb{h}", [128, 256], BF16) for h in range(4)]
            self.gl_nb = k.sb("gl_nb", [128, 4], F32)
            c_ = lambda n, sz, dt: self.carve('m3', n, sz, dt)
            self.gl_al = c_("gl_al", T, BF16)
            self.gl_wa2 = c_("gl_wa2", 512, BF16)
            self.gl_dc = [c_(f"gl_dc{h}", 8, F32) for h in range(4)]
            self.gl_qg = [c_(f"gl_qg{h}", T, BF16) for h in range(4)]
            self.gl_kg = [c_(f"gl_kg{h}", T, BF16) for h in range(4)]
            self.gl_ks = [c_(f"gl_ks{h}", T, BF16) for h in range(4)]
            self.gl_v = [c_(f"gl_v{e}", T, BF16) for e in range(8)]
            self.gl_gate = [c_(f"gl_gate{e}", T, F32) for e in range(8)]
            self.gl_z = c_("gl_z", 8 * T, BF16)
            self.aoff["m3_end"] = self.aoff["m3"]
            ca = lambda n, sz, dt: self.carve('m3a', n, sz, dt, parent='m3')
            cb_ = lambda n, sz, dt: self.carve('m3b', n, sz, dt, parent='m3')
            self.gl_la = [ca(f"gl_la{i}", T, F32) for i in range(4)]
            self.gl_eb = [ca(f"gl_eb{h}", T, F32) for h in range(4)]
            self.gl_enb = [ca(f"gl_enb{h}", T, F32) for h in range(4)]
            self.gl_ksf = [ca(f"gl_ksf{h}", T, F32) for h in range(4)]
            self.gl_tok = [cb_(f"gl_tok{i}", 8 * 384, BF16) for i in range(4)]
            self.gl_att = [cb_(f"gl_att{i}", 8 * 64, BF16) for i in range(4)]
            self.gl_ofa = cb_("gl_ofa", 8 * T, F32)
            self.gl_osqa = cb_("gl_osqa", 8 * T, BF16)
            self.gl_rs = [[cb_(f"gl_rs{q}{i}", T, F32) for i in range(2)] for q in range(2)]
            self.gl_pssb = [Buf("gl_pss0"), Buf("gl_pss1")]
            for b_ in self.gl_pssb:
                b_.excl = True

    def prologue(self):
        k = self.k
        k.dma('sp', self.vec.t[:], self.vecd, writes=[self.vec.b])
        k.dma('sp', self.cf.t[:], self.cstd, writes=[self.cf.b])
        k.dma('pool', self.cb.t[:], self.cstd[:, 0:CSTB_N], writes=[self.cb.b])

    def rmsnorm(self, gname, out_f32=None, stats_only=False):
        k = self.k
        ps = self.ps[7]
        ones = self.cB("ones")
        for c in range(KC):
            k.act(self.sqv[:, c, :], self.hTv[:, c, :], AF.Square, [self.hTb[c]], [self.sqb8[c]])
            k.mm(ps.t[:], ones, self.sqv[:, c, :], c == 0, c == KC - 1, [self.sqb8[c], self.cb.b], [ps.b])
        k.act(self.nrm1.t[:], ps.t[:], AF.Ln, [ps.b, self.cf.b], [self.nrm1.b], bias=self.cF("eps")[:, 0:1], scale=1.0 / D)
        k.act(self.nrm2.t[:], self.nrm1.t[:], AF.Exp, [self.nrm1.b], [self.nrm2.b], scale=-0.5)
        if stats_only:
            return
        for c in range(KC):
            if out_f32 is None:
                k.stt(self.xnv[:, c, :], self.hTv[:, c, :], self.vcol(gname, c), self.nrm2.t[:], ALU.mult, ALU.mult,
                      [self.hTb[c], self.vec.b, self.nrm2.b], [self.xnb[c]])
            else:
                ov = out_f32.t[:].rearrange("p (c t) -> p c t", c=KC)
                k.stt(ov[:, c, :], self.hTv[:, c, :], self.vcol(gname, c), self.nrm2.t[:], ALU.mult, ALU.mult,
                      [self.hTb[c], self.vec.b, self.nrm2.b], [out_f32.b])

    def preload_exp_table(self):
        k = self.k
        k.act(self.nrm1.t[:, 0:1], self.cF("one")[:, 0:1], AF.Ln, [self.cf.b], [self.nrm1.b])

    def ffn(self, li):
        k = self.k
        pre = f"L{li}_"
        xn, hid = self.xn, self.hid
        for j in range(NJ):
            slot = self.load_piece(pre + f"fin{j}")
            wv = slot.t[:, 0:2 * KC * 128].rearrange("p (g c m) -> p g c m", g=2, c=KC)
            pg = self.ps[(2 * j) % 4]
            pu = self.ps[(2 * j + 1) % 4]
            for c in range(KC):
                k.mm(pg.t[:], wv[:, 0, c, :], self.xnv[:, c, :], c == 0, c == KC - 1, [slot.b, self.xnb[c]], [pg.b])
            for c in range(KC):
                k.mm(pu.t[:], wv[:, 1, c, :], self.xnv[:, c, :], c == 0, c == KC - 1, [slot.b, self.xnb[c]], [pu.b])
            sg = self.sg[j % 2]
            k.act(sg.t[:], pg.t[:], AF.Silu, [pg.b], [sg.b])
            k.tt(self.hidv[:, j, :], sg.t[:], pu.t[:], ALU.mult, [sg.b, pu.b], [hid.b])
        self.preload_exp_table()
        for fc in range(8):
            slot = self.load_piece(pre + f"fout{fc}")
            wv = slot.t[:, 0:NJ * 128].rearrange("p (j m) -> p j m", j=NJ)
            po = self.ps[4 + fc % 2]
            for j in range(NJ):
                k.mm(po.t[:], wv[:, j, :], self.hidv[:, j, :], j == 0, j == NJ - 1, [slot.b, hid.b], [po.b])
            k.tt(self.hTv[:, fc, :], self.hTv[:, fc, :], po.t[:], ALU.add, [self.hTb[fc], po.b], [self.hTb[fc]])

    def sgu(self, li, ti):
        k = self.k
        pre = f"L{li}_"
        xn = self.xn
        uv = self.sg_u.t[:].rearrange("p (c t) -> p c t", c=16)
        vv = self.sg_v.t[:].rearrange("p (c t) -> p c t", c=16)
        zv = self.sg_z.t[:].rearrange("p (c t) -> p c t", c=16)
        ones = self.cB("ones")
        pm, pq = self.ps[4], self.ps[5]
        zb = self.sg_zb
        k.handoff([self.sg_z.b], zb)

        def stats_mm(c):
            k.mm(pm.t[:], ones, zv[:, c, :], c == 0, c == 15, [zb[c], self.cb.b], [pm.b])
            sq = self.sg_sqa[c % 2]
            k.mm(pq.t[:], ones, sq.t[:], c == 0, c == 15, [sq.b, self.cb.b], [pq.b])
        for g in range(8):
            slot = self.load_piece(pre + f"sgin{g}")
            wv = slot.t[:, 0:4 * KC * 128].rearrange("p (o c m) -> p o c m", o=4, c=KC)
            for o in range(4):
                oc = 4 * g + o
                ps = self.ps[oc % 4]
                for c in range(KC):
                    k.mm(ps.t[:], wv[:, o, c, :], self.xnv[:, c, :], c == 0, c == KC - 1, [slot.b, self.xnb[c]], [ps.b])
                if oc > 16:
                    stats_mm(oc - 17)
                if oc < 16:
                    k.act(uv[:, oc, :], ps.t[:], AF.Gelu, [ps.b, self.vec.b], [self.sg_u.b], bias=self.vcol(pre + "sg_b_in", oc))
                else:
                    c = oc - 16
                    k.act(vv[:, c, :], ps.t[:], AF.Gelu, [ps.b, self.vec.b], [self.sg_vb[c]], bias=self.vcol(pre + "sg_b_in", oc))
                    k.cp(zv[:, c, :], vv[:, c, :], [self.sg_vb[c]], [zb[c]])
                    sq = self.sg_sqa[c % 2]
                    k.act(sq.t[:], vv[:, c, :], AF.Square, [self.sg_vb[c]], [sq.b])
        stats_mm(15)
        mean, msq, rstd = self.sg_st
        k.act(mean.t[:], pm.t[:], AF.Copy, [pm.b], [mean.b], scale=1.0 / 2048)
        k.act(msq.t[:], mean.t[:], AF.Square, [mean.b], [msq.b])
        k.stt(msq.t[:], pq.t[:], 1.0 / 2048, msq.t[:], ALU.mult, ALU.subtract, [pq.b, msq.b], [msq.b])
        k.act(msq.t[:], msq.t[:], AF.Ln, [msq.b, self.cf.b], [msq.b], bias=self.cF("eps")[:, 0:1], scale=1.0)
        k.act(rstd.t[:], msq.t[:], AF.Exp, [msq.b], [rstd.b], scale=-0.5)
        k.handoff(zb, [self.sg_z.b])
        slot = self.load_piece(pre + "sgws")
        wsv = self.sg_ws.t[:].rearrange("p (g t) -> p g t", g=16)
        k.tt(wsv, slot.t[:, 0:2048].rearrange("p (g t) -> p g t", g=16),
             self.cB("mask_le").unsqueeze(1).broadcast_to([128, 16, 128]), ALU.mult, [slot.b, self.cb.b], [self.sg_ws.b])
        ident = self.cB("ident")
        onesrow = self.cB("ones", 0, 1)
        k.dma('pool', self.sg_bs.t[0:1, :], self.bsd, writes=[self.sg_bs.b])
        bsv = self.sg_bs.t[:].rearrange("p (g t) -> p g t", g=16)
        ptb = [self.ps[6], self.ps[3]]
        pob = [self.ps[0], self.ps[1]]

        def s1(g):
            t1, vn = self.sg_t1[g % 2], self.sg_vn[g % 2]
            k.tt(t1.t[:], vv[:, g, :], mean.t[:], ALU.subtract, [self.sg_vb[g], mean.b], [t1.b])
            k.tt(t1.t[:], t1.t[:], rstd.t[:], ALU.mult, [t1.b, rstd.b], [t1.b])
            k.ts(vn.t[:], t1.t[:], self.vcol(pre + "sg_ln_g", g), ALU.mult, [t1.b, self.vec.b], [vn.b],
                 s2=self.vcol(pre + "sg_ln_b", g), op1=ALU.add)

        def s2(g):
            pt, vn = ptb[g % 2], self.sg_vn[g % 2]
            ptv = pt.t[:].bitcast(BF16)[:, 0:T].rearrange("p (n c) -> p n c", n=4)
            for n in range(4):
                k.tr(ptv[:, n, :], vn.t[:, n * 128:(n + 1) * 128], ident, [vn.b, self.cb.b], [pt.b])

        def s3(g):
            pt, vt = ptb[g % 2], self.sg_vt[g % 2]
            k.cp(vt.t[:], pt.t[:].bitcast(BF16)[:, 0:T], [pt.b], [vt.b], eng='act')

        def s4(g):
            vt, po = self.sg_vt[g % 2], pob[g % 2]
            vtv = vt.t[:].rearrange("p (n c) -> p n c", n=4)
            for n in range(4):
                k.mm(po.t[:, n * 128:(n + 1) * 128], vtv[:, n, :], wsv[:, g, :], True, False, [vt.b, self.sg_ws.b], [po.b])
                k.mm(po.t[:, n * 128:(n + 1) * 128], onesrow, bsv[0:1, g, :], False, True, [self.cb.b, self.sg_bs.b], [po.b])

        def s5(g):
            po = pob[g % 2]
            k.tt(zv[:, g, :], po.t[:], uv[:, g, :], ALU.mult, [po.b, self.sg_u.b], [self.sg_z.b])
        stages = (s1, s2, s3, s4, s5)
        for step in range(16 + len(stages) - 1):
            for d in range(len(stages) - 1, -1, -1):
                g = step - d
                if 0 <= g < 16:
                    stages[d](g)
        for g in range(4):
            slot = self.load_piece(pre + f"sgwo{g}")
            wv = slot.t[:, 0:2 * 16 * 128].rearrange("p (o c m) -> p o c m", o=2, c=16)
            for o in range(2):
                fc = 2 * g + o
                ps = self.ps[2 + fc % 2]
                for c in range(16):
                    k.mm(ps.t[:], wv[:, o, c, :], zv[:, c, :], c == 0, c == 15, [slot.b, self.sg_z.b], [ps.b])
                k.stt(self.hTv[:, fc, :], ps.t[:], self.vcol(pre + "sg_b_o", fc), self.hTv[:, fc, :], ALU.add, ALU.add,
                      [ps.b, self.vec.b, self.hTb[fc]], [self.hTb[fc]])

    def rwkv(self, li, ti):
        k = self.k
        pre = f"L{li}_"
        R = self.R
        C0 = float(np.exp(-0.5))
        NCH = TR // 64
        identF = self.cF("ident", 0, 64)[:, 0:64]
        identB = self.cB("ident")
        identB64 = self.cB("ident", 0, 64)[:, 0:64]
        blk = self.cF("blk64")
        blkB = self.cB("blk64")
        vec = self.vec
        V = lambda nm, c=0, p0=0, p1=128: self.vcol(pre + nm, c, p0, p1)
        xnF, xx = R["xnF"], R["xx"]
        xnFv = xnF.t[:].rearrange("p (c t) -> p c t", c=KC)
        xxv = xx.t[:].rearrange("p (c t) -> p c t", c=KC)
        if ti == 0:
            c0_ = self.pk.vec_cols[pre + "rw_w0"]
            c1_ = self.pk.vec_cols[pre + "rw_a0"]
            k.ts(self.rw_nb.t[:, 0:8], self.vec.t[:, c0_:c0_ + 8], -1.0, ALU.mult, [self.vec.b], [self.rw_nb.b])
            k.ts(self.rw_nb.t[:, 8:16], self.vec.t[:, c1_:c1_ + 8], -1.0, ALU.mult, [self.vec.b], [self.rw_nb.b])
            k.op('dve', lambda e: e.memset(self.rw_prev.t[:], 0.0), [], [self.rw_prev.b])
            for h in range(8):
                k.op('dve', lambda e, h=h: e.memset(self.rw_Sf[h].t[:], 0.0), [], [self.rw_Sf[h].b] + self.rw_Sfb[h])
                k.op('dve', lambda e, h=h: e.memset(self.rw_Sb[h].t[:], 0.0), [], [self.rw_Sb[h].b])
        self.rmsnorm(pre + "nmix", stats_only=True)
        k.handoff(self.xnb + self.sqb8, [self.xn.b, self.sq.b])
        zv = R["zT"].t[:].rearrange("p (c t) -> p c t", c=KC)
        m320 = self.cF("m320", 0, 64)
        reset = self.cF("reset64")[:, 0:TR]

        import os
        PE_ = os.environ.get('RWPOOL', 'dve')

        def variant(ci, dst, t0):
            dv = dst.t[:, 0:KC * TR].rearrange("p (c t) -> p c t", c=KC)
            for fc in range(KC):
                k.stt(dv[:, fc, :], xxv[:, fc, :], V("rw_mu", ci * 8 + fc), xnFv[:, fc, :], ALU.mult, ALU.add,
                      [xx.b, vec.b, xnF.b], [dst.b])

        def preamble(sub):
            t0 = sub * TR
            for fc in range(KC):
                k.stt(xnFv[:, fc, :], self.hTv[:, fc, t0:t0 + TR], self.vcol(pre + "nmix", fc), self.nrm2.t[:, t0:t0 + TR], ALU.mult, ALU.mult,
                      [self.hTb[fc], vec.b, self.nrm2.b], [xnF.b])
            yield
            k.tt(xxv[:, :, 1:TR], xnFv[:, :, 0:TR - 1], xnFv[:, :, 1:TR], ALU.subtract, [xnF.b], [xx.b])
            k.tt(xxv[:, :, 0], self.rw_prev.t[:], xnFv[:, :, 0], ALU.subtract, [self.rw_prev.b, xnF.b], [xx.b])
            k.cp(self.rw_prev.t[:], xnFv[:, :, TR - 1], [xnF.b], [self.rw_prev.b])
            yield
            slot1 = self.load_piece(pre + "rwl1")
            w1v = slot1.t[:, 0:512].rearrange("p (c m) -> p c m", c=KC)
            a1v = slot1.t[:, 512:1024].rearrange("p (c m) -> p c m", c=KC)
            g1v = slot1.t[:, 1024:2048].rearrange("p (c m) -> p c m", c=KC)
            for (ci, wv_, M, dst, fn) in ((1, w1v, 64, R["w1T"], AF.Tanh), (4, a1v, 64, R["a1T"], AF.Copy), (5, g1v, 128, R["g1T"], AF.Sigmoid)):
                tmp = self.xn if ci != 4 else self.sq
                tv = tmp.t[:, 0:KC * TR].rearrange("p (c t) -> p c t", c=KC)
                variant(ci, tmp, t0)
                ps = self.ps[0]
                for c in range(KC):
                    k.mm(ps.t[0:M, 0:TR], wv_[:, c, :], tv[:, c, :], c == 0, c == KC - 1, [slot1.b, tmp.b], [ps.b])
                one_ = self.cF("one")[0:M, 0:1]
                if fn == AF.Tanh:
                    tq = R["t1"]
                    k.act(tq.t[0:M, :], ps.t[0:M, 0:TR], AF.Exp, [ps.b], [tq.b], scale=2.0)
                    k.act(tq.t[0:M, :], tq.t[0:M, :], AF.Ln, [tq.b, self.cf.b], [tq.b], bias=one_)
                    k.act(tq.t[0:M, :], tq.t[0:M, :], AF.Exp, [tq.b], [tq.b], scale=-1.0)
                    k.ts(dst.t[0:M, :], tq.t[0:M, :], -2.0, ALU.mult, [tq.b], [dst.b], s2=1.0, op1=ALU.add)
                elif fn == AF.Sigmoid:
                    tq = R["t2"]
                    k.act(tq.t[0:M, :], ps.t[0:M, 0:TR], AF.Exp, [ps.b], [tq.b], scale=-1.0)
                    k.act(tq.t[0:M, :], tq.t[0:M, :], AF.Ln, [tq.b, self.cf.b], [tq.b], bias=one_)
                    k.act(dst.t[0:M, :], tq.t[0:M, :], AF.Exp, [tq.b], [dst.b], scale=-1.0)
                else:
                    k.act(dst.t[0:M, :], ps.t[0:M, 0:TR], fn, [ps.b], [dst.b])
                yield
            variant(0, R["xr"], t0)
            yield
            variant(2, R["xk"], t0)
            yield
            variant(3, R["xv"], t0)
            yield

        def stageA1(sub, hp, st, bg):
            slot = self.load_piece(pre + f"rwrkv{hp}")
            wv = slot.t[:, 0:3 * KC * 128].rearrange("p (j c m) -> p j c m", j=3, c=KC)
            rf, kf, vf, sgw, cs, a, kk = (R[n] for n in ("rf", "kf", "vf", "sgw", "cs", "a", "kk"))
            t1, t2, W, Winv, Wex, WLr, ka, kp = (R[n] for n in ("t1", "t2", "W", "Winv", "Wex", "WLr", "ka", "kp"))
            gf, bonus = bg["gf"], bg["bonus"]
            l2v = slot.t[:, 3072:3456]
            sqb = R["BTh"]
            pbank = [self.ps[0], self.ps[2]]
            nb = [0]

            def bank():
                nb[0] += 1
                return pbank[nb[0] % 2]
            for j, (src, dst) in enumerate(((R["xr"], rf), (R["xk"], kf), (R["xv"], vf))):
                sv = src.t[:].rearrange("p (c t) -> p c t", c=KC)
                ps = bank()
                for c in range(KC):
                    k.mm(ps.t[:, 0:TR], wv[:, j, c, :], sv[:, c, :], c == 0, c == KC - 1, [slot.b, src.b], [ps.b])
                yield
                k.cp(dst.t[:], ps.t[:, 0:TR], [ps.b], [dst.b], eng='act')
                if j == 2:
                    k.cp(R["VB"].t[:], ps.t[:, 0:TR], [ps.b], [R["VB"].b])
            ps = bank()
            k.mm(ps.t[:, 0:TR], l2v[0:64, 0:128], R["w1T"].t[0:64, :], True, True, [slot.b, R["w1T"].b], [ps.b])
            ps2 = bank()
            k.mm(ps2.t[:, 0:TR], l2v[0:64, 128:256], R["a1T"].t[0:64, :], True, True, [slot.b, R["a1T"].b], [ps2.b])
            yield
            if os.environ.get("RWSIG"):
                k.act(sgw.t[:], ps.t[:, 0:TR], AF.Sigmoid, [ps.b, vec.b], [sgw.b], bias=V("rw_w0", hp))
                k.act(a.t[:], ps2.t[:, 0:TR], AF.Sigmoid, [ps2.b, vec.b], [a.b], bias=V("rw_a0", hp))
            else:
                k.act(sgw.t[:], ps.t[:, 0:TR], AF.Exp, [ps.b, self.rw_nb.b], [sgw.b], bias=self.rw_nb.t[:, hp:hp + 1], scale=-1.0)
                k.act(a.t[:], ps2.t[:, 0:TR], AF.Exp, [ps2.b, self.rw_nb.b], [a.b], bias=self.rw_nb.t[:, 8 + hp:9 + hp], scale=-1.0)
                k.act(sgw.t[:], sgw.t[:], AF.Ln, [sgw.b, self.cf.b], [sgw.b], bias=self.cF("one")[:, 0:1])
                k.act(a.t[:], a.t[:], AF.Ln, [a.b, self.cf.b], [a.b], bias=self.cF("one")[:, 0:1])
                k.act(sgw.t[:], sgw.t[:], AF.Exp, [sgw.b], [sgw.b], scale=-1.0)
                k.act(a.t[:], a.t[:], AF.Exp, [a.b], [a.b], scale=-1.0)
            ps = bank()
            k.mm(ps.t[:, 0:TR], l2v[:, 256:384], R["g1T"].t[:], True, True, [slot.b, R["g1T"].b], [ps.b])
            k.ts(kk.t[:], kf.t[:], V("rw_k_k", hp), ALU.mult, [kf.b, vec.b], [kk.b])
            yield
            k.cp(gf.t[:], ps.t[:, 0:TR], [ps.b], [gf.b], eng='act')
            k.tt(sqb.t[:], kk.t[:], kk.t[:], ALU.mult, [kk.b], [sqb.b], eng=PE_)
            k.op('dve', lambda e, cs=cs, sgw=sgw: e.tensor_tensor_scan(out=cs.t[:], data0=reset, data1=sgw.t[:], initial=0.0,
                                                                     op0=ALU.mult, op1=ALU.add), [sgw.b, self.cf.b], [cs.b])
            yield
            psk = bank()
            k.mm(psk.t[:, 0:TR], blkB, sqb.t[:], True, True, [self.cb.b, sqb.b], [psk.b])
            csv = cs.t[:].rearrange("p (c j) -> p c j", c=NCH)
            k.act(W.t[:], cs.t[:], AF.Exp, [cs.b], [W.b], scale=-C0)
            k.act(Winv.t[:], cs.t[:], AF.Exp, [cs.b], [Winv.b], scale=C0)
            k.tt(t1.t[:], cs.t[:], sgw.t[:], ALU.subtract, [cs.b, sgw.b], [t1.b], eng=PE_)
            k.tt(t2.t[:].rearrange("p (c j) -> p c j", c=NCH), csv[:, :, 63:64].broadcast_to([128, NCH, 64]), csv, ALU.subtract, [cs.b], [t2.b], eng=PE_)
            yield
            k.act(Wex.t[:], t1.t[:], AF.Exp, [t1.b], [Wex.b], scale=-C0)
            k.act(WLr.t[:], t2.t[:], AF.Exp, [t2.b], [WLr.b], scale=-C0)
            k.act(st["WL"].t[:], csv[:, :, 63], AF.Exp, [cs.b], [st["WL"].b], scale=-C0)
            k.ts(t2.t[:], a.t[:], -1.0, ALU.add, [a.b, WLr.b], [t2.b])
            k.ts(t1.t[:], psk.t[:, 0:TR], 1e-24, ALU.max, [psk.b, Wex.b], [t1.b])
            k.ts(t2.t[:], t2.t[:], V("rw_k_a", hp), ALU.mult, [t2.b, vec.b], [t2.b])
            yield
            k.stt(kp.t[:], t2.t[:], 1.0, kf.t[:], ALU.add, ALU.mult, [t2.b, kf.b], [kp.b])
            k.act(t1.t[:], t1.t[:], AF.Ln, [t1.b], [t1.b])
            k.act(t1.t[:], t1.t[:], AF.Exp, [t1.b], [t1.b], scale=-0.5)
            AR, ARh = st["AR"], st["ARh"]
            ARv = AR.t[:].rearrange("p (c q j) -> p c q j", c=NCH, q=2)
            k.tt(ARv[:, :, 1, :], rf.t[:].rearrange("p (c j) -> p c j", c=NCH), W.t[:].rearrange("p (c j) -> p c j", c=NCH), ALU.mult,
                 [rf.b, W.b], [AR.b], eng=PE_)
            k.tt(R["KT"].t[:], kp.t[:], Winv.t[:], ALU.mult, [kp.b, Winv.b], [R["KT"].b], eng=PE_)
            k.tt(R["KH"].t[:], kp.t[:], WLr.t[:], ALU.mult, [kp.b, WLr.b], [R["KH"].b], eng=PE_)
            k.stt(sqb.t[:], rf.t[:], V("rw_r_k", hp), kp.t[:], ALU.mult, ALU.mult, [rf.b, vec.b, kp.b], [sqb.b])
            yield
            k.dma('sp', R["KTh"].t[0:64, :], R["KT"].t[64:128, :], reads=[R["KT"].b], writes=[R["KTh"].b])
            k.dma('sp', st["WLh"].t[0:64, :], st["WL"].t[64:128, :], reads=[st["WL"].b], writes=[st["WLh"].b])
            psb = bank()
            k.mm(psb.t[:, 0:TR], blkB, sqb.t[:], True, True, [self.cb.b, sqb.b], [psb.b])
            k.tt(kk.t[:], kk.t[:], t1.t[:], ALU.mult, [kk.b, t1.b], [kk.b], eng=PE_)
            k.tt(ka.t[:], kk.t[:], a.t[:], ALU.mult, [kk.b, a.b], [ka.b], eng=PE_)
            k.stt(ARv[:, :, 0, :], kk.t[:].rearrange("p (c j) -> p c j", c=NCH), -1.0, Wex.t[:].rearrange("p (c j) -> p c j", c=NCH),
                  ALU.mult, ALU.mult, [kk.b, Wex.b], [AR.b])
            k.tt(R["BT"].t[:], ka.t[:], Winv.t[:], ALU.mult, [ka.b, Winv.b], [R["BT"].b], eng=PE_)
            k.tt(R["BH"].t[:], ka.t[:], WLr.t[:], ALU.mult, [ka.b, WLr.b], [R["BH"].b], eng=PE_)
            yield
            k.dma('sp', ARh.t[0:64, :], AR.t[64:128, :], reads=[AR.b], writes=[ARh.b])
            k.dma('sp', R["BTh"].t[0:64, :], R["BT"].t[64:128, :], reads=[R["BT"].b], writes=[R["BTh"].b])
            k.tt(bonus.t[:], psb.t[:, 0:TR], vf.t[:], ALU.mult, [psb.b, vf.b], [bonus.b])
            ARe = [AR.t[0:64, :].rearrange("p (c q j) -> p c q j", c=NCH, q=2), ARh.t[0:64, :].rearrange("p (c q j) -> p c q j", c=NCH, q=2)]
            ARb = [AR.b, ARh.b]
            BTe = [R["BT"].t[0:64, :], R["BTh"].t[0:64, :]]
            BTb = [R["BT"].b, R["BTh"].b]
            KTe = [R["KT"].t[0:64, :], R["KTh"].t[0:64, :]]
            KTb = [R["KT"].b, R["KTh"].b]
            tok = st["tok"]
            tokv = tok.t[0:64, :].rearrange("p (c q m) -> p c q m", c=NCH, q=4)
            pts = []
            for cc in range(NCH // 2):
                pt = pbank[cc % 2]
                ptv = pt.t[0:64, :].bitcast(BF16)[:, 0:1024].rearrange("p (c q m) -> p c q m", c=2, q=4)
                for c2 in range(2):
                    c = 2 * cc + c2
                    cs_ = slice(c * 64, (c + 1) * 64)
                    for q, src in enumerate((R["VB"], R["BH"], R["KH"])):
                        k.tr(ptv[:, c2, q, :], src.t[:, cs_], identB, [src.b, self.cb.b], [pt.b])
                    k.tr(ptv[:, c2, 3, :], ARv[:, c, 0, :], identB, [AR.b, self.cb.b], [pt.b])
                pts.append((pt, ptv, cc))
            yield
            for pt, ptv, cc in pts:
                k.cp(tokv[:, 2 * cc:2 * cc + 2, :, :], ptv, [pt.b], [tok.b], eng='act')
            yield
            SC = st["SC"]
            SCv = SC.t[0:64, :].rearrange("p (j n) -> p j n", j=8)
            pend = []
            for j in range(8):
                c, e = j // 2, j % 2
                cs_ = slice(c * 64, (c + 1) * 64)
                psc = pbank[j % 2]
                k.mm(psc.t[0:64, 0:128], BTe[e][:, cs_], ARe[e][:, c, :, :], True, True, [BTb[e], ARb[e]], [psc.b])
                k.mm(psc.t[0:64, 128:256], KTe[e][:, cs_], ARe[e][:, c, :, :], True, True, [KTb[e], ARb[e]], [psc.b])
                k.mm(psc.t[0:64, 256:320], ARe[e][:, c, 0, :], BTe[e][:, cs_], True, True, [BTb[e], ARb[e]], [psc.b])
                pend.append((j, psc))
                if j % 2 == 1:
                    yield
                    for j_, p_ in pend:
                        k.tt(SCv[:, j_, :], p_.t[0:64, 0:320], m320, ALU.mult, [p_.b, self.cf.b], [SC.b])
                    pend = []
            yield

        def stageA2(sub, hp, st):
            SC, Tall = st["SC"], st["Tall"]
            SCv = SC.t[0:64, :].rearrange("p (j n) -> p j n", j=8)
            Tallv = Tall.t[0:64, :].rearrange("p (j n) -> p j n", j=8)
            TT, PP = R["TT"], R["PP"]
            TTv = [t_.t[0:64, :].rearrange("p (j n) -> p j n", j=8) for t_ in TT]
            PPv = [p_.t[0:64, :].rearrange("p (j n) -> p j n", j=8) for p_ in PP]
            k.tt(TTv[0], SCv[:, :, 0:64], identF.unsqueeze(1).broadcast_to([64, 8, 64]), ALU.add, [SC.b, self.cf.b], [TT[0].b])
            ppb = [self.ps[4], self.ps[5]]
            tt_ps = self.ps[6]
            ppv = [b_.t[0:64, :].rearrange("p (j n) -> p j n", j=4) for b_ in ppb]
            ttv = tt_ps.t[0:64, :].rearrange("p (j n) -> p j n", j=8)
            def tmm(s_):
                cur = (s_ - 1) % 2
                tin = (s_ - 1) % 2
                for j in range(8):
                    k.mm(ttv[:, j, :], PPv[cur][:, j, 64:128], TTv[tin][:, j, :], True, True, [PP[cur].b, TT[tin].b], [tt_ps.b])

            def tev(s_):
                tin, tout = (s_ - 1) % 2, s_ % 2
                if s_ < 5:
                    k.tt(TTv[tout], ttv, TTv[tin], ALU.add, [tt_ps.b, TT[tin].b], [TT[tout].b])
                else:
                    k.tt(Tallv, ttv, TTv[tin], ALU.add, [tt_ps.b, TT[tin].b], [Tall.b])
            for s_ in range(1, 6):
                cur, prv = (s_ - 1) % 2, s_ % 2
                for j in range(8):
                    if s_ == 1:
                        P_, PT_ = SCv[:, j, 0:64], SCv[:, j, 256:320]
                        rb = [SC.b]
                    else:
                        P_, PT_ = PPv[prv][:, j, 0:64], PPv[prv][:, j, 64:128]
                        rb = [PP[prv].b]
                    if s_ < 5:
                        k.mm(ppv[j // 4][:, j % 4, 0:64], PT_, P_, True, True, rb, [ppb[j // 4].b])
                    k.mm(ppv[j // 4][:, j % 4, 64:128], P_, PT_, True, True, rb, [ppb[j // 4].b])
                if s_ >= 2:
                    tmm(s_ - 1)
                yield
                if s_ >= 2:
                    tev(s_ - 1)
                k.cp(PPv[cur][:, 0:4, :], ppv[0], [ppb[0].b], [PP[cur].b], eng='act')
                k.cp(PPv[cur][:, 4:8, :], ppv[1], [ppb[1].b], [PP[cur].b], eng='act')
                yield
            tmm(5)
            yield
            tev(5)
            yield
            tok = st["tok"]
            tokv = tok.t[0:64, :].rearrange("p (c q m) -> p c q m", c=NCH, q=4)
            AT, Uv, RV = st["AT"], st["Uv"], R["PP"][0]
            ATv = AT.t[0:64, :].rearrange("p (j n) -> p j n", j=8)
            Uvv = Uv.t[0:64, :].rearrange("p (j n) -> p j n", j=8)
            RVv = RV.t[0:64, 0:512].rearrange("p (j n) -> p j n", j=8)
            pa_, pb_ = self.ps[4], self.ps[5]
            pav = pa_.t[0:64, :].rearrange("p (j n) -> p j n", j=8)
            pbv = pb_.t[0:64, :].rearrange("p (j n) -> p j n", j=8)
            for j in range(8):
                c, e = j // 2, j % 2
                es = slice(e * 64, (e + 1) * 64)
                k.mm(pav[:, j, :], tokv[:, c, 3, es], Tallv[:, j, :], True, True, [tok.b, Tall.b], [pa_.b])
                k.mm(pbv[:, j, :], SCv[:, j, 128:192], tokv[:, c, 0, es], True, True, [SC.b, tok.b], [pb_.b])
            yield
            k.cp(ATv, pav, [pa_.b], [AT.b], eng='act')
            k.cp(RVv, pbv, [pb_.b], [RV.b], eng='act')
            yield
            for j in range(8):
                k.mm(pav[:, j, :], Tallv[:, j, :], RVv[:, j, :], True, True, [Tall.b, RV.b], [pa_.b])
            yield
            k.cp(Uvv, pav, [pa_.b], [Uv.b], eng='act')
            yield

        def stageB(sub, hp, st):
            AR, ARh, tok, SC, Tall = st["AR"], st["ARh"], st["tok"], st["SC"], st["Tall"]
            ARe = [AR.t[0:64, :].rearrange("p (c q j) -> p c q j", c=NCH, q=2), ARh.t[0:64, :].rearrange("p (c q j) -> p c q j", c=NCH, q=2)]
            ARb = [AR.b, ARh.b]
            WLe = [st["WL"].t[0:64, :], st["WLh"].t[0:64, :]]
            WLb = [st["WL"].b, st["WLh"].b]
            tokv = tok.t[0:64, :].rearrange("p (c q m) -> p c q m", c=NCH, q=4)
            SCv = SC.t[0:64, :].rearrange("p (j n) -> p j n", j=8)
            Tallv = Tall.t[0:64, :].rearrange("p (j n) -> p j n", j=8)
            Sf, Sb = self.rw_Sf[hp], self.rw_Sb[hp]
            Ub = R["Ub"]
            ATv = st["AT"].t[0:64, :].rearrange("p (j n) -> p j n", j=8)
            Uvv = st["Uv"].t[0:64, :].rearrange("p (j n) -> p j n", j=8)
            pst, psY = self.ps[7], self.ps[1]
            for c in range(NCH):
                cs_ = slice(c * 64, (c + 1) * 64)
                for e in range(2):
                    es = slice(e * 64, (e + 1) * 64)
                    k.mm(pst.t[0:64, es], ATv[:, 2 * c + e, :], Sb.t[:, es], True, True, [st["AT"].b, Sb.b], [pst.b])
                for e in range(2):
                    es = slice(e * 64, (e + 1) * 64)
                    k.mm(psY.t[es, cs_], Sb.t[:, es], ARe[e][:, c, 1, :], True, False, [Sb.b, ARb[e]], [psY.b])
                    k.mm(psY.t[es, cs_], tokv[:, c, 0, es], SCv[:, 2 * c + e, 192:256], False, False, [tok.b, SC.b], [psY.b], skip=True)
                yield
                k.tt(Ub.t[0:64, :].rearrange("p (e v) -> p e v", e=2), pst.t[0:64, 0:128].rearrange("p (e v) -> p e v", e=2),
                     Uvv[:, 2 * c:2 * c + 2, :], ALU.add, [pst.b, st["Uv"].b], [Ub.b])
                yield
                for e in range(2):
                    es = slice(e * 64, (e + 1) * 64)
                    k.mm(pst.t[0:64, 256 + e * 64:320 + e * 64], tokv[:, c, 1, es], Ub.t[0:64, es], True, False, [tok.b, Ub.b], [pst.b])
                    k.mm(pst.t[0:64, 256 + e * 64:320 + e * 64], tokv[:, c, 2, es], tokv[:, c, 0, es], False, True, [tok.b], [pst.b])
                for e in range(2):
                    es = slice(e * 64, (e + 1) * 64)
                    k.mm(psY.t[es, cs_], Ub.t[0:64, es], SCv[:, 2 * c + e, 64:128], False, True, [Ub.b, SC.b], [psY.b], skip=True)
                yield
                for e in range(2):
                    es = slice(e * 64, (e + 1) * 64)
                    k.stt(Sb.t[:, es], Sf.t[:, es], WLe[e][:, c:c + 1], pst.t[0:64, 256 + e * 64:320 + e * 64], ALU.mult, ALU.add,
                          [self.rw_Sfb[hp][e], WLb[e], pst.b], [Sb.b])
                for e in range(2):
                    es = slice(e * 64, (e + 1) * 64)
                    k.stt(Sf.t[:, es], Sf.t[:, es], WLe[e][:, c:c + 1], pst.t[0:64, 256 + e * 64:320 + e * 64], ALU.mult, ALU.add,
                          [self.rw_Sfb[hp][e], WLb[e], pst.b], [self.rw_Sfb[hp][e]])
                yield
            y = R["y"][hp % 2]
            yb = R["Ub2"][hp % 2]
            k.cp(y.t[:], psY.t[:, 0:TR], [psY.b], [y.b], eng='act')
            k.cp(yb.t[:], psY.t[:, 0:TR], [psY.b], [yb.b])
            yield

        def stageBe(sub, hp, bg):
            y, yc, ysq = R["y"][hp % 2], R["yc"], R["ysq"]
            yb = R["Ub2"][hp % 2]
            pe_ = self.ps[3]
            pev = pe_.t[:, 0:TR]
            k.mm(pev, blkB, yb.t[:], True, True, [self.cb.b, yb.b], [pe_.b])
            yield
            k.stt(yc.t[:], pev, -1.0 / 64, y.t[:], ALU.mult, ALU.add, [pe_.b, y.b], [yc.b])
            yield
            k.act(yb.t[:], yc.t[:], AF.Square, [yc.b], [yb.b])
            yield
            k.mm(pev, blkB, yb.t[:], True, True, [self.cb.b, yb.b], [pe_.b])
            yield
            k.act(ysq.t[:], pev, AF.Ln, [pe_.b, self.cf.b], [ysq.b], bias=self.cF("gneps")[:, 0:1], scale=1.0 / 64)
            k.act(ysq.t[:], ysq.t[:], AF.Exp, [ysq.b], [ysq.b], scale=-0.5)
            yield
            k.tt(yc.t[:], yc.t[:], ysq.t[:], ALU.mult, [yc.b, ysq.b], [yc.b])
            yield
            k.ts(yc.t[:], yc.t[:], V("rw_gn_g", hp), ALU.mult, [yc.b, vec.b], [yc.b], s2=V("rw_gn_b", hp), op1=ALU.add)
            yield
            k.tt(yc.t[:], yc.t[:], bg["bonus"].t[:], ALU.add, [yc.b, bg["bonus"].b], [yc.b])
            yield
            k.tt(zv[:, hp, :], yc.t[:], bg["gf"].t[:], ALU.mult, [yc.b, bg["gf"].b], [R["zT"].b])
            yield
            if hp == 7:
                t0 = sub * TR
                for g in range(2):
                    slot = self.load_piece(pre + f"rwwo{g}")
                    wv = slot.t[:, 0:4 * KC * 128].rearrange("p (o c m) -> p o c m", o=4, c=KC)
                    for o in range(4):
                        fc = 4 * g + o
                        ps = self.ps[3]
                        for c in range(KC):
                            k.mm(ps.t[:, 0:TR], wv[:, o, c, :], zv[:, c, :], c == 0, c == KC - 1, [slot.b, R["zT"].b], [ps.b])
                        yield
                        k.tt(self.hTv[:, fc, t0:t0 + TR], self.hTv[:, fc, t0:t0 + TR], ps.t[:, 0:TR], ALU.add, [self.hTb[fc], ps.b], [self.hTb[fc]])
                        yield

        def chain(*gens):
            for g in gens:
                yield from g

        def interleave(*gens):
            gens = [g for g in gens if g is not None]
            done = [False] * len(gens)
            while not all(done):
                for i_, g in enumerate(gens):
                    if not done[i_]:
                        try:
                            next(g)
                        except StopIteration:
                            done[i_] = True

        tasks = [(sub, hp) for sub in range(T // TR) for hp in range(8)]
        sets = R["sets"]
        N_ = len(tasks)
        bgs = R["bg"]
        for i in range(N_ + 3):
            g1 = g2 = g3 = g4 = None
            if i < N_:
                sub, hp = tasks[i]
                g1 = stageA1(sub, hp, sets[i % 3], bgs[i % 4])
                if hp == 0:
                    g1 = chain(preamble(sub), g1)
            if 0 <= i - 1 < N_:
                sub, hp = tasks[i - 1]
                g2 = stageA2(sub, hp, sets[(i - 1) % 3])
            if 0 <= i - 2 < N_:
                sub, hp = tasks[i - 2]
                g3 = stageB(sub, hp, sets[(i - 2) % 3])
            if os.environ.get("RWDBG") == "1":
                if g3 is not None:
                    sub, hp = tasks[i - 2]
                    g3 = chain(g3, stageBe(sub, hp, bgs[(i - 2) % 4]))
            elif 0 <= i - 3 < N_:
                sub, hp = tasks[i - 3]
                g4 = stageBe(sub, hp, bgs[(i - 3) % 4])
            interleave(g4, g3, g2, g1)
        k.handoff([self.xn.b, self.sq.b], self.xnb + self.sqb8)

    def swa(self, li, ti):
        k = self.k
        pre = f"L{li}_"
        xn = self.xn
        ident = self.cB("ident")
        t0 = ti * T
        if ti == 0:
            k.act(self.sw_es.t[:], self.vec.t[:, self.pk.vec_cols[pre + "sw_sink"]:self.pk.vec_cols[pre + "sw_sink"] + 8], AF.Exp,
                  [self.vec.b], [self.sw_es.b])
        pi_, y, f, g, ni = self.sw_pi, self.sw_y, self.sw_f, self.sw_g, self.sw_ni
        k.dma('sp', pi_.t[:], self.pos[0, t0:t0 + T].partition_broadcast(128), writes=[pi_.b])
        k.cp(y.t[:], pi_.t[:], [pi_.b], [y.b])
        k.ts(y.t[:], y.t[:], self.cF("invf")[:, 0:1], ALU.mult, [y.b, self.cf.b], [y.b])
        for which, dst in ((0, self.sw_sin), (1, self.sw_cos)):
            if which == 1:
                k.ts(y.t[:], y.t[:], 0.25, ALU.add, [y.b], [y.b])
            k.cp(ni.t[:], y.t[:], [y.b], [ni.b])
            k.cp(f.t[:], ni.t[:], [ni.b], [f.b])
            k.tt(f.t[:], y.t[:], f.t[:], ALU.subtract, [y.b, f.b], [f.b])
            k.ts(g.t[:], f.t[:], 0.5, ALU.is_gt, [f.b], [g.b])
            k.tt(f.t[:], f.t[:], g.t[:], ALU.subtract, [f.b, g.b], [f.b])
            k.ts(g.t[:], f.t[:], -0.5, ALU.is_lt, [f.b], [g.b])
            k.tt(f.t[:], f.t[:], g.t[:], ALU.add, [f.b, g.b], [f.b])
            k.act(dst.t[:], f.t[:], AF.Sin, [f.b], [dst.b], scale=2.0 * np.pi * (1.0 - 1e-6))
        import os
        stop = int(os.environ.get("SWSTOP", "9"))
        self.preload_exp_table()
        rotT = self.cF("rotT")
        pieces = []
        for pname, G in (("swin0", 4), ("swin1", 4), ("swin2", 2)):
            for o in range(G):
                pieces.append((pname, G, o))

        def post(oc):
            if oc < 9:
                qf, t1, t2 = self.sw_qf[oc % 2], self.sw_t1[oc % 2], self.sw_t2[oc % 2]
                pr = self.ps[2 + oc % 2]
                k.mm(pr.t[:], rotT, qf.t[:], True, True, [self.cf.b, qf.b], [pr.b])
                k.tt(t1.t[:], qf.t[:], self.sw_cos.t[:], ALU.mult, [qf.b, self.sw_cos.b], [t1.b])
                k.tt(t2.t[:], pr.t[:], self.sw_sin.t[:], ALU.mult, [pr.b, self.sw_sin.b], [t2.b])
                if oc < 8:
                    k.tt(self.sw_qb[oc].t[:], t1.t[:], t2.t[:], ALU.add, [t1.b, t2.b], [self.sw_qb[oc].b])
                else:
                    k.tt(self.sw_kT.t[:, 128:640], t1.t[:], t2.t[:], ALU.add, [t1.b, t2.b], [self.sw_kT.b])
            else:
                pt = self.ps[2]
                ptv = pt.t[:].bitcast(BF16)[:, 0:512]
                for n in range(4):
                    k.tr(ptv[:, n * 128:(n + 1) * 128], self.sw_vb.t[:, n * 128:(n + 1) * 128], ident, [self.sw_vb.b, self.cb.b], [pt.b])
                k.cp(self.sw_vt.t[:, 128:640], ptv, [pt.b], [self.sw_vt.b], eng='act')
        slot = None
        for oc, (pname, G, o) in enumerate(pieces):
            if o == 0:
                slot = self.load_piece(pre + pname)
                wv = slot.t[:, 0:G * KC * 128].rearrange("p (o c m) -> p o c m", o=G, c=KC)
            ps = self.ps[oc % 2]
            for c in range(KC):
                k.mm(ps.t[:], wv[:, o, c, :], self.xnv[:, c, :], c == 0, c == KC - 1, [slot.b, self.xnb[c]], [ps.b])
            if oc < 9:
                qf = self.sw_qf[oc % 2]
                k.act(qf.t[:], ps.t[:], AF.Identity, [ps.b, self.vec.b], [qf.b], bias=self.vcol(pre + "sw_b_qkv", oc))
            else:
                k.act(self.sw_vb.t[:], ps.t[:], AF.Identity, [ps.b, self.vec.b], [self.sw_vb.b], bias=self.vcol(pre + "sw_b_qkv", oc))
            if oc >= 1:
                post(oc - 1)
        post(9)
        o_le = CST["mask_le"][0]
        mask_cp = self.cb.t[:, o_le:o_le + 256]
        onesb = self.cB("ones")
        ov = self.sw_o.t[:].rearrange("p (c t) -> p c t", c=8)

        def sc(c):
            qb = self.sw_qb[c]
            E = self.sw_E[c % 2]
            for kb in range(5):
                if kb == 0 and ti == 0:
                    continue
                Ev = E[kb].t[:].rearrange("p (e q) -> p e q", e=2)
                lo, hi = (128, 256) if kb == 0 else ((0, 128) if kb == 4 else (0, 256))
                q0 = (kb - 1) * 128 + lo
                for e in range(2):
                    ps = self.ps[e + 2 * (kb % 2)]
                    k.mm(ps.t[:, lo:hi], self.sw_kT.t[e * 64:(e + 1) * 64, kb * 128:(kb + 1) * 128],
                         qb.t[e * 64:(e + 1) * 64, q0:q0 + (hi - lo)], True, True, [self.sw_kT.b, qb.b], [ps.b])
                    k.act(Ev[:, e, lo:hi], ps.t[:, lo:hi], AF.Exp, [ps.b], [E[kb].b], scale=0.125)
                k.tt(Ev[:, :, lo:hi], Ev[:, :, lo:hi], mask_cp[:, lo:hi].unsqueeze(1).broadcast_to([128, 2, hi - lo]), ALU.mult,
                     [E[kb].b, self.cb.b], [E[kb].b])

        def pv(c):
            E = self.sw_E[c % 2]
            pnum, pden = self.ps[4 + 2 * (c % 2)], self.ps[5 + 2 * (c % 2)]
            for n in range(4):
                for e in range(2):
                    has_prev = not (ti == 0 and n == 0)
                    for (pp, isnum) in ((pnum, True), (pden, False)):
                        outap = pp.t[e * 64:(e + 1) * 64, n * 128:(n + 1) * 128]
                        if has_prev:
                            Ep = E[n].t[:].rearrange("p (e q) -> p e q", e=2)[:, e, 128:256]
                            lh = self.sw_vt.t[:, n * 128 + e * 64:n * 128 + (e + 1) * 64] if isnum else onesb[:, 0:64]
                            k.mm(outap, lh, Ep, True, False, [self.sw_vt.b, self.cb.b, E[n].b], [pp.b])
                        Ec = E[n + 1].t[:].rearrange("p (e q) -> p e q", e=2)[:, e, 0:128]
                        lh = self.sw_vt.t[:, (n + 1) * 128 + e * 64:(n + 1) * 128 + (e + 1) * 64] if isnum else onesb[:, 0:64]
                        k.mm(outap, lh, Ec, not has_prev, True, [self.sw_vt.b, self.cb.b, E[n + 1].b], [pp.b])

        def nm(c):
            pnum, pden = self.ps[4 + 2 * (c % 2)], self.ps[5 + 2 * (c % 2)]
            ds = self.sw_ds[c % 2]
            k.ts(ds.t[:], pden.t[:], self.sw_es.t[:, c:c + 1], ALU.add, [pden.b, self.sw_es.b], [ds.b])
            k.act(ds.t[:], ds.t[:], AF.Ln, [ds.b], [ds.b])
            k.act(ds.t[:], ds.t[:], AF.Exp, [ds.b], [ds.b], scale=-1.0)
            k.tt(ov[:, c, :], pnum.t[:], ds.t[:], ALU.mult, [pnum.b, ds.b], [self.sw_o.b])
        for step in range(8 + 2):
            if step < 8:
                sc(step)
            if 0 <= step - 1 < 8:
                pv(step - 1)
            if 0 <= step - 2 < 8:
                nm(step - 2)
        k.cp(self.sw_kT.t[:, 0:128], self.sw_kT.t[:, 512:640], [self.sw_kT.b], [self.sw_kT.b])
        k.cp(self.sw_vt.t[:, 0:128], self.sw_vt.t[:, 512:640], [self.sw_vt.b], [self.sw_vt.b])
        for g in range(2):
            slot = self.load_piece(pre + f"swwo{g}")
            wv = slot.t[:, 0:4 * KC * 128].rearrange("p (o c m) -> p o c m", o=4, c=KC)
            for o in range(4):
                fc = 4 * g + o
                ps = self.ps[fc % 2]
                for c in range(KC):
                    k.mm(ps.t[:], wv[:, o, c, :], ov[:, c, :], c == 0, c == KC - 1, [slot.b, self.sw_o.b], [ps.b])
                k.stt(self.hTv[:, fc, :], ps.t[:], self.vcol(pre + "sw_b_o", fc), self.hTv[:, fc, :], ALU.add, ALU.add,
                      [ps.b, self.vec.b, self.hTb[fc]], [self.hTb[fc]])

    def gla(self, li, ti):
        k = self.k
        pre = f"L{li}_"
        xn = self.xn
        ident = self.cB("ident")
        ones = self.cB("ones")
        if ti == 0:
            c0_ = self.pk.vec_cols[pre + "gl_b_a"]
            k.ts(self.gl_nb.t[:], self.vec.t[:, c0_:c0_ + 4], -1.0, ALU.mult, [self.vec.b], [self.gl_nb.b])
            for h in range(4):
                k.op('dve', lambda e, h=h: e.memset(self.gl_Sf[h].t[:], 0.0), [], [self.gl_Sf[h].b])
                k.op('dve', lambda e, h=h: e.memset(self.gl_Sb[h].t[:], 0.0), [], [self.gl_Sb[h].b])
        k.handoff(self.owners['m3b'], self.owners['m3a'])

        def run(*gens):
            gens = list(gens)
            done = [False] * len(gens)
            while not all(done):
                for i_, g in enumerate(gens):
                    if not done[i_]:
                        try:
                            next(g)
                        except StopIteration:
                            done[i_] = True
        slot = self.load_piece(pre + "glal")
        wv = slot.t[:, 0:KC * 16].rearrange("p (c m) -> p c m", c=KC)
        ps = self.ps[4]
        for c in range(KC):
            k.mm(ps.t[0:16, :], wv[:, c, :], self.xnv[:, c, :], c == 0, c == KC - 1, [slot.b, self.xnb[c]], [ps.b])
        k.cp(self.gl_al.t[0:16, :], ps.t[0:16, :], [ps.b], [self.gl_al.b], eng='act')
        slot = self.load_piece(pre + "glwa2")
        k.cp(self.gl_wa2.t[0:16, :], slot.t[0:16, 0:512], [slot.b], [self.gl_wa2.b])

        def dec(h):
            ps = self.ps[h]
            la = self.gl_la[h]
            k.mm(ps.t[:], self.gl_wa2.t[0:16, h * 128:(h + 1) * 128], self.gl_al.t[0:16, :], True, True,
                 [self.gl_wa2.b, self.gl_al.b], [ps.b])
            yield
            k.act(la.t[:], ps.t[:], AF.Exp, [ps.b, self.gl_nb.b], [la.b], bias=self.gl_nb.t[:, h:h + 1], scale=-1.0)
            k.act(la.t[:], la.t[:], AF.Ln, [la.b, self.cf.b], [la.b], bias=self.cF("one")[:, 0:1])
            yield
            k.op('dve', lambda e, la=la: e.tensor_tensor_scan(out=la.t[:], data0=self.cF("reset64"), data1=la.t[:], initial=0.0,
                                                              op0=ALU.mult, op1=ALU.add), [la.b, self.cf.b], [la.b])
            yield
            lav = la.t[:].rearrange("p (c j) -> p c j", c=8)
            k.act(self.gl_eb[h].t[:], la.t[:], AF.Exp, [la.b], [self.gl_eb[h].b], scale=-1.0 / 16)
            k.act(self.gl_enb[h].t[:], la.t[:], AF.Exp, [la.b], [self.gl_enb[h].b], scale=1.0 / 16)
            k.act(self.gl_dc[h].t[:], lav[:, :, 63], AF.Exp, [la.b], [self.gl_dc[h].b], scale=-1.0 / 16)
            ksfv = self.gl_ksf[h].t[:].rearrange("p (c j) -> p c j", c=8)
            k.tt(ksfv, lav[:, :, 63:64].broadcast_to([128, 8, 64]), lav, ALU.subtract, [la.b], [self.gl_ksf[h].b])
            yield
            k.act(self.gl_ksf[h].t[:], self.gl_ksf[h].t[:], AF.Exp, [self.gl_ksf[h].b], [self.gl_ksf[h].b], scale=-1.0 / 16)
        run(*[dec(h) for h in range(4)])
        for g in range(6):
            slot = self.load_piece(pre + f"glin{g}")
            wv = slot.t[:, 0:4 * KC * 128].rearrange("p (o c m) -> p o c m", o=4, c=KC)
            for o in range(4):
                oc = 4 * g + o
                ps = self.ps[oc % 4]
                for c in range(KC):
                    k.mm(ps.t[:], wv[:, o, c, :], self.xnv[:, c, :], c == 0, c == KC - 1, [slot.b, self.xnb[c]], [ps.b])
                if oc < 4:
                    h = oc
                    k.stt(self.gl_qg[h].t[:], ps.t[:], 128.0 ** -0.5, self.gl_eb[h].t[:], ALU.mult, ALU.mult,
                          [ps.b, self.gl_eb[h].b], [self.gl_qg[h].b])
                elif oc < 8:
                    h = oc - 4
                    k.tt(self.gl_kg[h].t[:], ps.t[:], self.gl_enb[h].t[:], ALU.mult, [ps.b, self.gl_enb[h].b], [self.gl_kg[h].b])
                    k.tt(self.gl_ks[h].t[:], ps.t[:], self.gl_ksf[h].t[:], ALU.mult, [ps.b, self.gl_ksf[h].b], [self.gl_ks[h].b])
                elif oc < 16:
                    k.cp(self.gl_v[oc - 8].t[:], ps.t[:], [ps.b], [self.gl_v[oc - 8].b], eng='act')
                else:
                    k.act(self.gl_gate[oc - 16].t[:], ps.t[:], AF.Silu, [ps.b], [self.gl_gate[oc - 16].b])
        self.preload_exp_table()
        k.handoff(self.owners['m3a'], self.owners['m3b'])
        zv = self.gl_z.t[:].rearrange("p (c t) -> p c t", c=8)
        mask = self.cF("mask_le", 0, 64)[:, 0:64]

        def prep(h):
            tok = self.gl_tok[h]
            tokv = tok.t[0:64, :].rearrange("p (c m) -> p c m", c=8)
            att = self.gl_att[h]
            attv = att.t[0:64, :].rearrange("p (c i) -> p c i", c=8)
            pt = self.ps[h]
            for cc in range(4):
                ptv = pt.t[0:64, :].bitcast(BF16)[:, 0:768].rearrange("p (c m) -> p c m", c=2)
                for c2 in range(2):
                    c = 2 * cc + c2
                    cs = slice(c * 64, (c + 1) * 64)
                    k.tr(ptv[:, c2, 0:128], self.gl_ks[h].t[:, cs], ident, [self.gl_ks[h].b, self.cb.b], [pt.b])
                    for e in range(2):
                        k.tr(ptv[:, c2, 128 + e * 128:256 + e * 128], self.gl_v[2 * h + e].t[:, cs], ident,
                             [self.gl_v[2 * h + e].b, self.cb.b], [pt.b])
                yield
                k.cp(tokv[:, 2 * cc:2 * cc + 2, :], ptv, [pt.b], [tok.b], eng='act')
                yield
            pa = pt
            for c in range(8):
                cs = slice(c * 64, (c + 1) * 64)
                k.mm(pa.t[0:64, cs], self.gl_kg[h].t[:, cs], self.gl_qg[h].t[:, cs], True, True,
                     [self.gl_kg[h].b, self.gl_qg[h].b], [pa.b])
            yield
            k.tt(attv, pa.t[0:64, :].rearrange("p (c i) -> p c i", c=8), mask.unsqueeze(1).broadcast_to([64, 8, 64]), ALU.mult,
                 [pa.b, self.cf.b], [att.b])
            yield
        run(*[prep(h) for h in range(4)])

        ofv = self.gl_ofa.t[:].rearrange("p (x t) -> p x t", x=8)
        osqv = self.gl_osqa.t[:].rearrange("p (x t) -> p x t", x=8)
        tokvs = [self.gl_tok[h].t[0:64, :].rearrange("p (c m) -> p c m", c=8) for h in range(4)]
        attvs = [self.gl_att[h].t[0:64, :].rearrange("p (c i) -> p c i", c=8) for h in range(4)]
        for c in range(8):
            cs = slice(c * 64, (c + 1) * 64)
            po = self.ps[2 + c % 2]
            pov = po.t[:].rearrange("p (x i) -> p x i", x=8)
            pss = [self.ps[4], self.ps[5]]
            for h in range(4):
                for e in range(2):
                    k.mm(pov[:, 2 * h + e, :], tokvs[h][:, c, 128 + e * 128:256 + e * 128], attvs[h][:, c, :], True, False,
                         [self.gl_tok[h].b, self.gl_att[h].b], [po.b])
                    k.mm(pov[:, 2 * h + e, :], self.gl_Sb[h].t[:, e * 128:(e + 1) * 128], self.gl_qg[h].t[:, cs], False, True,
                         [self.gl_Sb[h].b, self.gl_qg[h].b], [po.b])
            for h in range(4):
                k.mm(pss[h // 2].t[:, (h % 2) * 256:(h % 2) * 256 + 256], tokvs[h][:, c, 0:128], tokvs[h][:, c, 128:384], True, True,
                     [self.gl_tok[h].b], [pss[h // 2].b])
            k.cp(ofv[:, :, cs], pov, [po.b], [self.gl_ofa.b], eng='act')
            for h in range(4):
                k.stt(self.gl_Sf[h].t[:], self.gl_Sf[h].t[:], self.gl_dc[h].t[:, c:c + 1], pss[h // 2].t[:, (h % 2) * 256:(h % 2) * 256 + 256],
                      ALU.mult, ALU.add, [self.gl_Sf[h].b, self.gl_dc[h].b, pss[h // 2].b], [self.gl_Sf[h].b])
            for h in range(4):
                k.cp(self.gl_Sb[h].t[:], self.gl_Sf[h].t[:], [self.gl_Sf[h].b], [self.gl_Sb[h].b], eng='act')
        k.act(self.gl_osqa.t[:], self.gl_ofa.t[:], AF.Square, [self.gl_ofa.b], [self.gl_osqa.b])

        def epi(h):
            q = h % 2
            pn = self.ps[q]
            rs = self.gl_rs[q]
            for e in range(2):
                k.mm(pn.t[:], ones, osqv[:, 2 * h + e, :], e == 0, e == 1, [self.gl_osqa.b, self.cb.b], [pn.b])
            yield
            k.act(rs[0].t[:], pn.t[:], AF.Ln, [pn.b, self.cf.b], [rs[0].b], bias=self.cF("eps")[:, 0:1], scale=1.0 / 256)
            k.act(rs[1].t[:], rs[0].t[:], AF.Exp, [rs[0].b], [rs[1].b], scale=-0.5)
            yield
            for e in range(2):
                k.stt(rs[0].t[:], ofv[:, 2 * h + e, :], self.vcol(pre + "gl_gn_g", 2 * h + e), rs[1].t[:], ALU.mult, ALU.mult,
                      [self.gl_ofa.b, self.vec.b, rs[1].b], [rs[0].b])
                k.tt(zv[:, 2 * h + e, :], rs[0].t[:], self.gl_gate[2 * h + e].t[:], ALU.mult,
                     [rs[0].b, self.gl_gate[2 * h + e].b], [self.gl_z.b])
            yield
        run(epi(0), epi(1))
        run(epi(2), epi(3))
        for g in range(2):
            slot = self.load_piece(pre + f"glwo{g}")
            wv = slot.t[:, 0:4 * KC * 128].rearrange("p (o c m) -> p o c m", o=4, c=KC)
            for o in range(4):
                fc = 4 * g + o
                ps = self.ps[fc % 2]
                for c in range(KC):
                    k.mm(ps.t[:], wv[:, o, c, :], zv[:, c, :], c == 0, c == KC - 1, [slot.b, self.gl_z.b], [ps.b])
                k.tt(self.hTv[:, fc, :], self.hTv[:, fc, :], ps.t[:], ALU.add, [self.hTb[fc], ps.b], [self.hTb[fc]])

    def build(self):
        k = self.k
        self.prologue()
        xTv = self.xT.rearrange("(c p) s -> p c s", p=128)
        oTv = self.outT.rearrange("(c p) s -> p c s", p=128)
        types = set(mt for mt, _ in self.layers)
        if 2 in types:
            li = [l for mt, l in self.layers if mt == 2][0]
        for ti in range(self.NT):
            t0 = ti * T
            for c in range(KC):
                k.dma('sp', self.hTv[:, c, :], xTv[:, c, t0:t0 + T], writes=[self.hTb[c]])
            for mt, li in self.layers:
                pre = f"L{li}_"
                if mt == 0:
                    self.switch("m0")
                    self.rwkv(li, ti)
                elif mt is not None:
                    self.rmsnorm(pre + "nmix")
                    self.switch(f"m{mt}")
                    if mt == 2:
                        self.sgu(li, ti)
                    if mt == 3:
                        self.gla(li, ti)
                    if mt == 1:
                        self.swa(li, ti)
                if self.do_ffn:
                    self.rmsnorm(pre + "nffn")
                    self.switch('ffn')
                    self.ffn(li)
            self.switch('ffn')
            self.rmsnorm("nfinal", out_f32=self.fin)
            k.dma('sp', oTv[:, :, t0:t0 + T], self.fin.t[:].rearrange("p (c t) -> p c t", c=KC), reads=[self.fin.b])
        k.finish()


CST = {}
CST_N = 0
CSTB_N = 640


def _make_consts():
    global CST_N
    cols = []

    def add(name, m):
        global CST_N
        m = np.asarray(m, np.float32)
        CST[name] = (CST_N, m.shape[1])
        cols.append(m)
        CST_N += m.shape[1]
    i = np.arange(128)
    add("ident", np.eye(128))
    add("ones", np.ones((128, 128)))
    add("mask_le", (i[:, None] <= i[None, :]).astype(np.float32))
    add("mask_gt", (i[:, None] > i[None, :]).astype(np.float32))
    add("blk64", (i[:, None] // 64 == i[None, :] // 64).astype(np.float32))
    add("eps", np.full((128, 1), EPS))
    invf = (10000.0 ** (-np.arange(0, 64, 2, dtype=np.float32) / 64)).astype(np.float32)
    add("invf", (invf[i % 32].astype(np.float64) / (2 * np.pi)).astype(np.float32)[:, None])
    rot = np.zeros((128, 128), np.float32)
    for m in range(128):
        if m % 64 < 32:
            rot[m + 32, m] = -1.0
        else:
            rot[m - 32, m] = 1.0
    add("rotT", rot)
    j64 = np.arange(64)
    lt = (j64[:, None] < j64[None, :]).astype(np.float32)
    le = (j64[:, None] <= j64[None, :]).astype(np.float32)
    gt = (j64[:, None] > j64[None, :]).astype(np.float32)
    m320 = np.zeros((128, 320), np.float32)
    m320[0:64] = np.concatenate([lt, le, lt, le, gt], 1)
    add("m320", m320)
    add("gneps", np.full((128, 1), 64e-5))
    add("one", np.full((128, 1), 1.0))
    add("reset64", np.tile((np.arange(512) % 64 != 0).astype(np.float32)[None, :], (128, 1)))
    return np.ascontiguousarray(np.concatenate(cols, 1))


CST_ARR = _make_consts()


def run_model(inputs, S, layers, n_cores, do_ffn=True):
    inp = {k_: np.asarray(v) for k_, v in inputs.items()}
    pk = Packer()
    order = []
    pk.vec("nfinal", inp['norm_final'])
    for mt, li in layers:
        pack_layer(pk, inp, li, mt, 0)
        if mt == 2:
            pack_sgu(pk, inp, li, order)
        if mt == 3:
            pack_gla(pk, inp, li, order)
        if mt == 1:
            pack_swa(pk, inp, li, order)
        if mt == 0:
            pack_rwkv(pk, inp, li, order)
        if do_ffn:
            pack_ffn(pk, inp, li, order)
    wpk, vec = pk.finish()
    nc = bass.Bass("TRN2", target_bir_lowering=False)
    with ExitStack() as es:
        prog = Prog(nc, es, S, layers, pk, do_ffn)
        if any(mt == 2 for mt, _ in layers):
            prog.bsd = nc.dram_tensor("bsd", [1, 2048], F32, kind="ExternalInput").ap()
        prog.build()
    x = inp['x']
    in_maps = []
    for c in range(n_cores):
        m = {"xT": np.ascontiguousarray(x[c, :S].T), "pos": np.ascontiguousarray(inp['positions'][c:c + 1, :S]).astype(np.int32),
             "wpk": wpk, "vec": vec, "cst": CST_ARR}
        if any(mt == 2 for mt, _ in layers):
            m["bsd"] = np.ascontiguousarray(inp['sg_b_s'][0].reshape(1, 2048))
        in_maps.append(m)
    import os
    if os.environ.get("KTRACE"):
        res = run_bass_kernel_spmd(nc, in_maps, core_ids=list(range(n_cores)), trace=True)
        print("EXEC_TIME_NS", res.exec_time_ns, "instr counts", {e: prog.k.cnt[e] for e in ENG}, "dmas", dict(prog.k.dma_i))
    else:
        res = run_bass_kernel_spmd(nc, in_maps, core_ids=list(range(n_cores)))
    out = np.stack([np.ascontiguousarray(r["outT"].T) for r in res.results], 0)
    return out


def kernel(**inputs):
    layers = [(0, 0), (1, 1), (2, 2), (3, 3)]
    out = run_model(inputs, 4096, layers, 8)
    return out.astype(np.float32)
```

```python
import numpy as np
from collections import defaultdict
from contextlib import ExitStack
import concourse.bass as bass
import concourse.mybir as mybir
from concourse.bass_utils import run_bass_kernel_spmd

F32 = mybir.dt.float32
BF16 = mybir.dt.bfloat16
I32 = mybir.dt.int32
AF = mybir.ActivationFunctionType
ALU = mybir.AluOpType

D = 1024
KC = 8
T = 512
TR = 256
FH = 2816
NJ = 22
SLOT = 4096
NSLOT = 4
EPS = 1e-5
ARENA = 29 * 1024
ENG = ('pe', 'act', 'dve', 'pool', 'sp')


class Buf:
    __slots__ = ('name', 'w', 'r', 'excl')

    def __init__(self, name):
        self.name = name
        self.w = None
        self.r = {}
        self.excl = False


class Tile:
    def __init__(self, t, name):
        self.t = t
        self.b = Buf(name)


class Packer:
    def __init__(self):
        self.pieces = {}
        self.chunks = []
        self.off = 0
        self.vec_cols = {}
        self.vecs = []
        self.nv = 0

    def piece(self, name, arr):
        arr = np.ascontiguousarray(arr, dtype=np.float32).reshape(arr.shape[0], -1)
        p, n = arr.shape
        assert p <= 128 and n <= SLOT, (name, arr.shape)
        if p < 128:
            arr = np.concatenate([arr, np.zeros((128 - p, n), np.float32)], 0)
        self.pieces[name] = (self.off, n)
        self.chunks.append(arr.reshape(-1))
        self.off += 128 * n

    def vec(self, name, v):
        v = np.asarray(v, np.float32).reshape(-1, 128).T
        self.vec_cols[name] = self.nv
        self.vecs.append(v)
        self.nv += v.shape[1]

    def vec_raw(self, name, m):
        m = np.asarray(m, np.float32)
        assert m.shape[0] == 128
        self.vec_cols[name] = self.nv
        self.vecs.append(m)
        self.nv += m.shape[1]

    def finish(self):
        wpk = np.concatenate(self.chunks) if self.chunks else np.zeros((128,), np.float32)
        vec = np.concatenate(self.vecs, 1)
        return wpk, np.ascontiguousarray(vec)


def lhsT_blocks(W):
    K, N = W.shape
    return W.reshape(K // 128, 128, N // 128, 128).transpose(2, 1, 0, 3)


def pack_layer(pk, inp, li, mt, j):
    pre = f"L{li}_"
    pk.vec(pre + "nmix", inp['norm_mix'][li])
    pk.vec(pre + "nffn", inp['norm_ffn'][li])
    return pre


def pack_ffn(pk, inp, li, order):
    pre = f"L{li}_"
    w_in = inp['ffn_w_in'][li]
    w_out = inp['ffn_w_out'][li]
    A = w_in.reshape(8, 128, 2, NJ, 128).transpose(3, 1, 2, 0, 4)
    for j in range(NJ):
        pk.piece(pre + f"fin{j}", A[j].reshape(128, -1))
        order.append(pre + f"fin{j}")
    Bm = w_out.reshape(NJ, 128, 8, 128).transpose(2, 1, 0, 3)
    for fc in range(8):
        pk.piece(pre + f"fout{fc}", Bm[fc].reshape(128, -1))
        order.append(pre + f"fout{fc}")


def pack_sgu(pk, inp, li, order):
    pre = f"L{li}_"
    blk = lhsT_blocks(inp['sg_w_in'][0])
    for g in range(8):
        pk.piece(pre + f"sgin{g}", blk[4 * g:4 * g + 4].transpose(1, 0, 2, 3).reshape(128, -1))
    ws = inp['sg_w_s'][0]
    pk.piece(pre + "sgws", ws.transpose(2, 0, 1).reshape(128, -1))
    blk = lhsT_blocks(inp['sg_w_o'][0])
    for g in range(4):
        pk.piece(pre + f"sgwo{g}", blk[2 * g:2 * g + 2].transpose(1, 0, 2, 3).reshape(128, -1))
    pk.vec(pre + "sg_b_in", inp['sg_b_in'][0])
    pk.vec(pre + "sg_ln_g", inp['sg_ln_g'][0])
    pk.vec(pre + "sg_ln_b", inp['sg_ln_b'][0])
    pk.vec(pre + "sg_b_o", inp['sg_b_o'][0])


def pack_rwkv(pk, inp, li, order):
    pre = f"L{li}_"
    kc = lambda W: W.reshape(8, 128, -1).transpose(1, 0, 2).reshape(128, -1)
    pk.piece(pre + "rwl1", np.concatenate([kc(inp['rw_w1'][0]), kc(inp['rw_a1'][0]), kc(inp['rw_g1'][0])], 1))
    W = inp['rw_w_rkv'][0]
    blk = np.stack([lhsT_blocks(W[j]) for j in range(3)], 0)
    for hp in range(8):
        l2 = np.zeros((128, 384), np.float32)
        l2[0:64, 0:128] = inp['rw_w2'][0][:, hp * 128:(hp + 1) * 128]
        l2[0:64, 128:256] = inp['rw_a2'][0][:, hp * 128:(hp + 1) * 128]
        l2[:, 256:384] = inp['rw_g2'][0][:, hp * 128:(hp + 1) * 128]
        pk.piece(pre + f"rwrkv{hp}", np.concatenate([blk[:, hp].transpose(1, 0, 2, 3).reshape(128, -1), l2], 1))
    blk = lhsT_blocks(inp['rw_w_o'][0])
    for g in range(2):
        pk.piece(pre + f"rwwo{g}", blk[4 * g:4 * g + 4].transpose(1, 0, 2, 3).reshape(128, -1))
    pk.vec(pre + "rw_mu", inp['rw_mu'][0].reshape(-1))
    for nm in ("w0", "a0", "k_k", "k_a", "gn_g", "gn_b"):
        pk.vec(pre + "rw_" + nm, inp['rw_' + nm][0])
    pk.vec(pre + "rw_r_k", inp['rw_r_k'][0].reshape(-1))


def sw_perm():
    idx = []
    for c in range(8):
        idx += list(range(c * 64, c * 64 + 64)) + list(range((c + 8) * 64, (c + 8) * 64 + 64))
    return np.array(idx)


def pack_swa(pk, inp, li, order):
    pre = f"L{li}_"
    perm = sw_perm()
    w = inp['sw_w_qkv'][0]
    b = inp['sw_b_qkv'][0]
    cols = np.concatenate([perm, np.arange(1024, 1280)])
    wp = w[:, cols]
    blk = lhsT_blocks(wp)
    pk.piece(pre + "swin0", blk[0:4].transpose(1, 0, 2, 3).reshape(128, -1))
    pk.piece(pre + "swin1", blk[4:8].transpose(1, 0, 2, 3).reshape(128, -1))
    pk.piece(pre + "swin2", blk[8:10].transpose(1, 0, 2, 3).reshape(128, -1))
    wo = inp['sw_w_o'][0][perm, :]
    blk = lhsT_blocks(wo)
    for g in range(2):
        pk.piece(pre + f"swwo{g}", blk[4 * g:4 * g + 4].transpose(1, 0, 2, 3).reshape(128, -1))
    pk.vec(pre + "sw_b_qkv", b[cols])
    pk.vec(pre + "sw_b_o", inp['sw_b_o'][0])
    sk = inp['sw_sinks'][0]
    m = np.zeros((128, 8), np.float32)
    for c in range(8):
        m[0:64, c] = sk[c]
        m[64:128, c] = sk[c + 8]
    pk.vec_raw(pre + "sw_sink", m)


def pack_gla(pk, inp, li, order):
    pre = f"L{li}_"
    w_in = inp['gla_w_in'][0]
    blk = lhsT_blocks(w_in[:, :3072])
    for g in range(6):
        pk.piece(pre + f"glin{g}", blk[4 * g:4 * g + 4].transpose(1, 0, 2, 3).reshape(128, -1))
    pk.piece(pre + "glal", w_in[:, 3072:3088].reshape(8, 128, 16).transpose(1, 0, 2).reshape(128, -1))
    pk.piece(pre + "glwa2", inp['gla_w_a2'][0])
    blk = lhsT_blocks(inp['gla_w_o'][0])
    for g in range(2):
        pk.piece(pre + f"glwo{g}", blk[4 * g:4 * g + 4].transpose(1, 0, 2, 3).reshape(128, -1))
    pk.vec(pre + "gl_b_a", inp['gla_b_a'][0])
    pk.vec(pre + "gl_gn_g", inp['gla_gn_g'][0])


class KB:
    NDS = 8

    def __init__(self, nc, es):
        self.nc = nc
        self.es = es
        self.prog = {e: [] for e in ENG}
        self.cnt = {e: 0 for e in ENG}
        self.waited = {e: {} for e in ENG}
        self.sem = {e: es.enter_context(nc.semaphore(f"s_{e}")) for e in ENG}
        for q in ('sp', 'pool'):
            for i in range(self.NDS):
                self.sem[f"d{q}{i}"] = es.enter_context(nc.semaphore(f"d{q}{i}"))
        self.dma_i = {'sp': 0, 'pool': 0}
        self.dma_tot = defaultdict(int)
        self.same_sync = {'pe': False, 'act': True, 'dve': True, 'pool': True, 'sp': False}
        self.ntile = 0

    def sb(self, name, shape, dt):
        t = self.es.enter_context(self.nc.sbuf_tensor(name, list(shape), dt))
        return Tile(t, name)

    def handoff(self, old, new):
        toks = {}
        for b in old:
            if b.w:
                toks[b.w[0]] = max(toks.get(b.w[0], 0), b.w[1])
            for s_, v in b.r.items():
                toks[s_] = max(toks.get(s_, 0), v)
            b.w = None
            b.r = {}
        for b in new:
            for s_, v in toks.items():
                b.r[s_] = max(b.r.get(s_, 0), v)

    def _deps(self, reads, writes):
        deps = []
        for b in reads:
            if b.w:
                deps.append((b.w[0], b.w[1], True))
            if b.excl:
                deps.extend((s_, v_, False) for s_, v_ in b.r.items())
        for b in writes:
            if b.w:
                deps.append((b.w[0], b.w[1], False))
            deps.extend((s_, v_, False) for s_, v_ in b.r.items())
        return deps

    def _waits(self, eng, deps):
        out = {}
        for s, v, raw in deps:
            if s == eng and (not self.same_sync[eng] or not raw):
                continue
            if self.waited[eng].get(s, 0) >= v:
                continue
            out[s] = max(out.get(s, 0), v)
        for s, v in out.items():
            self.waited[eng][s] = v
        return list(out.items())

    def _mark(self, tok, reads, writes):
        for b in reads:
            b.r[tok[0]] = max(b.r.get(tok[0], 0), tok[1])
        for b in writes:
            b.w = tok
            b.r = {}

    def op(self, eng, fn, reads=(), writes=()):
        waits = self._waits(eng, self._deps(reads, writes))
        self.cnt[eng] += 1
        tok = (eng, self.cnt[eng])
        sem = self.sem

        def run(e):
            for s, v in waits:
                e.wait_ge(sem[s], v)
            fn(e).then_inc(sem[eng], 1)
        self.prog[eng].append(run)
        self._mark(tok, reads, writes)

    def dma(self, q, out, in_, reads=(), writes=()):
        deps = self._deps(reads, writes)
        i = self.dma_i[q]
        self.dma_i[q] += 1
        sname = f"d{q}{i % self.NDS}"
        prev = self.dma_tot[sname]
        if prev:
            deps.append((sname, prev, True))
        waits = self._waits(q, deps)
        self.dma_tot[sname] = prev + 16
        tok = (sname, prev + 16)
        sem = self.sem

        def run(e):
            for s, v in waits:
                e.wait_ge(sem[s], v)
            e.dma_start(out=out, in_=in_).then_inc(sem[sname], 16)
        self.prog[q].append(run)
        self._mark(tok, reads, writes)

    def finish(self):
        sem = self.sem
        finals = [(s, v) for s, v in self.dma_tot.items() if v]

        def run(e):
            for s, v in finals:
                e.wait_ge(sem[s], v)
        self.prog['sp'].append(run)
        prog = self.prog
        with self.nc.Block() as block:
            @block.sync
            def _(e):
                for f in prog['sp']:
                    f(e)

            @block.tensor
            def _(e):
                for f in prog['pe']:
                    f(e)

            @block.scalar
            def _(e):
                for f in prog['act']:
                    f(e)

            @block.vector
            def _(e):
                for f in prog['dve']:
                    f(e)

            @block.gpsimd
            def _(e):
                for f in prog['pool']:
                    f(e)

    def mm(self, out, lhsT, rhs, start, stop, reads, writes, skip=False):
        if skip:
            self.op('pe', lambda e: e.matmul(out, lhsT=lhsT, rhs=rhs, start=start, stop=stop, skip_group_check=True), reads, writes)
        else:
            self.op('pe', lambda e: e.matmul(out, lhsT=lhsT, rhs=rhs, start=start, stop=stop), reads, writes)

    def tr(self, out, in_, ident, reads, writes):
        self.op('pe', lambda e: e.transpose(out, in_, ident), reads, writes)

    def act(self, out, in_, func, reads, writes, bias=None, scale=None):
        kw = {}
        if bias is not None:
            kw['bias'] = bias
        if scale is not None:
            kw['scale'] = scale
        self.op('act', lambda e: e.activation(out=out, in_=in_, func=func, **kw), reads, writes)

    def tt(self, out, in0, in1, op, reads, writes, eng='dve'):
        self.op(eng, lambda e: e.tensor_tensor(out=out, in0=in0, in1=in1, op=op), reads, writes)

    def ts(self, out, in0, s1, op0, reads, writes, s2=None, op1=None, eng='dve'):
        if op1 is None:
            self.op(eng, lambda e: e.tensor_scalar(out=out, in0=in0, scalar1=s1, scalar2=None, op0=op0), reads, writes)
        else:
            self.op(eng, lambda e: e.tensor_scalar(out=out, in0=in0, scalar1=s1, scalar2=s2, op0=op0, op1=op1), reads, writes)

    def stt(self, out, in0, scalar, in1, op0, op1, reads, writes):
        self.op('dve', lambda e: e.scalar_tensor_tensor(out=out, in0=in0, scalar=scalar, in1=in1, op0=op0, op1=op1), reads, writes)

    def cp(self, out, in_, reads, writes, eng='dve'):
        if eng == 'act':
            self.op(eng, lambda e: e.activation(out=out, in_=in_, func=AF.Copy), reads, writes)
        else:
            self.op(eng, lambda e: e.tensor_copy(out=out, in_=in_), reads, writes)


class Prog:
    def __init__(self, nc, es, S, layers, pk, do_ffn=True):
        self.nc = nc
        self.S = S
        self.NT = S // T
        self.layers = layers
        self.pk = pk
        self.do_ffn = do_ffn
        k = self.k = KB(nc, es)
        nv = pk.nv
        self.xT = nc.dram_tensor("xT", [D, S], F32, kind="ExternalInput").ap()
        self.pos = nc.dram_tensor("pos", [1, S], I32, kind="ExternalInput").ap()
        self.wpk = nc.dram_tensor("wpk", [max(pk.off, 128)], F32, kind="ExternalInput").ap()
        self.vecd = nc.dram_tensor("vec", [128, nv], F32, kind="ExternalInput").ap()
        self.cstd = nc.dram_tensor("cst", [128, CST_N], F32, kind="ExternalInput").ap()
        self.outT = nc.dram_tensor("outT", [D, S], F32, kind="ExternalOutput").ap()
        self.vec = k.sb("vec_sb", [128, nv], F32)
        self.cf = k.sb("cst_f", [128, CST_N], F32)
        self.cb = k.sb("cst_b", [128, CSTB_N], BF16)
        self.hT = k.sb("hT", [128, KC * T], F32)
        self.xn = k.sb("xn", [128, KC * T], BF16)
        self.sq = k.sb("sq", [128, KC * T], BF16)
        self.nrm1 = k.sb("nrm1", [128, T], F32)
        self.nrm2 = k.sb("nrm2", [128, T], F32)
        self.ring = [k.sb(f"ring{i}", [128, SLOT], BF16) for i in range(NSLOT)]
        self.ring_i = 0
        self.arena = es.enter_context(nc.sbuf_tensor("arena", [128, ARENA], F32))
        self.owners = defaultdict(list)
        self.cur_owner = None
        self.aoff = {}
        self.hid = self.carve('ffn', "hid", NJ * T, BF16)
        self.sg = [self.carve('ffn', f"sg{i}", T, F32) for i in range(2)]
        self.fin = self.carve('ffn', "fin", KC * T, F32)
        self.ps = []
        for i in range(8):
            t = es.enter_context(nc.psum_tensor(f"ps{i}", [128, 512], F32))
            self.ps.append(Tile(t, f"ps{i}"))
            self.ps[-1].b.excl = True
        self.hTb = [Buf(f"hTb{c}") for c in range(KC)]
        self.xnb = [Buf(f"xnb{c}") for c in range(KC)]
        self.sqb8 = [Buf(f"sqb{c}") for c in range(KC)]
        self.hTv = self.hT.t[:].rearrange("p (c t) -> p c t", c=KC)
        self.xnv = self.xn.t[:].rearrange("p (c t) -> p c t", c=KC)
        self.sqv = self.sq.t[:].rearrange("p (c t) -> p c t", c=KC)
        self.hidv = self.hid.t[:].rearrange("p (c t) -> p c t", c=NJ)
        self.alloc_mixers()

    def carve(self, owner, name, n, dt, parent=None):
        words = n if dt != BF16 else (n + 1) // 2
        if parent is not None and owner not in self.aoff:
            self.aoff[owner] = self.aoff[parent + "_end"]
        off = self.aoff.get(owner, 0)
        assert off + words <= ARENA, (owner, name, off, words)
        self.aoff[owner] = off + words
        v = self.arena[:, off:off + words]
        if dt != F32:
            v = v.bitcast(dt)
        tl = Tile(v, name)
        self.owners[owner].append(tl.b)
        if parent is not None:
            self.owners[parent].append(tl.b)
        return tl

    def switch(self, owner):
        if self.cur_owner is not None and self.cur_owner != owner:
            self.k.handoff(self.owners[self.cur_owner], self.owners[owner])
        self.cur_owner = owner

    def vcol(self, name, c=0, p0=0, p1=128):
        i = self.pk.vec_cols[name] + c
        return self.vec.t[p0:p1, i:i + 1]

    def cF(self, name, p0=0, p1=128):
        o, n = CST[name]
        return self.cf.t[p0:p1, o:o + n]

    def cB(self, name, p0=0, p1=128):
        o, n = CST[name]
        return self.cb.t[p0:p1, o:o + n]

    def load_piece(self, name):
        off, n = self.pk.pieces[name]
        slot = self.ring[self.ring_i % NSLOT]
        self.ring_i += 1
        src = self.wpk[off:off + 128 * n].rearrange("(p n) -> p n", p=128)
        self.k.dma('pool', slot.t[:, 0:n], src, reads=[], writes=[slot.b])
        return slot

    def alloc_mixers(self):
        k = self.k
        types = set(mt for mt, _ in self.layers)
        if 2 in types:
            self.sg_u = self.carve('m2', "sg_u", 16 * T, BF16)
            self.sg_v = self.carve('m2', "sg_v", 16 * T, F32)
            self.sg_vn = [self.carve('m2', f"sg_vn{i}", T, BF16) for i in range(2)]
            self.sg_vt = [self.carve('m2', f"sg_vt{i}", T, BF16) for i in range(2)]
            self.sg_sqa = [self.carve('m2', f"sg_sqa{i}", T, BF16) for i in range(2)]
            self.sg_z = self.carve('m2', "sg_z", 16 * T, BF16)
            self.sg_ws = self.carve('m2', "sg_ws", 16 * 128, BF16)
            self.sg_st = [self.carve('m2', f"sg_st{i}", T, F32) for i in range(3)]
            self.sg_t1 = [self.carve('m2', f"sg_t1{i}", T, F32) for i in range(2)]
            self.sg_bs = self.carve('m2', "sg_bs", 16 * 128, BF16)
            self.sg_zb = [Buf(f"sg_zb{i}") for i in range(16)]
            self.sg_vb = [Buf(f"sg_vb{i}") for i in range(16)]
            self.owners['m2'] += self.sg_zb + self.sg_vb
        if 0 in types:
            self.rw_Sf = [k.sb(f"rw_Sf{h}", [64, 128], F32) for h in range(8)]
            self.rw_Sb = [k.sb(f"rw_Sb{h}", [64, 128], BF16) for h in range(8)]
            self.rw_prev = k.sb("rw_prev", [128, 8], F32)
            self.rw_Sfb = [[Buf(f"rw_Sfb{h}_{e}") for e in range(2)] for h in range(8)]
            self.rw_nb = k.sb("rw_nb", [128, 16], F32)
            c_ = lambda n, sz, dt: self.carve('m0', n, sz, dt)
            R = self.R = {}
            R["xnF"] = c_("rw_xnF", KC * TR, F32)
            R["xx"] = c_("rw_xx", KC * TR, F32)
            for nm in ("xr", "xk", "xv", "zT"):
                R[nm] = c_("rw_" + nm, KC * TR, BF16)
            for nm in ("w1T", "a1T", "g1T", "BT", "KT", "BH", "KH", "VB", "BTh", "KTh"):
                R[nm] = c_("rw_" + nm, TR, BF16)
            R["bg"] = [{"bonus": c_(f"rw_bonus{i}", TR, F32), "gf": c_(f"rw_gf{i}", TR, F32)} for i in range(4)]
            R["sets"] = []
            for s_ in range(3):
                st = {}
                st["AR"] = c_(f"rw_AR{s_}", 2 * TR, BF16)
                st["ARh"] = c_(f"rw_ARh{s_}", 2 * TR, BF16)
                st["tok"] = c_(f"rw_tok{s_}", 4 * 512, BF16)
                st["AT"] = c_(f"rw_AT{s_}", 8 * 64, BF16)
                st["Uv"] = c_(f"rw_Uv{s_}", 8 * 64, BF16)
                st["SC"] = c_(f"rw_SC{s_}", 8 * 320, BF16)
                st["Tall"] = c_(f"rw_Tall{s_}", 8 * 64, BF16)
                st["WL"] = c_(f"rw_WL{s_}", 4, F32)
                st["WLh"] = c_(f"rw_WLh{s_}", 4, F32)
                R["sets"].append(st)
            R["PP"] = [c_(f"rw_PP{i}", 8 * 128, BF16) for i in range(2)]
            R["TT"] = [c_(f"rw_TT{i}", 8 * 64, BF16) for i in range(2)]
            R["Ub"] = c_("rw_Ub", 128, BF16)
            R["Ub2"] = [c_(f"rw_Ub2{i}", TR, BF16) for i in range(2)]
            for nm in ("rf", "kf", "vf", "sgw", "cs", "a", "kk", "t1", "t2", "W", "Winv", "Wex", "WLr", "ka", "kp"):
                R[nm] = c_("rw_" + nm, TR, F32)
            R["y"] = [c_(f"rw_y{i}", TR, F32) for i in range(2)]
            for nm in ("yc", "ysq"):
                R[nm] = c_("rw_" + nm, TR, F32)
        if 1 in types:
            self.sw_kT = k.sb("sw_kT", [128, 640], BF16)
            self.sw_vt = k.sb("sw_vt", [128, 640], BF16)
            self.sw_es = k.sb("sw_es", [128, 8], F32)
            c_ = lambda n, sz, dt: self.carve('m1', n, sz, dt)
            self.sw_pi = c_("sw_pi", T, I32)
            self.sw_y = c_("sw_y", T, F32)
            self.sw_f = c_("sw_f", T, F32)
            self.sw_g = c_("sw_g", T, F32)
            self.sw_ni = c_("sw_ni", T, I32)
            self.sw_cos = c_("sw_cos", T, F32)
            self.sw_sin = c_("sw_sin", T, F32)
            self.sw_qf = [c_(f"sw_qf{i}", T, F32) for i in range(2)]
            self.sw_t1 = [c_(f"sw_t1{i}", T, F32) for i in range(2)]
            self.sw_t2 = [c_(f"sw_t2{i}", T, F32) for i in range(2)]
            self.sw_qb = [c_(f"sw_qb{i}", T, BF16) for i in range(8)]
            self.sw_vb = c_("sw_vb", T, BF16)
            self.sw_E = [[c_(f"sw_E{i}_{kb}", 512, BF16) for kb in range(5)] for i in range(2)]
            self.sw_ds = [c_(f"sw_ds{i}", T, F32) for i in range(2)]
            self.sw_o = c_("sw_o", 8 * T, BF16)
        if 3 in types:
            self.gl_Sf = [k.sb(f"gl_Sf{h}", [128, 256], F32) for h in range(4)]
            self.gl_Sb = [k.sb(f"gl_Sb{h}", [128, 256], BF16) for h in range(4)]
            self.gl_nb = k.sb("gl_nb", [128, 4], F32)
            c_ = lambda n, sz, dt: self.carve('m3', n, sz, dt)
            self.gl_al = c_("gl_al", T, BF16)
            self.gl_wa2 = c_("gl_wa2", 512, BF16)
            self.gl_dc = [c_(f"gl_dc{h}", 8, F32) for h in range(4)]
            self.gl_qg = [c_(f"gl_qg{h}", T, BF16) for h in range(4)]
            self.gl_kg = [c_(f"gl_kg{h}", T, BF16) for h in range(4)]
            self.gl_ks = [c_(f"gl_ks{h}", T, BF16) for h in range(4)]
            self.gl_v = [c_(f"gl_v{e}", T, BF16) for e in range(8)]
            self.gl_gate = [c_(f"gl_gate{e}", T, F32) for e in range(8)]
            self.gl_z = c_("gl_z", 8 * T, BF16)
            self.aoff["m3_end"] = self.aoff["m3"]
            ca = lambda n, sz, dt: self.carve('m3a', n, sz, dt, parent='m3')
            cb_ = lambda n, sz, dt: self.carve('m3b', n, sz, dt, parent='m3')
            self.gl_la = [ca(f"gl_la{i}", T, F32) for i in range(4)]
            self.gl_eb = [ca(f"gl_eb{h}", T, F32) for h in range(4)]
            self.gl_enb = [ca(f"gl_enb{h}", T, F32) for h in range(4)]
            self.gl_ksf = [ca(f"gl_ksf{h}", T, F32) for h in range(4)]
            self.gl_tok = [cb_(f"gl_tok{i}", 8 * 384, BF16) for i in range(4)]
            self.gl_att = [cb_(f"gl_att{i}", 8 * 64, BF16) for i in range(4)]
            self.gl_ofa = cb_("gl_ofa", 8 * T, F32)
            self.gl_osqa = cb_("gl_osqa", 8 * T, BF16)
            self.gl_rs = [[cb_(f"gl_rs{q}{i}", T, F32) for i in range(2)] for q in range(2)]
            self.gl_pssb = [Buf("gl_pss0"), Buf("gl_pss1")]
            for b_ in self.gl_pssb:
                b_.excl = True

    def prologue(self):
        k = self.k
        k.dma('sp', self.vec.t[:], self.vecd, writes=[self.vec.b])
        k.dma('sp', self.cf.t[:], self.cstd, writes=[self.cf.b])
        k.dma('pool', self.cb.t[:], self.cstd[:, 0:CSTB_N], writes=[self.cb.b])

    def rmsnorm(self, gname, out_f32=None, stats_only=False):
        k = self.k
        ps = self.ps[7]
        ones = self.cB("ones")
        for c in range(KC):
            k.act(self.sqv[:, c, :], self.hTv[:, c, :], AF.Square, [self.hTb[c]], [self.sqb8[c]])
            k.mm(ps.t[:], ones, self.sqv[:, c, :], c == 0, c == KC - 1, [self.sqb8[c], self.cb.b], [ps.b])
        k.act(self.nrm1.t[:], ps.t[:], AF.Ln, [ps.b, self.cf.b], [self.nrm1.b], bias=self.cF("eps")[:, 0:1], scale=1.0 / D)
        k.act(self.nrm2.t[:], self.nrm1.t[:], AF.Exp, [self.nrm1.b], [self.nrm2.b], scale=-0.5)
        if stats_only:
            return
        for c in range(KC):
            if out_f32 is None:
                k.stt(self.xnv[:, c, :], self.hTv[:, c, :], self.vcol(gname, c), self.nrm2.t[:], ALU.mult, ALU.mult,
                      [self.hTb[c], self.vec.b, self.nrm2.b], [self.xnb[c]])
            else:
                ov = out_f32.t[:].rearrange("p (c t) -> p c t", c=KC)
                k.stt(ov[:, c, :], self.hTv[:, c, :], self.vcol(gname, c), self.nrm2.t[:], ALU.mult, ALU.mult,
                      [self.hTb[c], self.vec.b, self.nrm2.b], [out_f32.b])

    def preload_exp_table(self):
        k = self.k
        k.act(self.nrm1.t[:, 0:1], self.cF("one")[:, 0:1], AF.Ln, [self.cf.b], [self.nrm1.b])

    def ffn(self, li):
        k = self.k
        pre = f"L{li}_"
        xn, hid = self.xn, self.hid
        for j in range(NJ):
            slot = self.load_piece(pre + f"fin{j}")
            wv = slot.t[:, 0:2 * KC * 128].rearrange("p (g c m) -> p g c m", g=2, c=KC)
            pg = self.ps[(2 * j) % 4]
            pu = self.ps[(2 * j + 1) % 4]
            for c in range(KC):
                k.mm(pg.t[:], wv[:, 0, c, :], self.xnv[:, c, :], c == 0, c == KC - 1, [slot.b, self.xnb[c]], [pg.b])
            for c in range(KC):
                k.mm(pu.t[:], wv[:, 1, c, :], self.xnv[:, c, :], c == 0, c == KC - 1, [slot.b, self.xnb[c]], [pu.b])
            sg = self.sg[j % 2]
            k.act(sg.t[:], pg.t[:], AF.Silu, [pg.b], [sg.b])
            k.tt(self.hidv[:, j, :], sg.t[:], pu.t[:], ALU.mult, [sg.b, pu.b], [hid.b])
        self.preload_exp_table()
        for fc in range(8):
            slot = self.load_piece(pre + f"fout{fc}")
            wv = slot.t[:, 0:NJ * 128].rearrange("p (j m) -> p j m", j=NJ)
            po = self.ps[4 + fc % 2]
            for j in range(NJ):
                k.mm(po.t[:], wv[:, j, :], self.hidv[:, j, :], j == 0, j == NJ - 1, [slot.b, hid.b], [po.b])
            k.tt(self.hTv[:, fc, :], self.hTv[:, fc, :], po.t[:], ALU.add, [self.hTb[fc], po.b], [self.hTb[fc]])

    def sgu(self, li, ti):
        k = self.k
        pre = f"L{li}_"
        xn = self.xn
        uv = self.sg_u.t[:].rearrange("p (c t) -> p c t", c=16)
        vv = self.sg_v.t[:].rearrange("p (c t) -> p c t", c=16)
        zv = self.sg_z.t[:].rearrange("p (c t) -> p c t", c=16)
        ones = self.cB("ones")
        pm, pq = self.ps[4], self.ps[5]
        zb = self.sg_zb
        k.handoff([self.sg_z.b], zb)

        def stats_mm(c):
            k.mm(pm.t[:], ones, zv[:, c, :], c == 0, c == 15, [zb[c], self.cb.b], [pm.b])
            sq = self.sg_sqa[c % 2]
            k.mm(pq.t[:], ones, sq.t[:], c == 0, c == 15, [sq.b, self.cb.b], [pq.b])
        for g in range(8):
            slot = self.load_piece(pre + f"sgin{g}")
            wv = slot.t[:, 0:4 * KC * 128].rearrange("p (o c m) -> p o c m", o=4, c=KC)
            for o in range(4):
                oc = 4 * g + o
                ps = self.ps[oc % 4]
                for c in range(KC):
                    k.mm(ps.t[:], wv[:, o, c, :], self.xnv[:, c, :], c == 0, c == KC - 1, [slot.b, self.xnb[c]], [ps.b])
                if oc > 16:
                    stats_mm(oc - 17)
                if oc < 16:
                    k.act(uv[:, oc, :], ps.t[:], AF.Gelu, [ps.b, self.vec.b], [self.sg_u.b], bias=self.vcol(pre + "sg_b_in", oc))
                else:
                    c = oc - 16
                    k.act(vv[:, c, :], ps.t[:], AF.Gelu, [ps.b, self.vec.b], [self.sg_vb[c]], bias=self.vcol(pre + "sg_b_in", oc))
                    k.cp(zv[:, c, :], vv[:, c, :], [self.sg_vb[c]], [zb[c]])
                    sq = self.sg_sqa[c % 2]
                    k.act(sq.t[:], vv[:, c, :], AF.Square, [self.sg_vb[c]], [sq.b])
        stats_mm(15)
        mean, msq, rstd = self.sg_st
        k.act(mean.t[:], pm.t[:], AF.Copy, [pm.b], [mean.b], scale=1.0 / 2048)
        k.act(msq.t[:], mean.t[:], AF.Square, [mean.b], [msq.b])
        k.stt(msq.t[:], pq.t[:], 1.0 / 2048, msq.t[:], ALU.mult, ALU.subtract, [pq.b, msq.b], [msq.b])
        k.act(msq.t[:], msq.t[:], AF.Ln, [msq.b, self.cf.b], [msq.b], bias=self.cF("eps")[:, 0:1], scale=1.0)
        k.act(rstd.t[:], msq.t[:], AF.Exp, [msq.b], [rstd.b], scale=-0.5)
        k.handoff(zb, [self.sg_z.b])
        slot = self.load_piece(pre + "sgws")
        wsv = self.sg_ws.t[:].rearrange("p (g t) -> p g t", g=16)
        k.tt(wsv, slot.t[:, 0:2048].rearrange("p (g t) -> p g t", g=16),
             self.cB("mask_le").unsqueeze(1).broadcast_to([128, 16, 128]), ALU.mult, [slot.b, self.cb.b], [self.sg_ws.b])
        ident = self.cB("ident")
        onesrow = self.cB("ones", 0, 1)
        k.dma('pool', self.sg_bs.t[0:1, :], self.bsd, writes=[self.sg_bs.b])
        bsv = self.sg_bs.t[:].rearrange("p (g t) -> p g t", g=16)
        ptb = [self.ps[6], self.ps[3]]
        pob = [self.ps[0], self.ps[1]]

        def s1(g):
            t1, vn = self.sg_t1[g % 2], self.sg_vn[g % 2]
            k.tt(t1.t[:], vv[:, g, :], mean.t[:], ALU.subtract, [self.sg_vb[g], mean.b], [t1.b])
            k.tt(t1.t[:], t1.t[:], rstd.t[:], ALU.mult, [t1.b, rstd.b], [t1.b])
            k.ts(vn.t[:], t1.t[:], self.vcol(pre + "sg_ln_g", g), ALU.mult, [t1.b, self.vec.b], [vn.b],
                 s2=self.vcol(pre + "sg_ln_b", g), op1=ALU.add)

        def s2(g):
            pt, vn = ptb[g % 2], self.sg_vn[g % 2]
            ptv = pt.t[:].bitcast(BF16)[:, 0:T].rearrange("p (n c) -> p n c", n=4)
            for n in range(4):
                k.tr(ptv[:, n, :], vn.t[:, n * 128:(n + 1) * 128], ident, [vn.b, self.cb.b], [pt.b])

        def s3(g):
            pt, vt = ptb[g % 2], self.sg_vt[g % 2]
            k.cp(vt.t[:], pt.t[:].bitcast(BF16)[:, 0:T], [pt.b], [vt.b], eng='act')

        def s4(g):
            vt, po = self.sg_vt[g % 2], pob[g % 2]
            vtv = vt.t[:].rearrange("p (n c) -> p n c", n=4)
            for n in range(4):
                k.mm(po.t[:, n * 128:(n + 1) * 128], vtv[:, n, :], wsv[:, g, :], True, False, [vt.b, self.sg_ws.b], [po.b])
                k.mm(po.t[:, n * 128:(n + 1) * 128], onesrow, bsv[0:1, g, :], False, True, [self.cb.b, self.sg_bs.b], [po.b])

        def s5(g):
            po = pob[g % 2]
            k.tt(zv[:, g, :], po.t[:], uv[:, g, :], ALU.mult, [po.b, self.sg_u.b], [self.sg_z.b])
        stages = (s1, s2, s3, s4, s5)
        for step in range(16 + len(stages) - 1):
            for d in range(len(stages) - 1, -1, -1):
                g = step - d
                if 0 <= g < 16:
                    stages[d](g)
        for g in range(4):
            slot = self.load_piece(pre + f"sgwo{g}")
            wv = slot.t[:, 0:2 * 16 * 128].rearrange("p (o c m) -> p o c m", o=2, c=16)
            for o in range(2):
                fc = 2 * g + o
                ps = self.ps[2 + fc % 2]
                for c in range(16):
                    k.mm(ps.t[:], wv[:, o, c, :], zv[:, c, :], c == 0, c == 15, [slot.b, self.sg_z.b], [ps.b])
                k.stt(self.hTv[:, fc, :], ps.t[:], self.vcol(pre + "sg_b_o", fc), self.hTv[:, fc, :], ALU.add, ALU.add,
                      [ps.b, self.vec.b, self.hTb[fc]], [self.hTb[fc]])

    def rwkv(self, li, ti):
        k = self.k
        pre = f"L{li}_"
        R = self.R
        C0 = float(np.exp(-0.5))
        NCH = TR // 64
        identF = self.cF("ident", 0, 64)[:, 0:64]
        identB = self.cB("ident")
        identB64 = self.cB("ident", 0, 64)[:, 0:64]
        blk = self.cF("blk64")
        blkB = self.cB("blk64")
        vec = self.vec
        V = lambda nm, c=0, p0=0, p1=128: self.vcol(pre + nm, c, p0, p1)
        xnF, xx = R["xnF"], R["xx"]
        xnFv = xnF.t[:].rearrange("p (c t) -> p c t", c=KC)
        xxv = xx.t[:].rearrange("p (c t) -> p c t", c=KC)
        if ti == 0:
            c0_ = self.pk.vec_cols[pre + "rw_w0"]
            c1_ = self.pk.vec_cols[pre + "rw_a0"]
            k.ts(self.rw_nb.t[:, 0:8], self.vec.t[:, c0_:c0_ + 8], -1.0, ALU.mult, [self.vec.b], [self.rw_nb.b])
            k.ts(self.rw_nb.t[:, 8:16], self.vec.t[:, c1_:c1_ + 8], -1.0, ALU.mult, [self.vec.b], [self.rw_nb.b])
            k.op('dve', lambda e: e.memset(self.rw_prev.t[:], 0.0), [], [self.rw_prev.b])
            for h in range(8):
                k.op('dve', lambda e, h=h: e.memset(self.rw_Sf[h].t[:], 0.0), [], [self.rw_Sf[h].b] + self.rw_Sfb[h])
                k.op('dve', lambda e, h=h: e.memset(self.rw_Sb[h].t[:], 0.0), [], [self.rw_Sb[h].b])
        self.rmsnorm(pre + "nmix", stats_only=True)
        k.handoff(self.xnb + self.sqb8, [self.xn.b, self.sq.b])
        zv = R["zT"].t[:].rearrange("p (c t) -> p c t", c=KC)
        m320 = self.cF("m320", 0, 64)
        reset = self.cF("reset64")[:, 0:TR]

        import os
        PE_ = os.environ.get('RWPOOL', 'dve')

        def variant(ci, dst, t0):
            dv = dst.t[:, 0:KC * TR].rearrange("p (c t) -> p c t", c=KC)
            for fc in range(KC):
                k.stt(dv[:, fc, :], xxv[:, fc, :], V("rw_mu", ci * 8 + fc), xnFv[:, fc, :], ALU.mult, ALU.add,
                      [xx.b, vec.b, xnF.b], [dst.b])

        def preamble(sub):
            t0 = sub * TR
            for fc in range(KC):
                k.stt(xnFv[:, fc, :], self.hTv[:, fc, t0:t0 + TR], self.vcol(pre + "nmix", fc), self.nrm2.t[:, t0:t0 + TR], ALU.mult, ALU.mult,
                      [self.hTb[fc], vec.b, self.nrm2.b], [xnF.b])
            yield
            k.tt(xxv[:, :, 1:TR], xnFv[:, :, 0:TR - 1], xnFv[:, :, 1:TR], ALU.subtract, [xnF.b], [xx.b])
            k.tt(xxv[:, :, 0], self.rw_prev.t[:], xnFv[:, :, 0], ALU.subtract, [self.rw_prev.b, xnF.b], [xx.b])
            k.cp(self.rw_prev.t[:], xnFv[:, :, TR - 1], [xnF.b], [self.rw_prev.b])
            yield
            slot1 = self.load_piece(pre + "rwl1")
            w1v = slot1.t[:, 0:512].rearrange("p (c m) -> p c m", c=KC)
            a1v = slot1.t[:, 512:1024].rearrange("p (c m) -> p c m", c=KC)
            g1v = slot1.t[:, 1024:2048].rearrange("p (c m) -> p c m", c=KC)
            for (ci, wv_, M, dst, fn) in ((1, w1v, 64, R["w1T"], AF.Tanh), (4, a1v, 64, R["a1T"], AF.Copy), (5, g1v, 128, R["g1T"], AF.Sigmoid)):
                tmp = self.xn if ci != 4 else self.sq
                tv = tmp.t[:, 0:KC * TR].rearrange("p (c t) -> p c t", c=KC)
                variant(ci, tmp, t0)
                ps = self.ps[0]
                for c in range(KC):
                    k.mm(ps.t[0:M, 0:TR], wv_[:, c, :], tv[:, c, :], c == 0, c == KC - 1, [slot1.b, tmp.b], [ps.b])
                one_ = self.cF("one")[0:M, 0:1]
                if fn == AF.Tanh:
                    tq = R["t1"]
                    k.act(tq.t[0:M, :], ps.t[0:M, 0:TR], AF.Exp, [ps.b], [tq.b], scale=2.0)
                    k.act(tq.t[0:M, :], tq.t[0:M, :], AF.Ln, [tq.b, self.cf.b], [tq.b], bias=one_)
                    k.act(tq.t[0:M, :], tq.t[0:M, :], AF.Exp, [tq.b], [tq.b], scale=-1.0)
                    k.ts(dst.t[0:M, :], tq.t[0:M, :], -2.0, ALU.mult, [tq.b], [dst.b], s2=1.0, op1=ALU.add)
                elif fn == AF.Sigmoid:
                    tq = R["t2"]
                    k.act(tq.t[0:M, :], ps.t[0:M, 0:TR], AF.Exp, [ps.b], [tq.b], scale=-1.0)
                    k.act(tq.t[0:M, :], tq.t[0:M, :], AF.Ln, [tq.b, self.cf.b], [tq.b], bias=one_)
                    k.act(dst.t[0:M, :], tq.t[0:M, :], AF.Exp, [tq.b], [dst.b], scale=-1.0)
                else:
                    k.act(dst.t[0:M, :], ps.t[0:M, 0:TR], fn, [ps.b], [dst.b])
                yield
            variant(0, R["xr"], t0)
            yield
            variant(2, R["xk"], t0)
            yield
            variant(3, R["xv"], t0)
            yield

        def stageA1(sub, hp, st, bg):
            slot = self.load_piece(pre + f"rwrkv{hp}")
            wv = slot.t[:, 0:3 * KC * 128].rearrange("p (j c m) -> p j c m", j=3, c=KC)
            rf, kf, vf, sgw, cs, a, kk = (R[n] for n in ("rf", "kf", "vf", "sgw", "cs", "a", "kk"))
            t1, t2, W, Winv, Wex, WLr, ka, kp = (R[n] for n in ("t1", "t2", "W", "Winv", "Wex", "WLr", "ka", "kp"))
            gf, bonus = bg["gf"], bg["bonus"]
            l2v = slot.t[:, 3072:3456]
            sqb = R["BTh"]
            pbank = [self.ps[0], self.ps[2]]
            nb = [0]

            def bank():
                nb[0] += 1
                return pbank[nb[0] % 2]
            for j, (src, dst) in enumerate(((R["xr"], rf), (R["xk"], kf), (R["xv"], vf))):
                sv = src.t[:].rearrange("p (c t) -> p c t", c=KC)
                ps = bank()
                for c in range(KC):
                    k.mm(ps.t[:, 0:TR], wv[:, j, c, :], sv[:, c, :], c == 0, c == KC - 1, [slot.b, src.b], [ps.b])
                yield
                k.cp(dst.t[:], ps.t[:, 0:TR], [ps.b], [dst.b], eng='act')
                if j == 2:
                    k.cp(R["VB"].t[:], ps.t[:, 0:TR], [ps.b], [R["VB"].b])
            ps = bank()
            k.mm(ps.t[:, 0:TR], l2v[0:64, 0:128], R["w1T"].t[0:64, :], True, True, [slot.b, R["w1T"].b], [ps.b])
            ps2 = bank()
            k.mm(ps2.t[:, 0:TR], l2v[0:64, 128:256], R["a1T"].t[0:64, :], True, True, [slot.b, R["a1T"].b], [ps2.b])
            yield
            if os.environ.get("RWSIG"):
                k.act(sgw.t[:], ps.t[:, 0:TR], AF.Sigmoid, [ps.b, vec.b], [sgw.b], bias=V("rw_w0", hp))
                k.act(a.t[:], ps2.t[:, 0:TR], AF.Sigmoid, [ps2.b, vec.b], [a.b], bias=V("rw_a0", hp))
            else:
                k.act(sgw.t[:], ps.t[:, 0:TR], AF.Exp, [ps.b, self.rw_nb.b], [sgw.b], bias=self.rw_nb.t[:, hp:hp + 1], scale=-1.0)
                k.act(a.t[:], ps2.t[:, 0:TR], AF.Exp, [ps2.b, self.rw_nb.b], [a.b], bias=self.rw_nb.t[:, 8 + hp:9 + hp], scale=-1.0)
                k.act(sgw.t[:], sgw.t[:], AF.Ln, [sgw.b, self.cf.b], [sgw.b], bias=self.cF("one")[:, 0:1])
                k.act(a.t[:], a.t[:], AF.Ln, [a.b, self.cf.b], [a.b], bias=self.cF("one")[:, 0:1])
                k.act(sgw.t[:], sgw.t[:], AF.Exp, [sgw.b], [sgw.b], scale=-1.0)
                k.act(a.t[:], a.t[:], AF.Exp, [a.b], [a.b], scale=-1.0)
            ps = bank()
            k.mm(ps.t[:, 0:TR], l2v[:, 256:384], R["g1T"].t[:], True, True, [slot.b, R["g1T"].b], [ps.b])
            k.ts(kk.t[:], kf.t[:], V("rw_k_k", hp), ALU.mult, [kf.b, vec.b], [kk.b])
            yield
            k.cp(gf.t[:], ps.t[:, 0:TR], [ps.b], [gf.b], eng='act')
            k.tt(sqb.t[:], kk.t[:], kk.t[:], ALU.mult, [kk.b], [sqb.b], eng=PE_)
            k.op('dve', lambda e, cs=cs, sgw=sgw: e.tensor_tensor_scan(out=cs.t[:], data0=reset, data1=sgw.t[:], initial=0.0,
                                                                     op0=ALU.mult, op1=ALU.add), [sgw.b, self.cf.b], [cs.b])
            yield
            psk = bank()
            k.mm(psk.t[:, 0:TR], blkB, sqb.t[:], True, True, [self.cb.b, sqb.b], [psk.b])
            csv = cs.t[:].rearrange("p (c j) -> p c j", c=NCH)
            k.act(W.t[:], cs.t[:], AF.Exp, [cs.b], [W.b], scale=-C0)
            k.act(Winv.t[:], cs.t[:], AF.Exp, [cs.b], [Winv.b], scale=C0)
            k.tt(t1.t[:], cs.t[:], sgw.t[:], ALU.subtract, [cs.b, sgw.b], [t1.b], eng=PE_)
            k.tt(t2.t[:].rearrange("p (c j) -> p c j", c=NCH), csv[:, :, 63:64].broadcast_to([128, NCH, 64]), csv, ALU.subtract, [cs.b], [t2.b], eng=PE_)
            yield
            k.act(Wex.t[:], t1.t[:], AF.Exp, [t1.b], [Wex.b], scale=-C0)
            k.act(WLr.t[:], t2.t[:], AF.Exp, [t2.b], [WLr.b], scale=-C0)
            k.act(st["WL"].t[:], csv[:, :, 63], AF.Exp, [cs.b], [st["WL"].b], scale=-C0)
            k.ts(t2.t[:], a.t[:], -1.0, ALU.add, [a.b, WLr.b], [t2.b])
            k.ts(t1.t[:], psk.t[:, 0:TR], 1e-24, ALU.max, [psk.b, Wex.b], [t1.b])
            k.ts(t2.t[:], t2.t[:], V("rw_k_a", hp), ALU.mult, [t2.b, vec.b], [t2.b])
            yield
            k.stt(kp.t[:], t2.t[:], 1.0, kf.t[:], ALU.add, ALU.mult, [t2.b, kf.b], [kp.b])
            k.act(t1.t[:], t1.t[:], AF.Ln, [t1.b], [t1.b])
            k.act(t1.t[:], t1.t[:], AF.Exp, [t1.b], [t1.b], scale=-0.5)
            AR, ARh = st["AR"], st["ARh"]
            ARv = AR.t[:].rearrange("p (c q j) -> p c q j", c=NCH, q=2)
            k.tt(ARv[:, :, 1, :], rf.t[:].rearrange("p (c j) -> p c j", c=NCH), W.t[:].rearrange("p (c j) -> p c j", c=NCH), ALU.mult,
                 [rf.b, W.b], [AR.b], eng=PE_)
            k.tt(R["KT"].t[:], kp.t[:], Winv.t[:], ALU.mult, [kp.b, Winv.b], [R["KT"].b], eng=PE_)
            k.tt(R["KH"].t[:], kp.t[:], WLr.t[:], ALU.mult, [kp.b, WLr.b], [R["KH"].b], eng=PE_)
            k.stt(sqb.t[:], rf.t[:], V("rw_r_k", hp), kp.t[:], ALU.mult, ALU.mult, [rf.b, vec.b, kp.b], [sqb.b])
            yield
            k.dma('sp', R["KTh"].t[0:64, :], R["KT"].t[64:128, :], reads=[R["KT"].b], writes=[R["KTh"].b])
            k.dma('sp', st["WLh"].t[0:64, :], st["WL"].t[64:128, :], reads=[st["WL"].b], writes=[st["WLh"].b])
            psb = bank()
            k.mm(psb.t[:, 0:TR], blkB, sqb.t[:], True, True, [self.cb.b, sqb.b], [psb.b])
            k.tt(kk.t[:], kk.t[:], t1.t[:], ALU.mult, [kk.b, t1.b], [kk.b], eng=PE_)
            k.tt(ka.t[:], kk.t[:], a.t[:], ALU.mult, [kk.b, a.b], [ka.b], eng=PE_)
            k.stt(ARv[:, :, 0, :], kk.t[:].rearrange("p (c j) -> p c j", c=NCH), -1.0, Wex.t[:].rearrange("p (c j) -> p c j", c=NCH),
                  ALU.mult, ALU.mult, [kk.b, Wex.b], [AR.b])
            k.tt(R["BT"].t[:], ka.t[:], Winv.t[:], ALU.mult, [ka.b, Winv.b], [R["BT"].b], eng=PE_)
            k.tt(R["BH"].t[:], ka.t[:], WLr.t[:], ALU.mult, [ka.b, WLr.b], [R["BH"].b], eng=PE_)
            yield
            k.dma('sp', ARh.t[0:64, :], AR.t[64:128, :], reads=[AR.b], writes=[ARh.b])
            k.dma('sp', R["BTh"].t[0:64, :], R["BT"].t[64:128, :], reads=[R["BT"].b], writes=[R["BTh"].b])
            k.tt(bonus.t[:], psb.t[:, 0:TR], vf.t[:], ALU.mult, [psb.b, vf.b], [bonus.b])
            ARe = [AR.t[0:64, :].rearrange("p (c q j) -> p c q j", c=NCH, q=2), ARh.t[0:64, :].rearrange("p (c q j) -> p c q j", c=NCH, q=2)]
            ARb = [AR.b, ARh.b]
            BTe = [R["BT"].t[0:64, :], R["BTh"].t[0:64, :]]
            BTb = [R["BT"].b, R["BTh"].b]
            KTe = [R["KT"].t[0:64, :], R["KTh"].t[0:64, :]]
            KTb = [R["KT"].b, R["KTh"].b]
            tok = st["tok"]
            tokv = tok.t[0:64, :].rearrange("p (c q m) -> p c q m", c=NCH, q=4)
            pts = []
            for cc in range(NCH // 2):
                pt = pbank[cc % 2]
                ptv = pt.t[0:64, :].bitcast(BF16)[:, 0:1024].rearrange("p (c q m) -> p c q m", c=2, q=4)
                for c2 in range(2):
                    c = 2 * cc + c2
                    cs_ = slice(c * 64, (c + 1) * 64)
                    for q, src in enumerate((R["VB"], R["BH"], R["KH"])):
                        k.tr(ptv[:, c2, q, :], src.t[:, cs_], identB, [src.b, self.cb.b], [pt.b])
                    k.tr(ptv[:, c2, 3, :], ARv[:, c, 0, :], identB, [AR.b, self.cb.b], [pt.b])
                pts.append((pt, ptv, cc))
            yield
            for pt, ptv, cc in pts:
                k.cp(tokv[:, 2 * cc:2 * cc + 2, :, :], ptv, [pt.b], [tok.b], eng='act')
            yield
            SC = st["SC"]
            SCv = SC.t[0:64, :].rearrange("p (j n) -> p j n", j=8)
            pend = []
            for j in range(8):
                c, e = j // 2, j % 2
                cs_ = slice(c * 64, (c + 1) * 64)
                psc = pbank[j % 2]
                k.mm(psc.t[0:64, 0:128], BTe[e][:, cs_], ARe[e][:, c, :, :], True, True, [BTb[e], ARb[e]], [psc.b])
                k.mm(psc.t[0:64, 128:256], KTe[e][:, cs_], ARe[e][:, c, :, :], True, True, [KTb[e], ARb[e]], [psc.b])
                k.mm(psc.t[0:64, 256:320], ARe[e][:, c, 0, :], BTe[e][:, cs_], True, True, [BTb[e], ARb[e]], [psc.b])
                pend.append((j, psc))
                if j % 2 == 1:
                    yield
                    for j_, p_ in pend:
                        k.tt(SCv[:, j_, :], p_.t[0:64, 0:320], m320, ALU.mult, [p_.b, self.cf.b], [SC.b])
                    pend = []
            yield

        def stageA2(sub, hp, st):
            SC, Tall = st["SC"], st["Tall"]
            SCv = SC.t[0:64, :].rearrange("p (j n) -> p j n", j=8)
            Tallv = Tall.t[0:64, :].rearrange("p (j n) -> p j n", j=8)
            TT, PP = R["TT"], R["PP"]
            TTv = [t_.t[0:64, :].rearrange("p (j n) -> p j n", j=8) for t_ in TT]
            PPv = [p_.t[0:64, :].rearrange("p (j n) -> p j n", j=8) for p_ in PP]
            k.tt(TTv[0], SCv[:, :, 0:64], identF.unsqueeze(1).broadcast_to([64, 8, 64]), ALU.add, [SC.b, self.cf.b], [TT[0].b])
            ppb = [self.ps[4], self.ps[5]]
            tt_ps = self.ps[6]
            ppv = [b_.t[0:64, :].rearrange("p (j n) -> p j n", j=4) for b_ in ppb]
            ttv = tt_ps.t[0:64, :].rearrange("p (j n) -> p j n", j=8)
            def tmm(s_):
                cur = (s_ - 1) % 2
                tin = (s_ - 1) % 2
                for j in range(8):
                    k.mm(ttv[:, j, :], PPv[cur][:, j, 64:128], TTv[tin][:, j, :], True, True, [PP[cur].b, TT[tin].b], [tt_ps.b])

            def tev(s_):
                tin, tout = (s_ - 1) % 2, s_ % 2
                if s_ < 5:
                    k.tt(TTv[tout], ttv, TTv[tin], ALU.add, [tt_ps.b, TT[tin].b], [TT[tout].b])
                else:
                    k.tt(Tallv, ttv, TTv[tin], ALU.add, [tt_ps.b, TT[tin].b], [Tall.b])
            for s_ in range(1, 6):
                cur, prv = (s_ - 1) % 2, s_ % 2
                for j in range(8):
                    if s_ == 1:
                        P_, PT_ = SCv[:, j, 0:64], SCv[:, j, 256:320]
                        rb = [SC.b]
                    else:
                        P_, PT_ = PPv[prv][:, j, 0:64], PPv[prv][:, j, 64:128]
                        rb = [PP[prv].b]
                    if s_ < 5:
                        k.mm(ppv[j // 4][:, j % 4, 0:64], PT_, P_, True, True, rb, [ppb[j // 4].b])
                    k.mm(ppv[j // 4][:, j % 4, 64:128], P_, PT_, True, True, rb, [ppb[j // 4].b])
                if s_ >= 2:
                    tmm(s_ - 1)
                yield
                if s_ >= 2:
                    tev(s_ - 1)
                k.cp(PPv[cur][:, 0:4, :], ppv[0], [ppb[0].b], [PP[cur].b], eng='act')
                k.cp(PPv[cur][:, 4:8, :], ppv[1], [ppb[1].b], [PP[cur].b], eng='act')
                yield
            tmm(5)
            yield
            tev(5)
            yield
            tok = st["tok"]
            tokv = tok.t[0:64, :].rearrange("p (c q m) -> p c q m", c=NCH, q=4)
            AT, Uv, RV = st["AT"], st["Uv"], R["PP"][0]
            ATv = AT.t[0:64, :].rearrange("p (j n) -> p j n", j=8)
            Uvv = Uv.t[0:64, :].rearrange("p (j n) -> p j n", j=8)
            RVv = RV.t[0:64, 0:512].rearrange("p (j n) -> p j n", j=8)
            pa_, pb_ = self.ps[4], self.ps[5]
            pav = pa_.t[0:64, :].rearrange("p (j n) -> p j n", j=8)
            pbv = pb_.t[0:64, :].rearrange("p (j n) -> p j n", j=8)
            for j in range(8):
                c, e = j // 2, j % 2
                es = slice(e * 64, (e + 1) * 64)
                k.mm(pav[:, j, :], tokv[:, c, 3, es], Tallv[:, j, :], True, True, [tok.b, Tall.b], [pa_.b])
                k.mm(pbv[:, j, :], SCv[:, j, 128:192], tokv[:, c, 0, es], True, True, [SC.b, tok.b], [pb_.b])
            yield
            k.cp(ATv, pav, [pa_.b], [AT.b], eng='act')
            k.cp(RVv, pbv, [pb_.b], [RV.b], eng='act')
            yield
            for j in range(8):
                k.mm(pav[:, j, :], Tallv[:, j, :], RVv[:, j, :], True, True, [Tall.b, RV.b], [pa_.b])
            yield
            k.cp(Uvv, pav, [pa_.b], [Uv.b], eng='act')
            yield

        def stageB(sub, hp, st):
            AR, ARh, tok, SC, Tall = st["AR"], st["ARh"], st["tok"], st["SC"], st["Tall"]
            ARe = [AR.t[0:64, :].rearrange("p (c q j) -> p c q j", c=NCH, q=2), ARh.t[0:64, :].rearrange("p (c q j) -> p c q j", c=NCH, q=2)]
            ARb = [AR.b, ARh.b]
            WLe = [st["WL"].t[0:64, :], st["WLh"].t[0:64, :]]
            WLb = [st["WL"].b, st["WLh"].b]
            tokv = tok.t[0:64, :].rearrange("p (c q m) -> p c q m", c=NCH, q=4)
            SCv = SC.t[0:64, :].rearrange("p (j n) -> p j n", j=8)
            Tallv = Tall.t[0:64, :].rearrange("p (j n) -> p j n", j=8)
            Sf, Sb = self.rw_Sf[hp], self.rw_Sb[hp]
            Ub = R["Ub"]
            ATv = st["AT"].t[0:64, :].rearrange("p (j n) -> p j n", j=8)
            Uvv = st["Uv"].t[0:64, :].rearrange("p (j n) -> p j n", j=8)
            pst, psY = self.ps[7], self.ps[1]
            for c in range(NCH):
                cs_ = slice(c * 64, (c + 1) * 64)
                for e in range(2):
                    es = slice(e * 64, (e + 1) * 64)
                    k.mm(pst.t[0:64, es], ATv[:, 2 * c + e, :], Sb.t[:, es], True, True, [st["AT"].b, Sb.b], [pst.b])
                for e in range(2):
                    es = slice(e * 64, (e + 1) * 64)
                    k.mm(psY.t[es, cs_], Sb.t[:, es], ARe[e][:, c, 1, :], True, False, [Sb.b, ARb[e]], [psY.b])
                    k.mm(psY.t[es, cs_], tokv[:, c, 0, es], SCv[:, 2 * c + e, 192:256], False, False, [tok.b, SC.b], [psY.b], skip=True)
                yield
                k.tt(Ub.t[0:64, :].rearrange("p (e v) -> p e v", e=2), pst.t[0:64, 0:128].rearrange("p (e v) -> p e v", e=2),
                     Uvv[:, 2 * c:2 * c + 2, :], ALU.add, [pst.b, st["Uv"].b], [Ub.b])
                yield
                for e in range(2):
                    es = slice(e * 64, (e + 1) * 64)
                    k.mm(pst.t[0:64, 256 + e * 64:320 + e * 64], tokv[:, c, 1, es], Ub.t[0:64, es], True, False, [tok.b, Ub.b], [pst.b])
                    k.mm(pst.t[0:64, 256 + e * 64:320 + e * 64], tokv[:, c, 2, es], tokv[:, c, 0, es], False, True, [tok.b], [pst.b])
                for e in range(2):
                    es = slice(e * 64, (e + 1) * 64)
                    k.mm(psY.t[es, cs_], Ub.t[0:64, es], SCv[:, 2 * c + e, 64:128], False, True, [Ub.b, SC.b], [psY.b], skip=True)
                yield
                for e in range(2):
                    es = slice(e * 64, (e + 1) * 64)
                    k.stt(Sb.t[:, es], Sf.t[:, es], WLe[e][:, c:c + 1], pst.t[0:64, 256 + e * 64:320 + e * 64], ALU.mult, ALU.add,
                          [self.rw_Sfb[hp][e], WLb[e], pst.b], [Sb.b])
                for e in range(2):
                    es = slice(e * 64, (e + 1) * 64)
                    k.stt(Sf.t[:, es], Sf.t[:, es], WLe[e][:, c:c + 1], pst.t[0:64, 256 + e * 64:320 + e * 64], ALU.mult, ALU.add,
                          [self.rw_Sfb[hp][e], WLb[e], pst.b], [self.rw_Sfb[hp][e]])
                yield
            y = R["y"][hp % 2]
            yb = R["Ub2"][hp % 2]
            k.cp(y.t[:], psY.t[:, 0:TR], [psY.b], [y.b], eng='act')
            k.cp(yb.t[:], psY.t[:, 0:TR], [psY.b], [yb.b])
            yield

        def stageBe(sub, hp, bg):
            y, yc, ysq = R["y"][hp % 2], R["yc"], R["ysq"]
            yb = R["Ub2"][hp % 2]
            pe_ = self.ps[3]
            pev = pe_.t[:, 0:TR]
            k.mm(pev, blkB, yb.t[:], True, True, [self.cb.b, yb.b], [pe_.b])
            yield
            k.stt(yc.t[:], pev, -1.0 / 64, y.t[:], ALU.mult, ALU.add, [pe_.b, y.b], [yc.b])
            yield
            k.act(yb.t[:], yc.t[:], AF.Square, [yc.b], [yb.b])
            yield
            k.mm(pev, blkB, yb.t[:], True, True, [self.cb.b, yb.b], [pe_.b])
            yield
            k.act(ysq.t[:], pev, AF.Ln, [pe_.b, self.cf.b], [ysq.b], bias=self.cF("gneps")[:, 0:1], scale=1.0 / 64)
            k.act(ysq.t[:], ysq.t[:], AF.Exp, [ysq.b], [ysq.b], scale=-0.5)
            yield
            k.tt(yc.t[:], yc.t[:], ysq.t[:], ALU.mult, [yc.b, ysq.b], [yc.b])
            yield
            k.ts(yc.t[:], yc.t[:], V("rw_gn_g", hp), ALU.mult, [yc.b, vec.b], [yc.b], s2=V("rw_gn_b", hp), op1=ALU.add)
            yield
            k.tt(yc.t[:], yc.t[:], bg["bonus"].t[:], ALU.add, [yc.b, bg["bonus"].b], [yc.b])
            yield
            k.tt(zv[:, hp, :], yc.t[:], bg["gf"].t[:], ALU.mult, [yc.b, bg["gf"].b], [R["zT"].b])
            yield
            if hp == 7:
                t0 = sub * TR
                last = (sub == T // TR - 1)
                banks = [self.ps[3], self.ps[0]] if last else [self.ps[3]]
                pend = None
                for g in range(2):
                    slot = self.load_piece(pre + f"rwwo{g}")
                    wv = slot.t[:, 0:4 * KC * 128].rearrange("p (o c m) -> p o c m", o=4, c=KC)
                    for o in range(4):
                        fc = 4 * g + o
                        ps = banks[fc % len(banks)]
                        for c in range(KC):
                            k.mm(ps.t[:, 0:TR], wv[:, o, c, :], zv[:, c, :], c == 0, c == KC - 1, [slot.b, R["zT"].b], [ps.b])
                        if last:
                            if pend is not None:
                                pf, pp = pend
                                k.tt(self.hTv[:, pf, t0:t0 + TR], self.hTv[:, pf, t0:t0 + TR], pp.t[:, 0:TR], ALU.add, [self.hTb[pf], pp.b], [self.hTb[pf]])
                            pend = (fc, ps)
                            yield
                        else:
                            yield
                            k.tt(self.hTv[:, fc, t0:t0 + TR], self.hTv[:, fc, t0:t0 + TR], ps.t[:, 0:TR], ALU.add, [self.hTb[fc], ps.b], [self.hTb[fc]])
                            yield
                if pend is not None:
                    pf, pp = pend
                    k.tt(self.hTv[:, pf, t0:t0 + TR], self.hTv[:, pf, t0:t0 + TR], pp.t[:, 0:TR], ALU.add, [self.hTb[pf], pp.b], [self.hTb[pf]])
                    yield

        def chain(*gens):
            for g in gens:
                yield from g

        def interleave(*gens):
            gens = [g for g in gens if g is not None]
            done = [False] * len(gens)
            while not all(done):
                for i_, g in enumerate(gens):
                    if not done[i_]:
                        try:
                            next(g)
                        except StopIteration:
                            done[i_] = True

        tasks = [(sub, hp) for sub in range(T // TR) for hp in range(8)]
        sets = R["sets"]
        N_ = len(tasks)
        bgs = R["bg"]
        for i in range(N_ + 3):
            g1 = g2 = g3 = g4 = None
            if i < N_:
                sub, hp = tasks[i]
                g1 = stageA1(sub, hp, sets[i % 3], bgs[i % 4])
                if hp == 0:
                    g1 = chain(preamble(sub), g1)
            if 0 <= i - 1 < N_:
                sub, hp = tasks[i - 1]
                g2 = stageA2(sub, hp, sets[(i - 1) % 3])
            if 0 <= i - 2 < N_:
                sub, hp = tasks[i - 2]
                g3 = stageB(sub, hp, sets[(i - 2) % 3])
            if os.environ.get("RWDBG") == "1":
                if g3 is not None:
                    sub, hp = tasks[i - 2]
                    g3 = chain(g3, stageBe(sub, hp, bgs[(i - 2) % 4]))
            elif 0 <= i - 3 < N_:
                sub, hp = tasks[i - 3]
                g4 = stageBe(sub, hp, bgs[(i - 3) % 4])
            interleave(g4, g3, g2, g1)
        k.handoff([self.xn.b, self.sq.b], self.xnb + self.sqb8)

    def swa(self, li, ti):
        k = self.k
        pre = f"L{li}_"
        xn = self.xn
        ident = self.cB("ident")
        t0 = ti * T
        if ti == 0:
            k.act(self.sw_es.t[:], self.vec.t[:, self.pk.vec_cols[pre + "sw_sink"]:self.pk.vec_cols[pre + "sw_sink"] + 8], AF.Exp,
                  [self.vec.b], [self.sw_es.b])
        pi_, y, f, g, ni = self.sw_pi, self.sw_y, self.sw_f, self.sw_g, self.sw_ni
        k.dma('sp', pi_.t[:], self.pos[0, t0:t0 + T].partition_broadcast(128), writes=[pi_.b])
        k.cp(y.t[:], pi_.t[:], [pi_.b], [y.b])
        k.ts(y.t[:], y.t[:], self.cF("invf")[:, 0:1], ALU.mult, [y.b, self.cf.b], [y.b])
        for which, dst in ((0, self.sw_sin), (1, self.sw_cos)):
            if which == 1:
                k.ts(y.t[:], y.t[:], 0.25, ALU.add, [y.b], [y.b])
            k.cp(ni.t[:], y.t[:], [y.b], [ni.b])
            k.cp(f.t[:], ni.t[:], [ni.b], [f.b])
            k.tt(f.t[:], y.t[:], f.t[:], ALU.subtract, [y.b, f.b], [f.b])
            k.ts(g.t[:], f.t[:], 0.5, ALU.is_gt, [f.b], [g.b])
            k.tt(f.t[:], f.t[:], g.t[:], ALU.subtract, [f.b, g.b], [f.b])
            k.ts(g.t[:], f.t[:], -0.5, ALU.is_lt, [f.b], [g.b])
            k.tt(f.t[:], f.t[:], g.t[:], ALU.add, [f.b, g.b], [f.b])
            k.act(dst.t[:], f.t[:], AF.Sin, [f.b], [dst.b], scale=2.0 * np.pi * (1.0 - 1e-6))
        import os
        stop = int(os.environ.get("SWSTOP", "9"))
        self.preload_exp_table()
        rotT = self.cF("rotT")
        pieces = []
        for pname, G in (("swin0", 4), ("swin1", 4), ("swin2", 2)):
            for o in range(G):
                pieces.append((pname, G, o))

        def post(oc):
            if oc < 9:
                qf, t1, t2 = self.sw_qf[oc % 2], self.sw_t1[oc % 2], self.sw_t2[oc % 2]
                pr = self.ps[2 + oc % 2]
                k.mm(pr.t[:], rotT, qf.t[:], True, True, [self.cf.b, qf.b], [pr.b])
                k.tt(t1.t[:], qf.t[:], self.sw_cos.t[:], ALU.mult, [qf.b, self.sw_cos.b], [t1.b])
                k.tt(t2.t[:], pr.t[:], self.sw_sin.t[:], ALU.mult, [pr.b, self.sw_sin.b], [t2.b])
                if oc < 8:
                    k.tt(self.sw_qb[oc].t[:], t1.t[:], t2.t[:], ALU.add, [t1.b, t2.b], [self.sw_qb[oc].b])
                else:
                    k.tt(self.sw_kT.t[:, 128:640], t1.t[:], t2.t[:], ALU.add, [t1.b, t2.b], [self.sw_kT.b])
            else:
                pt = self.ps[2]
                ptv = pt.t[:].bitcast(BF16)[:, 0:512]
                for n in range(4):
                    k.tr(ptv[:, n * 128:(n + 1) * 128], self.sw_vb.t[:, n * 128:(n + 1) * 128], ident, [self.sw_vb.b, self.cb.b], [pt.b])
                k.cp(self.sw_vt.t[:, 128:640], ptv, [pt.b], [self.sw_vt.b], eng='act')
        slot = None
        for oc, (pname, G, o) in enumerate(pieces):
            if o == 0:
                slot = self.load_piece(pre + pname)
                wv = slot.t[:, 0:G * KC * 128].rearrange("p (o c m) -> p o c m", o=G, c=KC)
            ps = self.ps[oc % 2]
            for c in range(KC):
                k.mm(ps.t[:], wv[:, o, c, :], self.xnv[:, c, :], c == 0, c == KC - 1, [slot.b, self.xnb[c]], [ps.b])
            if oc < 9:
                qf = self.sw_qf[oc % 2]
                k.act(qf.t[:], ps.t[:], AF.Identity, [ps.b, self.vec.b], [qf.b], bias=self.vcol(pre + "sw_b_qkv", oc))
            else:
                k.act(self.sw_vb.t[:], ps.t[:], AF.Identity, [ps.b, self.vec.b], [self.sw_vb.b], bias=self.vcol(pre + "sw_b_qkv", oc))
            if oc >= 1:
                post(oc - 1)
        post(9)
        o_le = CST["mask_le"][0]
        mask_cp = self.cb.t[:, o_le:o_le + 256]
        onesb = self.cB("ones")
        ov = self.sw_o.t[:].rearrange("p (c t) -> p c t", c=8)

        def sc(c):
            qb = self.sw_qb[c]
            E = self.sw_E[c % 2]
            for kb in range(5):
                if kb == 0 and ti == 0:
                    continue
                Ev = E[kb].t[:].rearrange("p (e q) -> p e q", e=2)
                lo, hi = (128, 256) if kb == 0 else ((0, 128) if kb == 4 else (0, 256))
                q0 = (kb - 1) * 128 + lo
                for e in range(2):
                    ps = self.ps[e + 2 * (kb % 2)]
                    k.mm(ps.t[:, lo:hi], self.sw_kT.t[e * 64:(e + 1) * 64, kb * 128:(kb + 1) * 128],
                         qb.t[e * 64:(e + 1) * 64, q0:q0 + (hi - lo)], True, True, [self.sw_kT.b, qb.b], [ps.b])
                    k.act(Ev[:, e, lo:hi], ps.t[:, lo:hi], AF.Exp, [ps.b], [E[kb].b], scale=0.125)
                k.tt(Ev[:, :, lo:hi], Ev[:, :, lo:hi], mask_cp[:, lo:hi].unsqueeze(1).broadcast_to([128, 2, hi - lo]), ALU.mult,
                     [E[kb].b, self.cb.b], [E[kb].b])

        def pv(c):
            E = self.sw_E[c % 2]
            pnum, pden = self.ps[4 + 2 * (c % 2)], self.ps[5 + 2 * (c % 2)]
            for n in range(4):
                for e in range(2):
                    has_prev = not (ti == 0 and n == 0)
                    for (pp, isnum) in ((pnum, True), (pden, False)):
                        outap = pp.t[e * 64:(e + 1) * 64, n * 128:(n + 1) * 128]
                        if has_prev:
                            Ep = E[n].t[:].rearrange("p (e q) -> p e q", e=2)[:, e, 128:256]
                            lh = self.sw_vt.t[:, n * 128 + e * 64:n * 128 + (e + 1) * 64] if isnum else onesb[:, 0:64]
                            k.mm(outap, lh, Ep, True, False, [self.sw_vt.b, self.cb.b, E[n].b], [pp.b])
                        Ec = E[n + 1].t[:].rearrange("p (e q) -> p e q", e=2)[:, e, 0:128]
                        lh = self.sw_vt.t[:, (n + 1) * 128 + e * 64:(n + 1) * 128 + (e + 1) * 64] if isnum else onesb[:, 0:64]
                        k.mm(outap, lh, Ec, not has_prev, True, [self.sw_vt.b, self.cb.b, E[n + 1].b], [pp.b])

        def nm(c):
            pnum, pden = self.ps[4 + 2 * (c % 2)], self.ps[5 + 2 * (c % 2)]
            ds = self.sw_ds[c % 2]
            k.ts(ds.t[:], pden.t[:], self.sw_es.t[:, c:c + 1], ALU.add, [pden.b, self.sw_es.b], [ds.b])
            k.act(ds.t[:], ds.t[:], AF.Ln, [ds.b], [ds.b])
            k.act(ds.t[:], ds.t[:], AF.Exp, [ds.b], [ds.b], scale=-1.0)
            k.tt(ov[:, c, :], pnum.t[:], ds.t[:], ALU.mult, [pnum.b, ds.b], [self.sw_o.b])
        for step in range(8 + 2):
            if step < 8:
                sc(step)
            if 0 <= step - 1 < 8:
                pv(step - 1)
            if 0 <= step - 2 < 8:
                nm(step - 2)
        k.cp(self.sw_kT.t[:, 0:128], self.sw_kT.t[:, 512:640], [self.sw_kT.b], [self.sw_kT.b])
        k.cp(self.sw_vt.t[:, 0:128], self.sw_vt.t[:, 512:640], [self.sw_vt.b], [self.sw_vt.b])
        for g in range(2):
            slot = self.load_piece(pre + f"swwo{g}")
            wv = slot.t[:, 0:4 * KC * 128].rearrange("p (o c m) -> p o c m", o=4, c=KC)
            for o in range(4):
                fc = 4 * g + o
                ps = self.ps[fc % 2]
                for c in range(KC):
                    k.mm(ps.t[:], wv[:, o, c, :], ov[:, c, :], c == 0, c == KC - 1, [slot.b, self.sw_o.b], [ps.b])
                k.stt(self.hTv[:, fc, :], ps.t[:], self.vcol(pre + "sw_b_o", fc), self.hTv[:, fc, :], ALU.add, ALU.add,
                      [ps.b, self.vec.b, self.hTb[fc]], [self.hTb[fc]])

    def gla(self, li, ti):
        k = self.k
        pre = f"L{li}_"
        xn = self.xn
        ident = self.cB("ident")
        ones = self.cB("ones")
        if ti == 0:
            c0_ = self.pk.vec_cols[pre + "gl_b_a"]
            k.ts(self.gl_nb.t[:], self.vec.t[:, c0_:c0_ + 4], -1.0, ALU.mult, [self.vec.b], [self.gl_nb.b])
            for h in range(4):
                k.op('dve', lambda e, h=h: e.memset(self.gl_Sf[h].t[:], 0.0), [], [self.gl_Sf[h].b])
                k.op('dve', lambda e, h=h: e.memset(self.gl_Sb[h].t[:], 0.0), [], [self.gl_Sb[h].b])
        k.handoff(self.owners['m3b'], self.owners['m3a'])

        def run(*gens):
            gens = list(gens)
            done = [False] * len(gens)
            while not all(done):
                for i_, g in enumerate(gens):
                    if not done[i_]:
                        try:
                            next(g)
                        except StopIteration:
                            done[i_] = True
        slot = self.load_piece(pre + "glal")
        wv = slot.t[:, 0:KC * 16].rearrange("p (c m) -> p c m", c=KC)
        ps = self.ps[4]
        for c in range(KC):
            k.mm(ps.t[0:16, :], wv[:, c, :], self.xnv[:, c, :], c == 0, c == KC - 1, [slot.b, self.xnb[c]], [ps.b])
        k.cp(self.gl_al.t[0:16, :], ps.t[0:16, :], [ps.b], [self.gl_al.b], eng='act')
        slot = self.load_piece(pre + "glwa2")
        k.cp(self.gl_wa2.t[0:16, :], slot.t[0:16, 0:512], [slot.b], [self.gl_wa2.b])

        def dec(h):
            ps = self.ps[h]
            la = self.gl_la[h]
            k.mm(ps.t[:], self.gl_wa2.t[0:16, h * 128:(h + 1) * 128], self.gl_al.t[0:16, :], True, True,
                 [self.gl_wa2.b, self.gl_al.b], [ps.b])
            yield
            k.act(la.t[:], ps.t[:], AF.Exp, [ps.b, self.gl_nb.b], [la.b], bias=self.gl_nb.t[:, h:h + 1], scale=-1.0)
            k.act(la.t[:], la.t[:], AF.Ln, [la.b, self.cf.b], [la.b], bias=self.cF("one")[:, 0:1])
            yield
            k.op('dve', lambda e, la=la: e.tensor_tensor_scan(out=la.t[:], data0=self.cF("reset64"), data1=la.t[:], initial=0.0,
                                                              op0=ALU.mult, op1=ALU.add), [la.b, self.cf.b], [la.b])
            yield
            lav = la.t[:].rearrange("p (c j) -> p c j", c=8)
            k.act(self.gl_eb[h].t[:], la.t[:], AF.Exp, [la.b], [self.gl_eb[h].b], scale=-1.0 / 16)
            k.act(self.gl_enb[h].t[:], la.t[:], AF.Exp, [la.b], [self.gl_enb[h].b], scale=1.0 / 16)
            k.act(self.gl_dc[h].t[:], lav[:, :, 63], AF.Exp, [la.b], [self.gl_dc[h].b], scale=-1.0 / 16)
            ksfv = self.gl_ksf[h].t[:].rearrange("p (c j) -> p c j", c=8)
            k.tt(ksfv, lav[:, :, 63:64].broadcast_to([128, 8, 64]), lav, ALU.subtract, [la.b], [self.gl_ksf[h].b])
            yield
            k.act(self.gl_ksf[h].t[:], self.gl_ksf[h].t[:], AF.Exp, [self.gl_ksf[h].b], [self.gl_ksf[h].b], scale=-1.0 / 16)
        run(*[dec(h) for h in range(4)])
        for g in range(6):
            slot = self.load_piece(pre + f"glin{g}")
            wv = slot.t[:, 0:4 * KC * 128].rearrange("p (o c m) -> p o c m", o=4, c=KC)
            for o in range(4):
                oc = 4 * g + o
                ps = self.ps[oc % 4]
                for c in range(KC):
                    k.mm(ps.t[:], wv[:, o, c, :], self.xnv[:, c, :], c == 0, c == KC - 1, [slot.b, self.xnb[c]], [ps.b])
                if oc < 4:
                    h = oc
                    k.stt(self.gl_qg[h].t[:], ps.t[:], 128.0 ** -0.5, self.gl_eb[h].t[:], ALU.mult, ALU.mult,
                          [ps.b, self.gl_eb[h].b], [self.gl_qg[h].b])
                elif oc < 8:
                    h = oc - 4
                    k.tt(self.gl_kg[h].t[:], ps.t[:], self.gl_enb[h].t[:], ALU.mult, [ps.b, self.gl_enb[h].b], [self.gl_kg[h].b])
                    k.tt(self.gl_ks[h].t[:], ps.t[:], self.gl_ksf[h].t[:], ALU.mult, [ps.b, self.gl_ksf[h].b], [self.gl_ks[h].b])
                elif oc < 16:
                    k.cp(self.gl_v[oc - 8].t[:], ps.t[:], [ps.b], [self.gl_v[oc - 8].b], eng='act')
                else:
                    k.act(self.gl_gate[oc - 16].t[:], ps.t[:], AF.Silu, [ps.b], [self.gl_gate[oc - 16].b])
        self.preload_exp_table()
        k.handoff(self.owners['m3a'], self.owners['m3b'])
        zv = self.gl_z.t[:].rearrange("p (c t) -> p c t", c=8)
        mask = self.cF("mask_le", 0, 64)[:, 0:64]

        def prep(h):
            tok = self.gl_tok[h]
            tokv = tok.t[0:64, :].rearrange("p (c m) -> p c m", c=8)
            att = self.gl_att[h]
            attv = att.t[0:64, :].rearrange("p (c i) -> p c i", c=8)
            pt = self.ps[h]
            for cc in range(4):
                ptv = pt.t[0:64, :].bitcast(BF16)[:, 0:768].rearrange("p (c m) -> p c m", c=2)
                for c2 in range(2):
                    c = 2 * cc + c2
                    cs = slice(c * 64, (c + 1) * 64)
                    k.tr(ptv[:, c2, 0:128], self.gl_ks[h].t[:, cs], ident, [self.gl_ks[h].b, self.cb.b], [pt.b])
                    for e in range(2):
                        k.tr(ptv[:, c2, 128 + e * 128:256 + e * 128], self.gl_v[2 * h + e].t[:, cs], ident,
                             [self.gl_v[2 * h + e].b, self.cb.b], [pt.b])
                yield
                k.cp(tokv[:, 2 * cc:2 * cc + 2, :], ptv, [pt.b], [tok.b], eng='act')
                yield
            pa = pt
            for c in range(8):
                cs = slice(c * 64, (c + 1) * 64)
                k.mm(pa.t[0:64, cs], self.gl_kg[h].t[:, cs], self.gl_qg[h].t[:, cs], True, True,
                     [self.gl_kg[h].b, self.gl_qg[h].b], [pa.b])
            yield
            k.tt(attv, pa.t[0:64, :].rearrange("p (c i) -> p c i", c=8), mask.unsqueeze(1).broadcast_to([64, 8, 64]), ALU.mult,
                 [pa.b, self.cf.b], [att.b])
            yield
        run(*[prep(h) for h in range(4)])

        ofv = self.gl_ofa.t[:].rearrange("p (x t) -> p x t", x=8)
        osqv = self.gl_osqa.t[:].rearrange("p (x t) -> p x t", x=8)
        tokvs = [self.gl_tok[h].t[0:64, :].rearrange("p (c m) -> p c m", c=8) for h in range(4)]
        attvs = [self.gl_att[h].t[0:64, :].rearrange("p (c i) -> p c i", c=8) for h in range(4)]
        for c in range(8):
            cs = slice(c * 64, (c + 1) * 64)
            po = self.ps[2 + c % 2]
            pov = po.t[:].rearrange("p (x i) -> p x i", x=8)
            pss = [self.ps[4], self.ps[5]]
            for h in range(4):
                for e in range(2):
                    k.mm(pov[:, 2 * h + e, :], tokvs[h][:, c, 128 + e * 128:256 + e * 128], attvs[h][:, c, :], True, False,
                         [self.gl_tok[h].b, self.gl_att[h].b], [po.b])
                    k.mm(pov[:, 2 * h + e, :], self.gl_Sb[h].t[:, e * 128:(e + 1) * 128], self.gl_qg[h].t[:, cs], False, True,
                         [self.gl_Sb[h].b, self.gl_qg[h].b], [po.b])
            for h in range(4):
                k.mm(pss[h // 2].t[:, (h % 2) * 256:(h % 2) * 256 + 256], tokvs[h][:, c, 0:128], tokvs[h][:, c, 128:384], True, True,
                     [self.gl_tok[h].b], [pss[h // 2].b])
            k.cp(ofv[:, :, cs], pov, [po.b], [self.gl_ofa.b], eng='act')
            for h in range(4):
                k.stt(self.gl_Sf[h].t[:], self.gl_Sf[h].t[:], self.gl_dc[h].t[:, c:c + 1], pss[h // 2].t[:, (h % 2) * 256:(h % 2) * 256 + 256],
                      ALU.mult, ALU.add, [self.gl_Sf[h].b, self.gl_dc[h].b, pss[h // 2].b], [self.gl_Sf[h].b])
            for h in range(4):
                k.cp(self.gl_Sb[h].t[:], self.gl_Sf[h].t[:], [self.gl_Sf[h].b], [self.gl_Sb[h].b], eng='act')
        k.act(self.gl_osqa.t[:], self.gl_ofa.t[:], AF.Square, [self.gl_ofa.b], [self.gl_osqa.b])

        def epi(h):
            q = h % 2
            pn = self.ps[q]
            rs = self.gl_rs[q]
            for e in range(2):
                k.mm(pn.t[:], ones, osqv[:, 2 * h + e, :], e == 0, e == 1, [self.gl_osqa.b, self.cb.b], [pn.b])
            yield
            k.act(rs[0].t[:], pn.t[:], AF.Ln, [pn.b, self.cf.b], [rs[0].b], bias=self.cF("eps")[:, 0:1], scale=1.0 / 256)
            k.act(rs[1].t[:], rs[0].t[:], AF.Exp, [rs[0].b], [rs[1].b], scale=-0.5)
            yield
            for e in range(2):
                k.stt(rs[0].t[:], ofv[:, 2 * h + e, :], self.vcol(pre + "gl_gn_g", 2 * h + e), rs[1].t[:], ALU.mult, ALU.mult,
                      [self.gl_ofa.b, self.vec.b, rs[1].b], [rs[0].b])
                k.tt(zv[:, 2 * h + e, :], rs[0].t[:], self.gl_gate[2 * h + e].t[:], ALU.mult,
                     [rs[0].b, self.gl_gate[2 * h + e].b], [self.gl_z.b])
            yield
        run(epi(0), epi(1))
        run(epi(2), epi(3))
        for g in range(2):
            slot = self.load_piece(pre + f"glwo{g}")
            wv = slot.t[:, 0:4 * KC * 128].rearrange("p (o c m) -> p o c m", o=4, c=KC)
            for o in range(4):
                fc = 4 * g + o
                ps = self.ps[fc % 2]
                for c in range(KC):
                    k.mm(ps.t[:], wv[:, o, c, :], zv[:, c, :], c == 0, c == KC - 1, [slot.b, self.gl_z.b], [ps.b])
                k.tt(self.hTv[:, fc, :], self.hTv[:, fc, :], ps.t[:], ALU.add, [self.hTb[fc], ps.b], [self.hTb[fc]])

    def build(self):
        k = self.k
        self.prologue()
        xTv = self.xT.rearrange("(c p) s -> p c s", p=128)
        oTv = self.outT.rearrange("(c p) s -> p c s", p=128)
        types = set(mt for mt, _ in self.layers)
        if 2 in types:
            li = [l for mt, l in self.layers if mt == 2][0]
        for ti in range(self.NT):
            t0 = ti * T
            for c in range(KC):
                k.dma('sp', self.hTv[:, c, :], xTv[:, c, t0:t0 + T], writes=[self.hTb[c]])
            for mt, li in self.layers:
                pre = f"L{li}_"
                if mt == 0:
                    self.switch("m0")
                    self.rwkv(li, ti)
                elif mt is not None:
                    self.rmsnorm(pre + "nmix")
                    self.switch(f"m{mt}")
                    if mt == 2:
                        self.sgu(li, ti)
                    if mt == 3:
                        self.gla(li, ti)
                    if mt == 1:
                        self.swa(li, ti)
                if self.do_ffn:
                    self.rmsnorm(pre + "nffn")
                    self.switch('ffn')
                    self.ffn(li)
            self.switch('ffn')
            self.rmsnorm("nfinal", out_f32=self.fin)
            k.dma('sp', oTv[:, :, t0:t0 + T], self.fin.t[:].rearrange("p (c t) -> p c t", c=KC), reads=[self.fin.b])
        k.finish()


CST = {}
CST_N = 0
CSTB_N = 640


def _make_consts():
    global CST_N
    cols = []

    def add(name, m):
        global CST_N
        m = np.asarray(m, np.float32)
        CST[name] = (CST_N, m.shape[1])
        cols.append(m)
        CST_N += m.shape[1]
    i = np.arange(128)
    add("ident", np.eye(128))
    add("ones", np.ones((128, 128)))
    add("mask_le", (i[:, None] <= i[None, :]).astype(np.float32))
    add("mask_gt", (i[:, None] > i[None, :]).astype(np.float32))
    add("blk64", (i[:, None] // 64 == i[None, :] // 64).astype(np.float32))
    add("eps", np.full((128, 1), EPS))
    invf = (10000.0 ** (-np.arange(0, 64, 2, dtype=np.float32) / 64)).astype(np.float32)
    add("invf", (invf[i % 32].astype(np.float64) / (2 * np.pi)).astype(np.float32)[:, None])
    rot = np.zeros((128, 128), np.float32)
    for m in range(128):
        if m % 64 < 32:
            rot[m + 32, m] = -1.0
        else:
            rot[m - 32, m] = 1.0
    add("rotT", rot)
    j64 = np.arange(64)
    lt = (j64[:, None] < j64[None, :]).astype(np.float32)
    le = (j64[:, None] <= j64[None, :]).astype(np.float32)
    gt = (j64[:, None] > j64[None, :]).astype(np.float32)
    m320 = np.zeros((128, 320), np.float32)
    m320[0:64] = np.concatenate([lt, le, lt, le, gt], 1)
    add("m320", m320)
    add("gneps", np.full((128, 1), 64e-5))
    add("one", np.full((128, 1), 1.0))
    add("reset64", np.tile((np.arange(512) % 64 != 0).astype(np.float32)[None, :], (128, 1)))
    return np.ascontiguousarray(np.concatenate(cols, 1))


CST_ARR = _make_consts()


def run_model(inputs, S, layers, n_cores, do_ffn=True):
    inp = {k_: np.asarray(v) for k_, v in inputs.items()}
    pk = Packer()
    order = []
    pk.vec("nfinal", inp['norm_final'])
    for mt, li in layers:
        pack_layer(pk, inp, li, mt, 0)
        if mt == 2:
            pack_sgu(pk, inp, li, order)
        if mt == 3:
            pack_gla(pk, inp, li, order)
        if mt == 1:
            pack_swa(pk, inp, li, order)
        if mt == 0:
            pack_rwkv(pk, inp, li, order)
        if do_ffn:
            pack_ffn(pk, inp, li, order)
    wpk, vec = pk.finish()
    nc = bass.Bass("TRN2", target_bir_lowering=False)
    with ExitStack() as es:
        prog = Prog(nc, es, S, layers, pk, do_ffn)
        if any(mt == 2 for mt, _ in layers):
            prog.bsd = nc.dram_tensor("bsd", [1, 2048], F32, kind="ExternalInput").ap()
        prog.build()
    x = inp['x']
    in_maps = []
    for c in range(n_cores):
        m = {"xT": np.ascontiguousarray(x[c, :S].T), "pos": np.ascontiguousarray(inp['positions'][c:c + 1, :S]).astype(np.int32),
             "wpk": wpk, "vec": vec, "cst": CST_ARR}
        if any(mt == 2 for mt, _ in layers):
            m["bsd"] = np.ascontiguousarray(inp['sg_b_s'][0].reshape(1, 2048))
        in_maps.append(m)
    import os
    if os.environ.get("KTRACE"):
        res = run_bass_kernel_spmd(nc, in_maps, core_ids=list(range(n_cores)), trace=True)
        print("EXEC_TIME_NS", res.exec_time_ns, "instr counts", {e: prog.k.cnt[e] for e in ENG}, "dmas", dict(prog.k.dma_i))
    else:
        res = run_bass_kernel_spmd(nc, in_maps, core_ids=list(range(n_cores)))
    out = np.stack([np.ascontiguousarray(r["outT"].T) for r in res.results], 0)
    return out


def kernel(**inputs):
    layers = [(0, 0), (1, 1), (2, 2), (3, 3)]
    out = run_model(inputs, 4096, layers, 8)
    return out.astype(np.float32)
```
